# Optimizing a Trainium2 kernel written in Bass

```python
import math
import jax, jax.numpy as jnp
from jax import lax
import numpy as np

D_MODEL = 1024
BATCH = 8
SEQ = 4096
DEPTH = 1

MIX_WIDTH = D_MODEL
DIFF_WIDTH = MIX_WIDTH // 2
FOX_WIDTH = MIX_WIDTH - DIFF_WIDTH
DIFF_QK_DIM = 64
DIFF_V_DIM = 2 * DIFF_QK_DIM
N_DIFF_HEADS = DIFF_WIDTH // DIFF_V_DIM
FOX_HEAD_DIM = 64
N_FOX_HEADS = FOX_WIDTH // FOX_HEAD_DIM
N_IN = 3 * DIFF_WIDTH + 3 * FOX_WIDTH + N_FOX_HEADS
D_FF = ((8 * D_MODEL // 3 + 255) // 256) * 256
CONV_WIDTH = 3
NUM_BUCKETS = 32
MAX_EXACT = NUM_BUCKETS // 2
MAX_DISTANCE = 128
BLOCK_Q = 128
N_MOD = 6
NORM_EPS = 1e-6
NEG_INF = -1e30

kernel_name = "hybrid_diff_fox_convffn_adaln"


def _rmsnorm(x, g):
    xf = x.astype(jnp.float32)
    y = xf * lax.rsqrt(jnp.mean(xf * xf, axis=-1, keepdims=True) + NORM_EPS)
    return (y * g.astype(jnp.float32)).astype(x.dtype)


def _modulate(x, g, shift, scale):
    return _rmsnorm(x, g) * (1 + scale[:, None, :]) + shift[:, None, :]


def _t5_causal_bucket(n):
    nf = jnp.maximum(n, 1).astype(jnp.float32)
    large = MAX_EXACT + (jnp.log(nf / MAX_EXACT) / math.log(MAX_DISTANCE / MAX_EXACT)
                         * (NUM_BUCKETS - MAX_EXACT)).astype(jnp.int32)
    large = jnp.minimum(large, NUM_BUCKETS - 1)
    return jnp.where(n < MAX_EXACT, n, large)


def _diff_attention(q, k, v, bias_dist, lam):
    B, S, H, _, Dk = q.shape
    Dv = v.shape[-1]
    scale = Dk ** -0.5
    kpos = jnp.arange(S)

    def block(i):
        q0 = i * BLOCK_Q
        qb = lax.dynamic_slice_in_dim(q, q0, BLOCK_Q, axis=1)
        s = jnp.einsum('bqhmd,bkhmd->bhmqk', qb, k,
                       preferred_element_type=jnp.float32) * scale
        dist = (q0 + jnp.arange(BLOCK_Q))[:, None] - kpos[None, :]
        bias = jnp.transpose(bias_dist[jnp.maximum(dist, 0)], (2, 0, 1)).astype(jnp.float32)
        s = jnp.where(dist >= 0, s + bias[None, :, None], NEG_INF)
        p = jax.nn.softmax(s, axis=-1)
        a = p[:, :, 0] - lam * p[:, :, 1]
        return jnp.einsum('bhqk,bkhd->bqhd', a.astype(v.dtype), v)

    out = lax.map(block, jnp.arange(S // BLOCK_Q))
    return jnp.moveaxis(out, 0, 1).reshape(B, S, H, Dv)


def _forgetting_attention(q, k, v, cum_logf):
    B, S, H, D = q.shape
    scale = D ** -0.5
    kpos = jnp.arange(S)
    cum_k = jnp.transpose(cum_logf, (0, 2, 1))

    def block(i):
        q0 = i * BLOCK_Q
        qb = lax.dynamic_slice_in_dim(q, q0, BLOCK_Q, axis=1)
        cq = lax.dynamic_slice_in_dim(cum_k, q0, BLOCK_Q, axis=2)
        s = jnp.einsum('bqhd,bkhd->bhqk', qb, k,
                       preferred_element_type=jnp.float32) * scale
        decay = cq[..., :, None] - cum_k[..., None, :]
        causal = ((q0 + jnp.arange(BLOCK_Q))[:, None] >= kpos[None, :])
        s = jnp.where(causal[None, None], s + decay, NEG_INF)
        p = jax.nn.softmax(s, axis=-1)
        return jnp.einsum('bhqk,bkhd->bqhd', p.astype(v.dtype), v)

    out = lax.map(block, jnp.arange(S // BLOCK_Q))
    return jnp.moveaxis(out, 0, 1).reshape(B, S, H, D)


def _causal_depthwise_conv(u, w, b):
    C = u.shape[-1]
    y = lax.conv_general_dilated(u, w[:, None, :].astype(u.dtype), window_strides=(1,),
                                 padding=[(CONV_WIDTH - 1, 0)],
                                 dimension_numbers=('NWC', 'WIO', 'NWC'),
                                 feature_group_count=C)
    return y + b


def setup_inputs(seed: int = 0) -> dict:
    key = jax.random.key(seed)
    ks = jax.random.split(key, 24)
    f32 = jnp.float32
    nrm = lambda k, shape, s: (jax.random.normal(k, shape, f32) * s)
    D = D_MODEL
    return {
        "x": nrm(ks[0], (BATCH, SEQ, D), 1.0),
        "c": nrm(ks[1], (BATCH, D), 1.0),
        "ada_w": nrm(ks[2], (DEPTH, D, N_MOD * D), D ** -0.5),
        "ada_b": nrm(ks[3], (DEPTH, N_MOD * D), 0.02),
        "attn_norm_g": 1.0 + nrm(ks[4], (DEPTH, D), 0.02),
        "w_in": nrm(ks[5], (DEPTH, D, N_IN), D ** -0.5),
        "forget_b": jax.random.uniform(ks[6], (DEPTH, N_FOX_HEADS), f32, 1.0, 4.0),
        "lambda_q1": nrm(ks[7], (DEPTH, DIFF_QK_DIM), 0.1),
        "lambda_k1": nrm(ks[8], (DEPTH, DIFF_QK_DIM), 0.1),
        "lambda_q2": nrm(ks[9], (DEPTH, DIFF_QK_DIM), 0.1),
        "lambda_k2": nrm(ks[10], (DEPTH, DIFF_QK_DIM), 0.1),
        "subln_g": 1.0 + nrm(ks[11], (DEPTH, DIFF_V_DIM), 0.02),
        "rel_bias": nrm(ks[12], (NUM_BUCKETS, N_DIFF_HEADS), 0.5),
        "w_out": nrm(ks[13], (DEPTH, MIX_WIDTH, D), MIX_WIDTH ** -0.5),
        "ffn_norm_g": 1.0 + nrm(ks[14], (DEPTH, D), 0.02),
        "w_up": nrm(ks[15], (DEPTH, D, 2 * D_FF), D ** -0.5),
        "conv_w": nrm(ks[16], (DEPTH, CONV_WIDTH, 2 * D_FF), CONV_WIDTH ** -0.5),
        "conv_b": nrm(ks[17], (DEPTH, 2 * D_FF), 0.02),
        "w_down": nrm(ks[18], (DEPTH, D_FF, D), D_FF ** -0.5),
        "final_norm_g": 1.0 + nrm(ks[19], (D,), 0.02),
    }


def reference(x, c, ada_w, ada_b, attn_norm_g, w_in, forget_b, lambda_q1, lambda_k1,
              lambda_q2, lambda_k2, subln_g, rel_bias, w_out, ffn_norm_g, w_up, conv_w,
              conv_b, w_down, final_norm_g):
    B, S, D = x.shape
    bias_dist = rel_bias[_t5_causal_bucket(jnp.arange(S, dtype=jnp.int32))]
    c_act = jax.nn.silu(c)

    for l in range(DEPTH):
        mod = c_act @ ada_w[l] + ada_b[l]
        sh1, sc1, g1, sh2, sc2, g2 = jnp.split(mod, N_MOD, axis=-1)

        h = _modulate(x, attn_norm_g[l], sh1, sc1)
        proj = h @ w_in[l]
        o = np.cumsum([DIFF_WIDTH, DIFF_WIDTH, DIFF_WIDTH, FOX_WIDTH, FOX_WIDTH, FOX_WIDTH]).tolist()
        dq, dk, dv, fq, fk, fv, fl = jnp.split(proj, o, axis=-1)

        dq = dq.reshape(B, S, N_DIFF_HEADS, 2, DIFF_QK_DIM)
        dk = dk.reshape(B, S, N_DIFF_HEADS, 2, DIFF_QK_DIM)
        dv = dv.reshape(B, S, N_DIFF_HEADS, DIFF_V_DIM)
        lambda_init = 0.8 - 0.6 * math.exp(-0.3 * l)
        lam = (jnp.exp(jnp.sum(lambda_q1[l].astype(jnp.float32) * lambda_k1[l].astype(jnp.float32)))
               - jnp.exp(jnp.sum(lambda_q2[l].astype(jnp.float32) * lambda_k2[l].astype(jnp.float32)))
               + lambda_init)
        d_out = _diff_attention(dq, dk, dv, bias_dist, lam)
        d_out = _rmsnorm(d_out, subln_g[l]) * (1.0 - lambda_init)
        d_out = d_out.reshape(B, S, DIFF_WIDTH)

        fq = fq.reshape(B, S, N_FOX_HEADS, FOX_HEAD_DIM)
        fk = fk.reshape(B, S, N_FOX_HEADS, FOX_HEAD_DIM)
        fv = fv.reshape(B, S, N_FOX_HEADS, FOX_HEAD_DIM)
        log_f = jax.nn.log_sigmoid((fl + forget_b[l]).astype(jnp.float32))
        cum_logf = jnp.cumsum(log_f, axis=1)
        f_out = _forgetting_attention(fq, fk, fv, cum_logf).reshape(B, S, FOX_WIDTH)

        mix = jnp.concatenate([d_out, f_out.astype(d_out.dtype)], axis=-1) @ w_out[l]
        x = x + g1[:, None, :] * mix

        h = _modulate(x, ffn_norm_g[l], sh2, sc2)
        u = _causal_depthwise_conv(h @ w_up[l], conv_w[l], conv_b[l])
        gate, val = jnp.split(u, 2, axis=-1)
        y = (jax.nn.silu(gate) * val) @ w_down[l]
        x = x + g2[:, None, :] * y

    return _rmsnorm(x, final_norm_g)
```

```python
import math
import os
import numpy as np
import concourse.bass as bass
import concourse.mybir as mybir
from concourse.bass_utils import run_bass_kernel_spmd

F32 = mybir.dt.float32
BF16 = mybir.dt.bfloat16
AF = mybir.ActivationFunctionType
ALU = mybir.AluOpType

ENGS = ("sync", "scalar", "vector", "gpsimd", "tensor")
S = 4096
D = 1024
NT = S // 128
NIN = 3080
DFF = 2816
EPS = 1e-6
LAMBDA_INIT = 0.8 - 0.6 * math.exp(-0.3 * 0)
DEBUG = bool(int(os.environ.get("MK_DEBUG", "0")))


class Prog:
    def __init__(self, nc, tag):
        self.nc = nc
        self.tag = tag
        self.ops = {e: [] for e in ENGS}
        self.cnt = {}
        self.n = {e: 0 for e in ENGS}
        self.waited = {e: {} for e in ENGS}
        self._ctx = []
        for e in ENGS:
            self.cnt[e] = self.sem("cnt_" + e)

    def sem(self, name):
        cm = self.nc.semaphore(self.tag + "_" + name)
        s = cm.__enter__()
        self._ctx.append(cm)
        return s

    def dsem(self, name):
        return [self.sem(name), {"v": 0}]

    def sb(self, name, shape, dt):
        cm = self.nc.sbuf_tensor(self.tag + "_" + name, list(shape), dt)
        t = cm.__enter__()
        self._ctx.append(cm)
        return t

    def ps(self, name, shape, dt=F32):
        cm = self.nc.psum_tensor(self.tag + "_" + name, list(shape), dt)
        t = cm.__enter__()
        self._ctx.append(cm)
        return t

    def _waits(self, eng, deps):
        w = []
        for d in deps:
            if d is None:
                continue
            sem, val = d
            key = id(sem)
            if self.waited[eng].get(key, 0) >= val:
                continue
            self.waited[eng][key] = val
            w.append((sem, val))
        return w

    def op(self, eng, fn, deps=(), inc=True):
        w = self._waits(eng, deps)
        tok = None
        if inc:
            self.n[eng] += 1
            tok = (self.cnt[eng], self.n[eng])
        cnt = self.cnt[eng]

        def run(e, w=w, fn=fn, inc=inc, cnt=cnt):
            for sem, val in w:
                e.wait_ge(sem, val)
            ins = fn(e)
            if inc:
                ins.then_inc(cnt, 1)
        self.ops[eng].append(run)
        return tok

    def dma(self, eng, out, in_, ds, deps=()):
        w = self._waits(eng, deps)
        sem, sv = ds
        sv["v"] += 16
        tok = (sem, sv["v"])

        def run(e, w=w):
            for s, val in w:
                e.wait_ge(s, val)
            e.dma_start(out=out, in_=in_).then_inc(sem, 16)
        self.ops[eng].append(run)
        return tok

    def wait(self, eng, deps):
        w = self._waits(eng, deps)

        def run(e, w=w):
            for s, val in w:
                e.wait_ge(s, val)
        self.ops[eng].append(run)

    def build(self):
        nc = self.nc
        ops = self.ops
        with nc.Block() as block:
            @block.sync
            def _(e):
                for f in ops["sync"]:
                    f(e)

            @block.scalar
            def _(e):
                for f in ops["scalar"]:
                    f(e)

            @block.vector
            def _(e):
                for f in ops["vector"]:
                    f(e)

            @block.gpsimd
            def _(e):
                for f in ops["gpsimd"]:
                    f(e)

            @block.tensor
            def _(e):
                for f in ops["tensor"]:
                    f(e)
        for cm in reversed(self._ctx):
            cm.__exit__(None, None, None)
        self._ctx = []


def phase0(nc, I, G):
    p = Prog(nc, "p0")
    ds = p.dsem("ld")
    dw = [p.dsem("aw%d" % i) for i in range(4)]
    dsc = p.dsem("scr")
    cT = p.sb("cT", [128, 8], F32)
    adab = p.sb("adab", [128, 48], F32)
    gat = p.sb("gat", [128, 8], F32)
    gff = p.sb("gff", [128, 8], F32)
    lq = p.sb("lq", [1, 256], F32)
    fb = p.sb("fb", [8, 1], F32)
    sg = p.sb("sg", [128, 1], F32)
    id32 = p.sb("id32", [128, 128], F32)
    loads = [
        p.dma("sync", cT[:], I["cT"], ds), p.dma("sync", adab[:], I["adabT"], ds),
        p.dma("sync", gat[:], I["gatT"], ds), p.dma("sync", gff[:], I["gffT"], ds),
        p.dma("sync", lq[:], I["lam4"], ds), p.dma("sync", fb[:], I["fb"], ds),
        p.dma("sync", sg[:], I["subg"], ds), p.dma("sync", id32[:], I["ident"], ds),
        p.dma("sync", G["fg_bc"][:], bass.AP(I["_H"]["fng"], 0, [[0, 128], [1, 1024]]), ds),
        p.dma("sync", G["b31"][:], bass.AP(I["_H"]["relb"], 31 * 4, [[0, 128], [1, 4]]), ds),
        p.dma("sync", G["convw"][:], I["convwT"], ds), p.dma("sync", G["convb"][:], I["convbT"], ds),
        p.dma("gpsimd", G["ident16"][:], I["ident"], ds),
    ]
    LD = loads[-1]
    LD = (ds[0], ds[1]["v"])
    m_ = [p.op("gpsimd", lambda e: e.memset(G["epsc"][:], EPS)),
          p.op("gpsimd", lambda e: e.memset(G["onec"][:], 1.0)),
          p.op("gpsimd", lambda e: e.memset(G["zeroc"][:], 0.0)),
          p.op("gpsimd", lambda e: e.memset(G["ones32"][:], 1.0)),
          p.op("gpsimd", lambda e: e.memset(G["ones16"][:], 1.0)),
          p.op("gpsimd", lambda e: e.memset(G["ones16"][:], 1.0), deps=[LD])]
    MS = m_[-1]
    cact = p.sb("cact", [128, 8], F32)
    t_c = p.op("scalar", lambda e: e.activation(cact[:], cT[:], AF.Silu), deps=[LD])
    aw = [p.sb("aw%d" % i, [128, 6144], F32) for i in range(4)]
    acc = p.sb("acc", [128, 6144], F32)
    ta = None
    rd = [None] * 4
    tls = []
    for c in range(8):
        if c >= 4:
            pass
        tls.append(None)
    for c in range(8):
        if c < 4:
            tls[c] = p.dma("sync", aw[c % 4][:], I["ada_w"][c * 128:(c + 1) * 128, :], dw[c % 4])
    for c in range(8):
        tl = tls[c]
        if c == 0:
            ta = p.op("vector", lambda e, c=c: e.tensor_scalar(acc[:], aw[0][:], cact[:, 0:1], None, op0=ALU.mult), deps=[tl, t_c])
        else:
            ta = p.op("vector", lambda e, c=c: e.scalar_tensor_tensor(acc[:], aw[c % 4][:], cact[:, c:c + 1], acc[:], op0=ALU.mult, op1=ALU.add), deps=[tl, ta])
        rd[c % 4] = ta
        if c + 4 < 8:
            tls[c + 4] = p.dma("sync", aw[c % 4][:], I["ada_w"][(c + 4) * 128:(c + 5) * 128, :], dw[c % 4], deps=[ta])
    pm = p.ps("pm", [128, 512], F32)
    tm = None
    for j in range(48):
        tm = p.op("tensor", lambda e, j=j: e.matmul(pm[:, j:j + 1], acc[:, j * 128:(j + 1) * 128], G["ones32"][:, 0:1], start=True, stop=True), deps=[ta, MS], inc=(j == 47))
    modT = G["modT"]
    t_mod = p.op("vector", lambda e: e.tensor_tensor(modT[:], pm[:, 0:48], adab[:], op=ALU.add), deps=[tm, LD])
    t1 = p.op("vector", lambda e: e.scalar_tensor_tensor(G["gsc1"][:], modT[:, 8:16], 1.0, gat[:], op0=ALU.add, op1=ALU.mult), deps=[t_mod])
    t2 = p.op("vector", lambda e: e.scalar_tensor_tensor(G["gsc2"][:], modT[:, 32:40], 1.0, gff[:], op0=ALU.add, op1=ALU.mult), deps=[t_mod])
    dg = p.sb("dg", [128, 16, 128], F32)
    pb = [p.ps("pb0", [128, 512], F32), p.ps("pb1", [128, 512], F32)]
    tcp = [None, None]
    for which, base, dst in ((0, 16, G["g1_bc"]), (1, 40, G["g2_bc"])):
        for c in range(8):
            k = which * 8 + c
            td = p.op("vector", lambda e, k=k, c=c, base=base: e.tensor_scalar(dg[:, k, :], id32[:], modT[:, base + c:base + c + 1], None, op0=ALU.mult), deps=[t_mod, LD])
            bi = k // 4
            tmm = p.op("tensor", lambda e, k=k, bi=bi: e.matmul(pb[bi % 2][:, (k % 4) * 128:(k % 4 + 1) * 128], G["ones32"][:], dg[:, k, :], start=True, stop=True), deps=[td, MS, tcp[bi % 2] if k % 4 == 0 else None])
            if k % 4 == 3:
                half = (k % 8) // 4
                tcp[bi % 2] = p.op("vector", lambda e, bi=bi, dst=dst, half=half: e.tensor_copy(dst[:, half * 512:(half + 1) * 512], pb[bi % 2][:]), deps=[tmm])
    lp = p.sb("lp", [1, 128], F32)
    ls = p.sb("ls", [1, 4], F32)
    tl1 = p.op("vector", lambda e: e.tensor_tensor(lp[:, 0:64], lq[:, 0:64], lq[:, 64:128], op=ALU.mult), deps=[LD])
    tl2 = p.op("vector", lambda e: e.tensor_tensor(lp[:, 64:128], lq[:, 128:192], lq[:, 192:256], op=ALU.mult), deps=[LD])
    tl3 = p.op("vector", lambda e: e.reduce_sum(ls[:, 0:1], lp[:, 0:64], axis=mybir.AxisListType.X), deps=[tl1])
    tl4 = p.op("vector", lambda e: e.reduce_sum(ls[:, 1:2], lp[:, 64:128], axis=mybir.AxisListType.X), deps=[tl2])
    tl5 = p.op("scalar", lambda e: e.activation(ls[:, 2:4], ls[:, 0:2], AF.Exp), deps=[tl3, tl4])
    tl6 = p.op("vector", lambda e: e.scalar_tensor_tensor(ls[:, 0:1], ls[:, 3:4], -LAMBDA_INIT, ls[:, 2:3], op0=ALU.add, op1=ALU.subtract), deps=[tl5])
    tl7 = p.op("tensor", lambda e: e.matmul(pm[:, 64:65], G["ones32"][0:1, :], ls[0:1, 0:1], start=True, stop=True), deps=[tl6, MS, t_mod])
    tl8 = p.op("vector", lambda e: e.tensor_copy(G["neglam"][:], pm[:, 64:65]), deps=[tl7])
    p.op("vector", lambda e: e.tensor_scalar(G["gsub"][:], sg[:], 1.0 - LAMBDA_INIT, None, op0=ALU.mult), deps=[LD])
    p.op("vector", lambda e: e.tensor_scalar(G["nfb"][:], fb[:], -1.0, None, op0=ALU.mult), deps=[LD])
    p.build()


def phase1(nc, I, G):
    p = Prog(nc, "p1")
    w16 = p.sb("w16", [128, 8, NIN], BF16)
    wsrc = I["w_in"].rearrange("(c p) n -> p c n", p=128)
    wgroups = [(0, 512), (512, 1024), (1536, 2048), (2048, 2560), (1024, 1536), (2560, NIN)]
    wgtok = []
    for gi, (ca, cb) in enumerate(wgroups):
        dwg = p.dsem("wg%d" % gi)
        wgtok.append(p.dma("gpsimd", w16[:, :, ca:cb], wsrc[:, :, ca:cb], dwg))

    def wtok(col):
        for gi, (ca, cb) in enumerate(wgroups):
            if ca <= col < cb:
                return wgtok[gi]
        raise ValueError(col)
    xt = [p.sb("xt%d" % i, [128, D], F32) for i in range(8)]
    dx = [p.dsem("x%d" % i) for i in range(8)]
    xn = [p.sb("xn%d" % i, [128, D], BF16) for i in range(8)]
    junk = p.sb("junk", [128, D], F32)
    ss = p.sb("ss", [128, NT], F32)
    rs = p.sb("rs", [128, NT], F32)
    hT = [p.sb("hT%d" % i, [128, 8, 512], BF16) for i in range(2)]
    pst = [p.ps("pst%d" % i, [128, 512], BF16) for i in range(2)]
    pp = [p.ps("pp%d" % i, [128, 512], F32) for i in range(4)]
    stg = [p.sb("stg%d" % i, [128, 512], BF16) for i in range(4)]
    dstg = [p.dsem("stg%d" % i) for i in range(4)]
    flst = [p.sb("flst%d" % i, [8, 512], F32) for i in range(2)]
    dfl = [p.dsem("fl%d" % i) for i in range(2)]
    fl_rd = [None, None]
    xn_rd = [None] * 8
    xt_rd = [None] * 8
    hT_rd = [None, None]
    pst_rd = [None, None]
    pp_rd = [None] * 4
    stg_rd = [None] * 4
    hT_ev = [[None] * 8, [None] * 8]
    st = {"pp": 0, "stg": 0, "alt": 0}
    fl_tok = []
    ptoe = p.ps("ptoe", [128, 1024], F32)
    btile1 = p.sb("btile1", [128, 4, 640], BF16)
    toe_tok, _ = build_toeplitz(p, I, G, btile1, None, ptoe)
    dbt = p.dsem("bts")
    bt_store = p.dma("sync", I["bt_s"].rearrange("p (h x) -> p h x", h=4), btile1[:], dbt, deps=[toe_tok])

    def norm_tr(tc):
        xnt = []
        for t in range(4):
            sl = (tc % 2) * 4 + t
            idx = tc * 4 + t
            tx = p.dma("sync", xt[sl][:], I["x"][idx * 128:(idx + 1) * 128, :], dx[sl], deps=[xt_rd[sl]])
            tsq = p.op("scalar", lambda e, sl=sl, idx=idx: e.activation(junk[:], xt[sl][:], AF.Square, accum_out=ss[:, idx:idx + 1]), deps=[tx])
            tln = p.op("scalar", lambda e, idx=idx: e.activation(rs[:, idx:idx + 1], ss[:, idx:idx + 1], AF.Ln, scale=1.0 / D, bias=G["epsc"][:, 0:1]), deps=[tsq])
            tex = p.op("scalar", lambda e, idx=idx: e.activation(rs[:, idx:idx + 1], rs[:, idx:idx + 1], AF.Exp, scale=-0.5), deps=[tln])
            tn = p.op("vector", lambda e, sl=sl, idx=idx: e.tensor_scalar(xn[sl][:], xt[sl][:], rs[:, idx:idx + 1], None, op0=ALU.mult), deps=[tex, tx, xn_rd[sl]])
            xt_rd[sl] = tn
            xnt.append(tn)
        hb = tc % 2
        for c in range(8):
            tt = None
            for t in range(4):
                sl = (tc % 2) * 4 + t
                tt = p.op("tensor", lambda e, c=c, t=t, sl=sl: e.transpose(pst[c % 2][:, t * 128:(t + 1) * 128], xn[sl][:, c * 128:(c + 1) * 128], G["ident16"][:]),
                          deps=[xnt[t], pst_rd[c % 2] if t == 0 else None], inc=(t == 3))
            tev = p.op("vector", lambda e, c=c, hb=hb: e.tensor_scalar(hT[hb][:, c, :], pst[c % 2][:], G["gsc1"][:, c:c + 1], G["modT"][:, c:c + 1], op0=ALU.mult, op1=ALU.add),
                       deps=[tt, hT_rd[hb] if c == 0 else None])
            pst_rd[c % 2] = tev
            hT_ev[hb][c] = tev
        for t in range(4):
            xn_rd[(tc % 2) * 4 + t] = tt

    def evac(src, dst_dram, scale, rows=128):
        si = st["stg"] % 4
        st["stg"] += 1
        pi = src
        if st["alt"] % 2 == 0:
            te = p.op("scalar", lambda e: e.activation(stg[si][0:rows, :], pp[pi][0:rows, :], AF.Copy, scale=scale), deps=[st["last_mm"], stg_rd[si]])
        else:
            te = p.op("vector", lambda e: e.tensor_scalar(stg[si][0:rows, :], pp[pi][0:rows, :], scale, None, op0=ALU.mult), deps=[st["last_mm"], stg_rd[si]])
        st["alt"] += 1
        pp_rd[pi] = te
        stg_rd[si] = p.dma("gpsimd", dst_dram, stg[si][0:rows, :], dstg[si], deps=[te])

    def proj(tc):
        hb = tc % 2
        tok0 = tc * 512
        last = None
        for m in range(16):
            col0 = [0, 512, 1536, 2048][m // 4] + (m % 4) * 128
            pi = st["pp"] % 4
            st["pp"] += 1
            for c in range(8):
                last = p.op("tensor", lambda e, c=c, pi=pi, col0=col0: e.matmul(pp[pi][:], w16[:, c, col0:col0 + 128], hT[hb][:, c, :], start=(c == 0), stop=(c == 7)),
                            deps=[hT_ev[hb][c], wtok(col0), pp_rd[pi] if c == 0 else None], inc=(c == 7))
            st["last_mm"] = last
            scale = 0.125 if (m // 4) in (0, 2) else 1.0
            evac(pi, I["qk_s"][m * 128:(m + 1) * 128, tok0:tok0 + 512], scale)
        for t in range(4):
            for half in range(2):
                vc0 = 1024 if half == 0 else 2560
                pi = st["pp"] % 4
                st["pp"] += 1
                for c in range(8):
                    last = p.op("tensor", lambda e, c=c, pi=pi, vc0=vc0, t=t: e.matmul(pp[pi][:], hT[hb][:, c, t * 128:(t + 1) * 128], w16[:, c, vc0:vc0 + 512], start=(c == 0), stop=(c == 7)),
                                deps=[hT_ev[hb][c], wtok(vc0), pp_rd[pi] if c == 0 else None], inc=(c == 7))
                st["last_mm"] = last
                r0 = tc * 512 + t * 128
                evac(pi, I["v_s"][r0:r0 + 128, half * 512:(half + 1) * 512], 1.0)
        pi = st["pp"] % 4
        st["pp"] += 1
        for c in range(8):
            last = p.op("tensor", lambda e, c=c, pi=pi: e.matmul(pp[pi][0:8, :], w16[:, c, 3072:3080], hT[hb][:, c, :], start=(c == 0), stop=(c == 7)),
                        deps=[hT_ev[hb][c], wtok(3072), pp_rd[pi] if c == 0 else None], inc=(c == 7))
        tf = p.op("vector", lambda e, pi=pi, hb=hb: e.tensor_copy(flst[hb][:], pp[pi][0:8, :]), deps=[last, fl_rd[hb]])
        pp_rd[pi] = tf
        fl_rd[hb] = p.dma("gpsimd", I["fl_s"][:, tok0:tok0 + 512], flst[hb][:], dfl[hb], deps=[tf])
        hT_rd[hb] = last

    norm_tr(0)
    for tc in range(8):
        if tc + 1 < 8:
            norm_tr(tc + 1)
        proj(tc)

    p.wait("gpsimd", [(d[0], d[1]["v"]) for d in dstg] + [(d[0], d[1]["v"]) for d in dfl])
    p.wait("sync", [bt_store])
    p.build()


def phase2(nc, I, G):
    p = Prog(nc, "p2")
    SEG = 256
    F = p.sb("F", [128, SEG], F32)
    A = p.sb("A", [128, SEG], F32)
    B = p.sb("B", [128, SEG], F32)
    tri = p.sb("tri", [128, 128], F32)
    nb = p.sb("nb", [128, 1], F32)
    off = p.sb("off", [128, 1], F32)
    pm = p.ps("pm", [128, 512], F32)
    dl = p.dsem("ld")
    p.dma("sync", F[:], I["fl_s"].rearrange("h (s t) -> (h s) t", t=SEG), dl)
    p.dma("sync", tri[:], I["segtri"], dl)
    p.dma("sync", nb[:], I["fb128"], dl)
    LD = (dl[0], dl[1]["v"])
    t0 = p.op("vector", lambda e: e.tensor_scalar(nb[:], nb[:], -1.0, None, op0=ALU.mult), deps=[LD])
    t1 = p.op("scalar", lambda e: e.activation(A[:], F[:], AF.Exp, scale=-1.0, bias=nb[:, 0:1]), deps=[t0, LD])
    t2 = p.op("scalar", lambda e: e.activation(A[:], A[:], AF.Ln, bias=G["onec"][:, 0:1]), deps=[t1])
    cur, nxt = A, B
    tk = t2
    ta = None
    s_ = 1
    while s_ < SEG:
        ta = p.op("vector", lambda e, cur=cur, nxt=nxt, s_=s_: e.tensor_copy(nxt[:, 0:s_], cur[:, 0:s_]), deps=[tk])
        tk = p.op("vector", lambda e, cur=cur, nxt=nxt, s_=s_: e.tensor_tensor(nxt[:, s_:SEG], cur[:, s_:SEG], cur[:, 0:SEG - s_], op=ALU.add), deps=[tk])
        cur, nxt = nxt, cur
        s_ *= 2
    tm = p.op("tensor", lambda e: e.matmul(pm[:, 0:1], tri[:], cur[:, SEG - 1:SEG], start=True, stop=True), deps=[tk, ta, LD])
    to_ = p.op("vector", lambda e: e.tensor_copy(off[:], pm[:, 0:1]), deps=[tm])
    tP = p.op("vector", lambda e: e.tensor_scalar(nxt[:], cur[:], off[:, 0:1], None, op0=ALU.add), deps=[to_, tk, ta])
    P_ = nxt
    R = cur
    ch = p.sb("ch", [128, SEG], BF16)
    cm = p.sb("cm", [128, SEG], BF16)
    cl = p.sb("cl", [128, SEG], BF16)
    nh = p.sb("nh", [128, SEG], BF16)
    nm = p.sb("nm", [128, SEG], BF16)
    nl = p.sb("nl", [128, SEG], BF16)
    on = p.sb("on", [128, SEG], BF16)
    R2 = p.sb("R2", [128, SEG], F32)
    to = p.op("gpsimd", lambda e: e.memset(on[:], 1.0))
    a1 = p.op("vector", lambda e: e.tensor_scalar(ch[:], P_[:], -1.0, None, op0=ALU.mult), deps=[tP])
    a2 = p.op("vector", lambda e: e.scalar_tensor_tensor(R[:], P_[:], -1.0, ch[:], op0=ALU.mult, op1=ALU.subtract), deps=[a1])
    a3 = p.op("vector", lambda e: e.tensor_copy(cm[:], R[:]), deps=[a2])
    a4 = p.op("vector", lambda e: e.tensor_tensor(R2[:], R[:], cm[:], op=ALU.subtract), deps=[a3])
    a5 = p.op("vector", lambda e: e.tensor_copy(cl[:], R2[:]), deps=[a4])
    a6 = p.op("vector", lambda e: e.tensor_scalar(nh[:], ch[:], -1.0, None, op0=ALU.mult), deps=[a1])
    a7 = p.op("vector", lambda e: e.tensor_scalar(nm[:], cm[:], -1.0, None, op0=ALU.mult), deps=[a3])
    a8 = p.op("vector", lambda e: e.tensor_scalar(nl[:], cl[:], -1.0, None, op0=ALU.mult), deps=[a5])
    da = p.dsem("aug")
    fin = []
    for r, (qsrc, ksrc) in enumerate(((ch, on), (cm, on), (cl, on), (on, nh), (on, nm), (on, nl))):
        fin.append(p.dma("sync", I["augq"][r, :, :].rearrange("h (s t) -> (h s) t", t=SEG), qsrc[:], da, deps=[a8, to]))
        fin.append(p.dma("sync", I["augk"][r, :, :].rearrange("h (s t) -> (h s) t", t=SEG), ksrc[:], da, deps=[a8, to]))
    p.wait("sync", [fin[-1]])
    p.build()


def build_toeplitz(p, I, G, btile, cmask, Sp):
    ds = p.dsem("tld")
    dsc = p.dsem("scr")
    rbe = p.sb("rbe", [33, 4], F32)
    oh = p.sb("oh", [33, 768], F32)
    p.dma("sync", rbe[0:32, :], I["relb"], ds)
    p.dma("sync", oh[:], I["onehot"], ds)
    if cmask is not None:
        p.dma("gpsimd", cmask[:], I["cmask"], ds)
    LD = (ds[0], ds[1]["v"])
    MS = p.op("gpsimd", lambda e: e.memset(rbe[32:33, :], -1e30), deps=[LD])
    rl = p.sb("rl", [33, 4, 128], F32)
    vecs = [p.sb("vec%d" % h, [128, 768], F32) for h in range(4)]
    b32s = [p.sb("b32_%d" % h, [128, 640], F32) for h in range(4)]
    tprev = None
    tlast = []
    for h in range(4):
        vec = vecs[h]
        b32 = b32s[h]
        tr = p.op("vector", lambda e, h=h: e.tensor_scalar(rl[:, h, :], G["ones32"][0:33, :], rbe[:, h:h + 1], None, op0=ALU.mult), deps=[MS, LD])
        ta_ = p.op("tensor", lambda e, h=h: e.matmul(Sp[:, 0:384], rl[:, h, :], oh[:, 0:384], start=True, stop=True), deps=[tr, LD, tprev], inc=False)
        tb_ = p.op("tensor", lambda e, h=h: e.matmul(Sp[:, 512:896], rl[:, h, :], oh[:, 384:768], start=True, stop=True), deps=[])
        tv1 = p.op("vector", lambda e, vec=vec: e.tensor_copy(vec[:, 0:384], Sp[:, 0:384]), deps=[tb_])
        tv2 = p.op("vector", lambda e, vec=vec: e.tensor_copy(vec[:, 384:768], Sp[:, 512:896]), deps=[tb_])
        tprev = tv2
        dsh = p.dsem("scr%d" % h)
        tw = p.dma("gpsimd", bass.AP(I["_H"]["tscr"], h * 128 * 768, [[768, 128], [1, 768]]), vec[:], dsh, deps=[tv2, tv1])
        trd = p.dma("gpsimd", b32[:], bass.AP(I["_H"]["tscr"], h * 128 * 768 + 127, [[767, 128], [1, 640]]), dsh, deps=[tw])
        tlast.append((h, trd))
    for h, trd in tlast:
        tprev = p.op("vector", lambda e, h=h: e.tensor_copy(btile[:, h, :], b32s[h][:]), deps=[trd])
    return tprev, LD


def phase3(nc, I, G):
    p = Prog(nc, "p3")
    qd = [p.sb("qd%d" % i, [128, S], BF16) for i in range(2)]
    kd = [p.sb("kd%d" % i, [128, S], BF16) for i in range(2)]
    qf = [[p.sb("qf%d_%d" % (b, m), [70, S], BF16) for m in range(2)] for b in range(2)]
    kf = [[p.sb("kf%d_%d" % (b, m), [70, S], BF16) for m in range(2)] for b in range(2)]
    dlf2 = [p.dsem("ldf%d" % i) for i in range(2)]
    foxb_rd = [None, None]
    vt = [p.sb("vt%d" % i, [128, NT, 128], BF16) for i in range(2)]
    vtf = [p.sb("vtf%d" % i, [128, NT, 130], BF16) for i in range(2)]
    NPT = 4
    NFILL = 1
    PT = [p.sb("PT%d" % i, [128, 1024], BF16) for i in range(NPT)]
    co = [[p.sb("co%d_%d" % (a, m), [128, 512], F32) for m in range(2)] for a in range(2)]
    rsb = [[p.sb("rsb%d_%d" % (a, m), [128, 512], F32) for m in range(2)] for a in range(2)]
    tt_ = [[p.sb("tt%d_%d" % (a, m), [128, 512], F32) for m in range(2)] for a in range(2)]
    av = [p.sb("av%d" % a, [128, 512], F32) for a in range(2)]
    sq = [p.sb("sq%d" % a, [128, 512], F32) for a in range(2)]
    rstd = [p.sb("rstd%d" % a, [128, 512], F32) for a in range(2)]
    dd = [p.sb("dd%d" % a, [128, 512], F32) for a in range(2)]
    acc = [p.sb("acc%d" % i, [128, 1024], F32) for i in range(2)]
    acc_free = [None, None]
    acc_tok = [None]
    NOB = 2
    ob = [p.sb("ob%d" % i, [128, 512], BF16) for i in range(NOB)]
    dob = [p.dsem("ob%d" % i) for i in range(NOB)]
    Sps = [p.ps("S%d" % i, [128, 1024], F32) for i in range(2)]
    Ops = [[p.ps("O%d_%d" % (a, m), [128, 512], F32) for m in range(2)] for a in range(2)]
    dl = [p.dsem("ld%d" % i) for i in range(2)]
    dlf = p.dsem("ldf")
    dlv = [p.dsem("ldv%d" % i) for i in range(2)]
    btile = p.sb("btile", [128, 4, 640], BF16)
    cmask = p.sb("cmask", [128, 512], BF16)
    dtoe = p.dsem("toe")
    TOE = p.dma("sync", btile[:], I["bt_s"].rearrange("p (h x) -> p h x", h=4), dtoe)
    dcm = p.dsem("cm")
    CML = p.dma("gpsimd", cmask[:], I["cmask"], dcm)
    ones_tok = None
    for i in range(2):
        p.op("gpsimd", lambda e, i=i: e.memset(vtf[i][:, :, 64:65], 1.0))
        ones_tok = p.op("gpsimd", lambda e, i=i: e.memset(vtf[i][:, :, 129:130], 1.0))

    unit_rd = [None, None]
    fox_rd = [None]
    vt_rd = [None, None]
    vtf_rd = [None, None]
    ld_tok = {}

    def load_unit(u):
        if u < 4:
            b = u % 2
            p.dma("sync", qd[b][:], I["qk_s"][u * 128:(u + 1) * 128, :], dl[b], deps=[unit_rd[b]])
            p.dma("sync", kd[b][:], I["qk_s"][512 + u * 128:512 + (u + 1) * 128, :], dl[b], deps=[unit_rd[b]])
            for tg in range(4):
                vsrc = I["v_s"][tg * 1024:(tg + 1) * 1024, u * 128:(u + 1) * 128].rearrange("(t p) d -> p t d", p=128)
                p.dma("sync", vt[b][:, tg * 8:(tg + 1) * 8, :], vsrc, dl[b], deps=[vt_rd[b]])
            ld_tok[u] = [(dl[b][0], dl[b][1]["v"])]
        else:
            f = u - 4
            b = f % 2
            for m in range(2):
                hd = 2 * f + m
                p.dma("sync", qf[b][m][0:64, :], I["qk_s"][1024 + hd * 64:1024 + (hd + 1) * 64, :], dlf2[b], deps=[foxb_rd[b]])
                p.dma("sync", kf[b][m][0:64, :], I["qk_s"][1536 + hd * 64:1536 + (hd + 1) * 64, :], dlf2[b], deps=[foxb_rd[b]])
                p.dma("sync", qf[b][m][64:70, :], I["augq"][:, hd, :], dlf2[b], deps=[foxb_rd[b]])
                p.dma("sync", kf[b][m][64:70, :], I["augk"][:, hd, :], dlf2[b], deps=[foxb_rd[b]])
                c0 = 512 + f * 128 + m * 64
                for tg in range(4):
                    vsrc = I["v_s"][tg * 1024:(tg + 1) * 1024, c0:c0 + 64].rearrange("(t p) d -> p t d", p=128)
                    p.dma("sync", vtf[b][:, tg * 8:(tg + 1) * 8, 65 * m:65 * m + 64], vsrc, dlv[b], deps=[vtf_rd[b], ones_tok])
            ld_tok[u] = [(dlf2[b][0], dlf2[b][1]["v"]), (dlv[b][0], dlv[b][1]["v"]), ones_tok]

    S_rd = [None, None]
    PT_rd = [[] for _ in range(NPT)]
    o_free = [[], []]
    sp_free = [None, None]
    pend = []
    gbc = [0]
    gch = [0]
    ob_i = [0]

    def next_ob():
        i = ob_i[0] % NOB
        ob_i[0] += 1
        return i, (dob[i][0], dob[i][1]["v"])

    def run_pending(limit_chunk=None, force=False):
        if force:
            while pend and (limit_chunk is None or pend[0][1] <= limit_chunk):
                pend.pop(0)[2]()
            return
        for _ in range(2):
            if pend and pend[0][0] <= gbc[0]:
                pend.pop(0)[2]()

    def epilogue(u, qc, pv_tok, atok, ai):
        cid = gch[0]
        gch[0] += 1
        run_pending(limit_chunk=cid - 2, force=True)
        a = cid % 2
        q0 = qc * 512
        diff = u < 4
        SPa = Ops[a][0]
        g = gbc[0]
        if diff:
            e1 = p.op("scalar", lambda e: e.activation(co[a][0][:], Ops[a][0][:], AF.Copy), deps=[pv_tok])
            e2 = p.op("scalar", lambda e: e.activation(co[a][1][:], Ops[a][1][:], AF.Copy), deps=[pv_tok])
            o_free[a] = [e1, e2]
            sp_free[a] = e1
            stt = {}

            def s1():
                stt["m0"] = p.op("tensor", lambda e: e.matmul(SPa[:], G["ones32"][:], acc[ai][:, 0:512], start=True, stop=True), deps=[atok, sp_free[a]])

            def s2():
                tl_ = p.op("scalar", lambda e: e.activation(rsb[a][0][:], SPa[:], AF.Ln), deps=[stt["m0"]])
                sp_free[a] = tl_
                o_free[a].append(tl_)
                stt["x0"] = p.op("scalar", lambda e: e.activation(rsb[a][0][:], rsb[a][0][:], AF.Exp, scale=-1.0), deps=[tl_])

            def s3():
                t = p.op("tensor", lambda e: e.matmul(SPa[:], G["ones32"][:], acc[ai][:, 512:1024], start=True, stop=True), deps=[atok, sp_free[a]])
                stt["m1"] = t
                acc_free[ai] = t
                stt["t0"] = p.op("vector", lambda e: e.tensor_tensor(tt_[a][0][:], co[a][0][:], rsb[a][0][:], op=ALU.mult), deps=[stt["x0"], e1])

            def s4():
                tl_ = p.op("scalar", lambda e: e.activation(rsb[a][1][:], SPa[:], AF.Ln), deps=[stt["m1"]])
                sp_free[a] = tl_
                o_free[a].append(tl_)
                stt["x1"] = p.op("scalar", lambda e: e.activation(rsb[a][1][:], rsb[a][1][:], AF.Exp, scale=-1.0), deps=[tl_])

            def s5():
                t1 = p.op("vector", lambda e: e.tensor_tensor(tt_[a][1][:], co[a][1][:], rsb[a][1][:], op=ALU.mult), deps=[stt["x1"], e2])
                stt["a"] = p.op("vector", lambda e: e.scalar_tensor_tensor(av[a][:], tt_[a][1][:], G["neglam"][:, 0:1], tt_[a][0][:], op0=ALU.mult, op1=ALU.add), deps=[t1, stt["t0"]])

            def s6():
                stt["sq"] = p.op("scalar", lambda e: e.activation(sq[a][:], av[a][:], AF.Square), deps=[stt["a"]])

            def s7():
                stt["m2"] = p.op("tensor", lambda e: e.matmul(SPa[:], G["ones32"][:], sq[a][:], start=True, stop=True), deps=[stt["sq"], sp_free[a]])

            def s8():
                tl = p.op("scalar", lambda e: e.activation(rstd[a][:], SPa[:], AF.Ln, scale=1.0 / 128, bias=G["epsc"][:, 0:1]), deps=[stt["m2"]])
                sp_free[a] = tl
                o_free[a].append(tl)
                stt["x2"] = p.op("scalar", lambda e: e.activation(rstd[a][:], rstd[a][:], AF.Exp, scale=-0.5), deps=[tl])

            def s9():
                stt["d"] = p.op("vector", lambda e: e.tensor_tensor(dd[a][:], av[a][:], rstd[a][:], op=ALU.mult), deps=[stt["x2"], stt["a"]])

            def s10():
                i, obt = next_ob()
                tob = p.op("gpsimd", lambda e, i=i: e.tensor_scalar(ob[i][:], dd[a][:], G["gsub"][:, 0:1], None, op0=ALU.mult), deps=[stt["d"], obt])
                p.dma("gpsimd", I["o_s"][u * 128:(u + 1) * 128, q0:q0 + 512], ob[i][:], dob[i], deps=[tob])

            nxt_len = 4 * (qc - 1) + 4 if qc > 0 else 32
            sp = 2 if nxt_len >= 22 else 1
            for k, fn in enumerate((s1, s2, s3, s4, s5, s6, s7, s8, s9, s10)):
                pend.append((g + 2 + sp * k, cid, fn))
        else:
            es = []
            rr = []
            for m in range(2):
                es.append(p.op("vector", lambda e, m=m: e.tensor_copy(co[a][m][0:65, :], Ops[a][m][0:65, :]), deps=[pv_tok]))
            o_free[a] = [es[1]]
            sp_free[a] = es[0]
            for m in range(2):
                rr.append(p.op("vector", lambda e, m=m: e.reciprocal(rsb[a][m][64:65, :], co[a][m][64:65, :]), deps=[es[m]]))
            stt = {}

            def mk_pe(m):
                def st():
                    stt[m] = p.op("tensor", lambda e: e.matmul(SPa[0:64, :], G["ones32"][64:65, 0:64], rsb[a][m][64:65, :], start=True, stop=True), deps=[rr[m], sp_free[a]])
                return st

            def mk_dv(m):
                def st():
                    i, obt = next_ob()
                    t0 = p.op("vector", lambda e, i=i: e.tensor_tensor(ob[i][0:64, :], co[a][m][0:64, :], SPa[0:64, :], op=ALU.mult), deps=[stt[m], es[m], obt])
                    sp_free[a] = t0
                    o_free[a].append(t0)
                    hd = 2 * (u - 4) + m
                    p.dma("gpsimd", I["o_s"][512 + hd * 64:512 + (hd + 1) * 64, q0:q0 + 512], ob[i][0:64, :], dob[i], deps=[t0])
                return st
            for k, fn in enumerate((mk_pe(0), mk_dv(0), mk_pe(1), mk_dv(1))):
                pend.append((g + 5 + 2 * k, cid, fn))

    def run_unit(u):
        diff = u < 4
        b = (u % 2) if diff else ((u - 4) % 2)
        M = 128 if diff else 65
        lts = ld_tok[u]
        blocks = [(qc, j) for qc in reversed(range(8)) for j in range(4 * qc + 4)]

        def qap(m, c0, c1):
            return qd[b][64 * m:64 * m + 64, c0:c1] if diff else qf[b][m][0:70, c0:c1]

        def kap(m, j):
            return kd[b][64 * m:64 * m + 64, j * 128:(j + 1) * 128] if diff else kf[b][m][0:70, j * 128:(j + 1) * 128]

        def vap(m, j):
            return vt[b][:, j, :] if diff else vtf[b][:, j, 65 * m:65 * m + 65]

        def geom(qc, j):
            c0 = max(0, j - 4 * qc) * 128
            return c0, 512 - c0

        def QK(bi):
            qc, j = blocks[bi]
            sb_ = bi % 2
            c0, W = geom(qc, j)
            near = diff and (j >= 4 * qc - 1)
            diag = (not diff) and (j >= 4 * qc)
            last = None
            for m in range(2):
                extra = near or diag
                last = p.op("tensor", lambda e, m=m: e.matmul(Sps[sb_][:, m * 512 + c0:m * 512 + 512], kap(m, j), qap(m, qc * 512 + c0, qc * 512 + 512), start=True, stop=not extra),
                            deps=lts + [S_rd[sb_] if m == 0 else None], inc=(m == 1 and not extra))
                if near:
                    xs = qc * 512 + c0 - 128 * j
                    last = p.op("tensor", lambda e, m=m, xs=xs: e.matmul(Sps[sb_][:, m * 512 + c0:m * 512 + 512], G["ident16"][:], btile[:, u, xs:xs + W], start=False, stop=True), deps=[TOE], inc=(m == 1))
                elif diag:
                    last = p.op("tensor", lambda e, m=m: e.matmul(Sps[sb_][:, m * 512 + c0:m * 512 + 512], G["ident16"][:], cmask[:, 0:W], start=False, stop=True), deps=[CML], inc=(m == 1))
            return last

        def EXP(bi, qk_tok):
            qc, j = blocks[bi]
            sb_ = bi % 2
            pb_ = bi % NPT
            c0, W = geom(qc, j)
            far = diff and not (j >= 4 * qc - 1)
            bias = G["b31"][:, u:u + 1] if far else G["zeroc"][:, 0:1]
            if W == 512:
                src, dst = Sps[sb_][:, :], PT[pb_][:, :]
            else:
                src = Sps[sb_][:, :].rearrange("p (m w) -> p m w", m=2)[:, :, c0:512]
                dst = PT[pb_][:, :].rearrange("p (m w) -> p m w", m=2)[:, :, c0:512]
            t = p.op("scalar", lambda e: e.activation(dst, src, AF.Exp, bias=bias), deps=[qk_tok] + PT_rd[pb_])
            S_rd[sb_] = t
            return t

        def PV(bi, exp_tok):
            qc, j = blocks[bi]
            pb_ = bi % NPT
            c0, W = geom(qc, j)
            first = (j == 0)
            lastj = (j == 4 * qc + 3)
            last = None
            a_ = gch[0] % 2
            if first:
                run_pending(limit_chunk=gch[0] - 2, force=True)
            for m in range(2):
                last = p.op("tensor", lambda e, m=m: e.matmul(Ops[a_][m][0:M, c0:512], vap(m, j), PT[pb_][:, m * 512 + c0:m * 512 + 512], start=first, stop=lastj),
                            deps=[exp_tok] + lts + (o_free[a_] if first else []), inc=(m == 1))
            ai = gch[0] % 2
            ta = None
            if diff:
                if first:
                    ta = p.op("vector", lambda e: e.tensor_copy(acc[ai][:, :], PT[pb_][:, :]), deps=[exp_tok, acc_free[ai]])
                elif W == 512:
                    ta = p.op("vector", lambda e: e.tensor_tensor(acc[ai][:, :], acc[ai][:, :], PT[pb_][:, :], op=ALU.add), deps=[exp_tok, acc_tok[0]])
                else:
                    a3 = acc[ai][:, :].rearrange("p (m w) -> p m w", m=2)[:, :, c0:512]
                    p3 = PT[pb_][:, :].rearrange("p (m w) -> p m w", m=2)[:, :, c0:512]
                    ta = p.op("vector", lambda e: e.tensor_tensor(a3, a3, p3, op=ALU.add), deps=[exp_tok, acc_tok[0]])
                acc_tok[0] = ta
                PT_rd[pb_] = [last, ta]
            else:
                PT_rd[pb_] = [last]
            gbc[0] += 1
            if lastj:
                epilogue(u, qc, last, ta, ai)
            else:
                run_pending()
            return last

        nb = len(blocks)
        qk = QK(0)
        last = None
        for bi in range(nb):
            ex = EXP(bi, qk)
            if bi + 1 < nb:
                qk = QK(bi + 1)
            last = PV(bi, ex)
        if diff:
            unit_rd[b] = last
            vt_rd[b] = last
        else:
            fox_rd[0] = last
            foxb_rd[b] = last
            vtf_rd[b] = last

    load_unit(0)
    for u in range(8):
        if u + 1 < 8:
            load_unit(u + 1)
        run_unit(u)
    run_pending(force=True)
    p.wait("gpsimd", [(d[0], d[1]["v"]) for d in dob])
    p.build()


def phase4(nc, I, G):
    p = Prog(nc, "p4")
    CH = 256
    NCH = S // CH
    wup = p.sb("wup", [128, 8, 2 * DFF], BF16)
    wdn = p.sb("wdn", [128, 22, D], BF16)
    wo = p.sb("wo", [128, 8, D], BF16)
    dwo = p.dsem("wo")
    dwu = p.dsem("wu")
    dwd = p.dsem("wd")
    for c in range(8):
        p.dma("gpsimd", wo[:, c, :], I["w_out"][c * 128:(c + 1) * 128, :], dwo)
    NG = 11
    dwug = [p.dsem("wug%d" % g) for g in range(NG)]
    wsrc = I["w_up"].rearrange("(c p) n -> p c n", p=128)
    WUg = []
    for g in range(NG):
        for base in (0, DFF):
            c0 = base + g * 256
            p.dma("gpsimd", wup[:, :, c0:c0 + 256], wsrc[:, :, c0:c0 + 256], dwug[g])
        WUg.append((dwug[g][0], dwug[g][1]["v"]))
    for c in range(22):
        p.dma("gpsimd", wdn[:, c, :], I["w_down"][c * 128:(c + 1) * 128, :], dwd)
    WO = (dwo[0], dwo[1]["v"])
    WU = (dwu[0], dwu[1]["v"])
    WD = (dwd[0], dwd[1]["v"])
    oT = [p.sb("oT%d" % i, [128, 8, CH], BF16) for i in range(2)]
    doT = [p.dsem("oT%d" % i) for i in range(2)]
    xt = [p.sb("xt%d" % i, [128, D], F32) for i in range(2)]
    dx = [p.dsem("x%d" % i) for i in range(2)]
    dot_ = [p.dsem("ot%d" % i) for i in range(2)]
    xn = [p.sb("xn%d" % i, [128, D], BF16) for i in range(2)]
    ss = p.sb("ss", [128, 4 * NT], F32)
    h2T = p.sb("h2T", [128, 8, CH], BF16)
    yT = p.sb("yT", [128, 22, CH], BF16)
    ug = [p.sb("ug%d" % i, [128, CH + 2], F32) for i in range(1)]
    uv = [p.sb("uv%d" % i, [128, CH + 2], F32) for i in range(1)]
    tmp = p.sb("tmp", [128, 512], F32)
    gc = [p.sb("gc%d" % i, [128, CH], F32) for i in range(1)]
    vc = [p.sb("vc%d" % i, [128, CH], F32) for i in range(1)]
    sgt = [p.sb("sg%d" % i, [128, CH], F32) for i in range(1)]
    carry = p.sb("carry", [128, 44, 2], F32)
    pm = [p.ps("pm%d" % i, [128, 512], F32) for i in range(4)]
    pst = [p.ps("pst%d" % i, [128, 512], BF16) for i in range(2)]
    pgv = [p.ps("pgv%d" % i, [128, 512], F32) for i in range(2)]
    tz = p.op("gpsimd", lambda e: e.memset(carry[:], 0.0))
    CW = G["convw"]
    CB = G["convb"]

    pm_rd = [None] * 4
    pst_rd = [None, None]
    pgv_rd = [[None, None], [None, None]]
    oT_rd = [None, None]
    oT_ld = {}
    xt_rd = [None, None]
    xn_rd = [None, None]
    h2T_rd = [None]
    yT_rd = [None]
    ug_rd = [None, None]
    uv_rd = [None, None]
    gc_rd = [None]
    vc_rd = [None]
    sg_rd = [None]
    cnt = {"f": 0, "ss": 0}
    carry_w = [tz] * 44
    ugc_rd = [None, None]
    uvc_rd = [None, None]

    def rstd_of(src_ap, dep, dump, dump_dep):
        k = cnt["ss"]
        cnt["ss"] += 1
        a = p.op("scalar", lambda e: e.activation(dump, src_ap, AF.Square, accum_out=ss[:, k:k + 1]), deps=[dep, dump_dep])
        b_ = p.op("scalar", lambda e: e.activation(ss[:, k:k + 1], ss[:, k:k + 1], AF.Ln, scale=1.0 / D, bias=G["epsc"][:, 0:1]), deps=[a])
        c_ = p.op("scalar", lambda e: e.activation(ss[:, k:k + 1], ss[:, k:k + 1], AF.Exp, scale=-0.5), deps=[b_])
        return ss[:, k:k + 1], c_

    def load_oT(ci):
        ob_ = ci % 2
        tok0 = ci * CH
        oT_ld[ci] = p.dma("sync", oT[ob_][:], I["o_s"][:, tok0:tok0 + CH].rearrange("(c p) t -> p c t", p=128), doT[ob_], deps=[oT_rd[ob_]])

    def emit_mix(ci):
        ob_ = ci % 2
        res = {}
        mm = None
        for t in range(2):
            for half in range(2):
                pi = t * 2 + half
                for c in range(8):
                    mm = p.op("tensor", lambda e, c=c, pi=pi, t=t, half=half: e.matmul(pm[pi][:], oT[ob_][:, c, t * 128:(t + 1) * 128], wo[:, c, half * 512:(half + 1) * 512], start=(c == 0), stop=(c == 7)),
                              deps=[oT_ld[ci], WO, pm_rd[pi] if c == 0 else None], inc=(c == 7))
                res[pi] = mm
        oT_rd[ob_] = mm
        return res

    load_oT(0)
    mixres = emit_mix(0)
    for ci in range(NCH):
        tok0 = ci * CH
        if ci + 1 < NCH:
            load_oT(ci + 1)
        x1tok = []
        for t in range(2):
            sl = t
            r0 = tok0 + t * 128
            tx = p.dma("sync", xt[sl][:], I["x"][r0:r0 + 128, :], dx[sl], deps=[xt_rd[sl]])
            lastadd = None
            for half in range(2):
                pi = t * 2 + half
                t1 = p.op("vector", lambda e, pi=pi, half=half: e.tensor_tensor(tmp[:], pm[pi][:], G["g1_bc"][:, half * 512:(half + 1) * 512], op=ALU.mult), deps=[mixres[pi], lastadd])
                pm_rd[pi] = t1
                lastadd = p.op("vector", lambda e, sl=sl, half=half, pi=pi: e.tensor_tensor(xt[sl][:, half * 512:(half + 1) * 512], tmp[:], xt[sl][:, half * 512:(half + 1) * 512], op=ALU.add), deps=[t1, tx])
            rs_ap, rtok = rstd_of(xt[sl][:], lastadd, xn[t][:], xn_rd[t])
            tn = p.op("vector", lambda e, sl=sl, t=t, rs_ap=rs_ap: e.tensor_scalar(xn[t][:], xt[sl][:], rs_ap, None, op0=ALU.mult), deps=[rtok, lastadd, xn_rd[t]])
            x1tok.append(tn)
        ev = [None] * 8
        tt = None
        for c in range(8):
            for t in range(2):
                tt = p.op("tensor", lambda e, c=c, t=t: e.transpose(pst[c % 2][:, t * 128:(t + 1) * 128], xn[t][:, c * 128:(c + 1) * 128], G["ident16"][:]),
                          deps=[x1tok[t], pst_rd[c % 2] if t == 0 else None], inc=(t == 1))
            ev[c] = p.op("vector", lambda e, c=c: e.tensor_scalar(h2T[:, c, :], pst[c % 2][:, 0:CH], G["gsc2"][:, c:c + 1], G["modT"][:, 24 + c:25 + c], op0=ALU.mult, op1=ALU.add),
                         deps=[tt, h2T_rd[0] if c == 0 else None])
            pst_rd[c % 2] = ev[c]
        xn_rd[0] = tt
        xn_rd[1] = tt
        ylast = None
        for fc in range(22):
            fj = cnt["f"] % 2
            cnt["f"] += 1
            PG = pgv[fj][:, 0:CH]
            PV_ = pgv[fj][:, 256:256 + CH]
            mg = mv = None
            for c in range(8):
                mg = p.op("tensor", lambda e, c=c, fc=fc, PG=PG: e.matmul(PG, wup[:, c, fc * 128:(fc + 1) * 128], h2T[:, c, :], start=(c == 0), stop=(c == 7)),
                          deps=[ev[c], WUg[fc // 2]] + (pgv_rd[fj] if c == 0 else []), inc=(c == 7))
            for c in range(8):
                mv = p.op("tensor", lambda e, c=c, fc=fc, PV_=PV_: e.matmul(PV_, wup[:, c, DFF + fc * 128:DFF + (fc + 1) * 128], h2T[:, c, :], start=(c == 0), stop=(c == 7)),
                          deps=[ev[c]], inc=(c == 7))
            h1 = p.op("vector", lambda e, fc=fc, fj=fj: e.tensor_copy(ug[0][:, 0:2], carry[:, fc, :]), deps=[carry_w[fc], ug_rd[0]])
            h2 = p.op("vector", lambda e, fc=fc, fj=fj: e.tensor_copy(uv[0][:, 0:2], carry[:, 22 + fc, :]), deps=[carry_w[22 + fc], uv_rd[0]])
            eg = p.op("scalar", lambda e, fj=fj, PG=PG: e.activation(ug[0][:, 2:CH + 2], PG, AF.Copy), deps=[mv, ug_rd[0], ugc_rd[0]])
            g1_ = p.op("scalar", lambda e, fc=fc, PG=PG: e.activation(gc[0][:], PG, AF.Identity, scale=CW[:, 2, fc:fc + 1], bias=CB[:, fc:fc + 1]), deps=[mv, gc_rd[0]])
            cg = p.op("gpsimd", lambda e, fc=fc, fj=fj: e.tensor_copy(carry[:, fc, :], ug[0][:, CH:CH + 2]), deps=[eg, h1])
            carry_w[fc] = cg
            ugc_rd[0] = cg
            evv = p.op("scalar", lambda e, fj=fj, PV_=PV_: e.activation(uv[0][:, 2:CH + 2], PV_, AF.Copy), deps=[mv, uv_rd[0], uvc_rd[0]])
            v1_ = p.op("scalar", lambda e, fc=fc, PV_=PV_: e.activation(vc[0][:], PV_, AF.Identity, scale=CW[:, 2, 22 + fc:23 + fc], bias=CB[:, 22 + fc:23 + fc]), deps=[mv, vc_rd[0]])
            cv = p.op("gpsimd", lambda e, fc=fc, fj=fj: e.tensor_copy(carry[:, 22 + fc, :], uv[0][:, CH:CH + 2]), deps=[evv, h2])
            carry_w[22 + fc] = cv
            uvc_rd[0] = cv
            pgv_rd[fj] = [g1_, v1_]
            g2_ = p.op("vector", lambda e, fc=fc, fj=fj: e.scalar_tensor_tensor(gc[0][:], ug[0][:, 1:CH + 1], CW[:, 1, fc:fc + 1], gc[0][:], op0=ALU.mult, op1=ALU.add), deps=[g1_, eg, h1])
            g3_ = p.op("vector", lambda e, fc=fc, fj=fj: e.scalar_tensor_tensor(gc[0][:], ug[0][:, 0:CH], CW[:, 0, fc:fc + 1], gc[0][:], op0=ALU.mult, op1=ALU.add), deps=[g2_])
            ug_rd[0] = g3_
            v2_ = p.op("vector", lambda e, fc=fc, fj=fj: e.scalar_tensor_tensor(vc[0][:], uv[0][:, 1:CH + 1], CW[:, 1, 22 + fc:23 + fc], vc[0][:], op0=ALU.mult, op1=ALU.add), deps=[v1_, evv, h2])
            v3_ = p.op("vector", lambda e, fc=fc, fj=fj: e.scalar_tensor_tensor(vc[0][:], uv[0][:, 0:CH], CW[:, 0, 22 + fc:23 + fc], vc[0][:], op0=ALU.mult, op1=ALU.add), deps=[v2_])
            uv_rd[0] = v3_
            s1 = p.op("scalar", lambda e: e.activation(sgt[0][:], gc[0][:], AF.Silu), deps=[g3_, sg_rd[0]])
            gc_rd[0] = s1
            y1 = p.op("gpsimd", lambda e, fc=fc: e.tensor_tensor(yT[:, fc, :], sgt[0][:], vc[0][:], op=ALU.mult), deps=[s1, v3_, yT_rd[0] if fc == 0 else None])
            sg_rd[0] = y1
            vc_rd[0] = y1
            ylast = y1
        h2T_rd[0] = mv
        wd_tok = {}
        mm = None
        for t in range(2):
            for half in range(2):
                pi = t * 2 + half
                for fc in range(22):
                    mm = p.op("tensor", lambda e, fc=fc, pi=pi, t=t, half=half: e.matmul(pm[pi][:], yT[:, fc, t * 128:(t + 1) * 128], wdn[:, fc, half * 512:(half + 1) * 512], start=(fc == 0), stop=(fc == 21)),
                              deps=[ylast, WD, pm_rd[pi] if fc == 0 else None], inc=(fc == 21))
                wd_tok[pi] = mm
        yT_rd[0] = mm
        fin = []
        for t in range(2):
            sl = t
            r0 = tok0 + t * 128
            lastadd = None
            for half in range(2):
                pi = t * 2 + half
                t1 = p.op("vector", lambda e, pi=pi, half=half: e.tensor_tensor(tmp[:], pm[pi][:], G["g2_bc"][:, half * 512:(half + 1) * 512], op=ALU.mult), deps=[wd_tok[pi], lastadd])
                pm_rd[pi] = t1
                lastadd = p.op("vector", lambda e, sl=sl, half=half, pi=pi: e.tensor_tensor(xt[sl][:, half * 512:(half + 1) * 512], tmp[:], xt[sl][:, half * 512:(half + 1) * 512], op=ALU.add), deps=[t1])
            rs_ap, rtok = rstd_of(xt[sl][:], lastadd, xn[t][:], xn_rd[t])
            tf = p.op("vector", lambda e, sl=sl, rs_ap=rs_ap: e.scalar_tensor_tensor(xt[sl][:], xt[sl][:], rs_ap, G["fg_bc"][:], op0=ALU.mult, op1=ALU.mult), deps=[rtok, lastadd])
            xt_rd[sl] = p.dma("sync", I["out"][r0:r0 + 128, :], xt[sl][:], dot_[sl], deps=[tf])
        if ci + 1 < NCH:
            mixres = emit_mix(ci + 1)
    p.wait("sync", [(d[0], d[1]["v"]) for d in dot_])
    p.build()


def build_nc():
    nc = bass.Bass("TRN2", target_bir_lowering=False)
    I = {}

    H = {}
    I["_H"] = H

    def inp(name, shape, dt=F32):
        H[name] = nc.dram_tensor(name, list(shape), dt, kind="ExternalInput")
        I[name] = H[name].ap()

    inp("x", [S, D]); inp("cT", [128, 8]); inp("ada_w", [D, 6 * D]); inp("adabT", [128, 48])
    inp("gatT", [128, 8]); inp("gffT", [128, 8]); inp("fng", [1, D]); inp("w_in", [D, NIN])
    inp("fb", [8, 1]); inp("lam4", [1, 256]); inp("subg", [128, 1]); inp("relb", [32, 4])
    inp("w_out", [D, D]); inp("w_up", [D, 2 * DFF]); inp("convwT", [128, 3, 44]); inp("convbT", [128, 44])
    inp("w_down", [DFF, D]); inp("segtri", [128, 128]); inp("fb128", [128, 1]); inp("onehot", [33, 768]); inp("ident", [128, 128]); inp("cmask", [128, 512])
    I["out"] = nc.dram_tensor("out", [S, D], F32, kind="ExternalOutput").ap()
    sk = "ExternalOutput" if DEBUG else "Internal"
    I["qk_s"] = nc.dram_tensor("qk_s", [2048, S], BF16, kind=sk).ap()
    I["v_s"] = nc.dram_tensor("v_s", [S, 1024], BF16, kind=sk).ap()
    I["o_s"] = nc.dram_tensor("o_s", [1024, S], BF16, kind=sk).ap()
    I["fl_s"] = nc.dram_tensor("fl_s", [8, S], F32, kind=sk).ap()
    I["bt_s"] = nc.dram_tensor("bt_s", [128, 4 * 640], BF16, kind="Internal").ap()
    I["augq"] = nc.dram_tensor("augq", [6, 8, S], BF16, kind=sk).ap()
    I["augk"] = nc.dram_tensor("augk", [6, 8, S], BF16, kind=sk).ap()
    H["tscr"] = nc.dram_tensor("tscr", [4 * 128 * 768], F32, kind="Internal")

    ctx = []

    def gsb(name, shape, dt):
        cm = nc.sbuf_tensor(name, list(shape), dt)
        t = cm.__enter__()
        ctx.append(cm)
        return t

    G = {
        "ident16": gsb("g_ident16", [128, 128], BF16), "ones16": gsb("g_ones16", [128, 128], BF16),
        "ones32": gsb("g_ones32", [128, 128], F32),
        "modT": gsb("g_modT", [128, 48], F32), "gsc1": gsb("g_gsc1", [128, 8], F32), "gsc2": gsb("g_gsc2", [128, 8], F32),
        "g1_bc": gsb("g_g1bc", [128, D], F32), "g2_bc": gsb("g_g2bc", [128, D], F32), "fg_bc": gsb("g_fgbc", [128, D], F32),
        "neglam": gsb("g_neglam", [128, 1], F32), "gsub": gsb("g_gsub", [128, 1], F32), "nfb": gsb("g_nfb", [8, 1], F32),
        "b31": gsb("g_b31", [128, 4], F32),
        "convw": gsb("g_convw", [128, 3, 44], F32), "convb": gsb("g_convb", [128, 44], F32),
        "epsc": gsb("g_epsc", [128, 1], F32), "onec": gsb("g_onec", [128, 1], F32), "zeroc": gsb("g_zeroc", [128, 1], F32),
    }
    phase0(nc, I, G)
    phase1(nc, I, G)
    phase2(nc, I, G)
    phase3(nc, I, G)
    phase4(nc, I, G)
    for cm in reversed(ctx):
        cm.__exit__(None, None, None)
    return nc


def _bucket_table():
    n = np.arange(640, dtype=np.int64)
    nf = np.maximum(n, 1).astype(np.float32)
    large = 16 + (np.log(nf / np.float32(16)) / np.float32(math.log(128 / 16)) * np.float32(16)).astype(np.int32)
    large = np.minimum(large, 31)
    return np.where(n < 16, n, large)


def make_in_maps(inputs):
    f = lambda a: np.ascontiguousarray(np.asarray(a, dtype=np.float32))
    x = f(inputs["x"]); c = f(inputs["c"])
    bk = _bucket_table()
    onehot = np.zeros((33, 768), np.float32)
    onehot[32, :127] = 1.0
    for m in range(127, 767):
        onehot[bk[m - 127], m] = 1.0
    ident = np.eye(128, dtype=np.float32)
    kk = np.arange(128)[:, None]; xx = np.arange(512)[None, :]
    cmask = np.where(xx >= kk, 0.0, -1e30).astype(np.float32)
    cw = f(inputs["conv_w"])[0]
    pi_ = np.arange(128)
    segtri = ((pi_[:, None] // 16 == pi_[None, :] // 16) & (pi_[:, None] % 16 < pi_[None, :] % 16)).astype(np.float32)
    shared = {
        "ada_w": f(inputs["ada_w"])[0],
        "adabT": np.ascontiguousarray(f(inputs["ada_b"])[0].reshape(48, 128).T),
        "gatT": np.ascontiguousarray(f(inputs["attn_norm_g"])[0].reshape(8, 128).T),
        "gffT": np.ascontiguousarray(f(inputs["ffn_norm_g"])[0].reshape(8, 128).T),
        "fng": f(inputs["final_norm_g"]).reshape(1, D),
        "w_in": f(inputs["w_in"])[0],
        "fb": f(inputs["forget_b"])[0].reshape(8, 1),
        "lam4": np.concatenate([f(inputs["lambda_q1"])[0], f(inputs["lambda_k1"])[0], f(inputs["lambda_q2"])[0], f(inputs["lambda_k2"])[0]]).reshape(1, 256),
        "subg": f(inputs["subln_g"])[0].reshape(128, 1),
        "relb": f(inputs["rel_bias"]),
        "w_out": f(inputs["w_out"])[0],
        "w_up": f(inputs["w_up"])[0],
        "convwT": np.ascontiguousarray(cw.reshape(3, 44, 128).transpose(2, 0, 1)),
        "convbT": np.ascontiguousarray(f(inputs["conv_b"])[0].reshape(44, 128).T),
        "w_down": f(inputs["w_down"])[0],
        "onehot": onehot, "ident": ident, "cmask": cmask,
        "segtri": segtri, "fb128": np.ascontiguousarray(np.repeat(f(inputs["forget_b"])[0], 16).reshape(128, 1)),
    }
    maps = []
    for b in range(8):
        m = dict(shared)
        m["x"] = np.ascontiguousarray(x[b])
        m["cT"] = np.ascontiguousarray(c[b].reshape(8, 128).T)
        maps.append(m)
    return maps


def kernel(**inputs):
    nc = build_nc()
    in_maps = make_in_maps(inputs)
    res = run_bass_kernel_spmd(nc, in_maps, core_ids=list(range(8)))
    out = np.stack([np.asarray(r["out"], dtype=np.float32) for r in res.results], axis=0)
    return out
```

```python
import math
import os
import numpy as np
import concourse.bass as bass
import concourse.mybir as mybir
from concourse.bass_utils import run_bass_kernel_spmd

F32 = mybir.dt.float32
BF16 = mybir.dt.bfloat16
AF = mybir.ActivationFunctionType
ALU = mybir.AluOpType

ENGS = ("sync", "scalar", "vector", "gpsimd", "tensor")
S = 4096
D = 1024
NT = S // 128
NIN = 3080
DFF = 2816
EPS = 1e-6
LAMBDA_INIT = 0.8 - 0.6 * math.exp(-0.3 * 0)
DEBUG = bool(int(os.environ.get("MK_DEBUG", "0")))


class Prog:
    def __init__(self, nc, tag):
        self.nc = nc
        self.tag = tag
        self.ops = {e: [] for e in ENGS}
        self.cnt = {}
        self.n = {e: 0 for e in ENGS}
        self.waited = {e: {} for e in ENGS}
        self._ctx = []
        for e in ENGS:
            self.cnt[e] = self.sem("cnt_" + e)

    def sem(self, name):
        cm = self.nc.semaphore(self.tag + "_" + name)
        s = cm.__enter__()
        self._ctx.append(cm)
        return s

    def dsem(self, name):
        return [self.sem(name), {"v": 0}]

    def sb(self, name, shape, dt):
        cm = self.nc.sbuf_tensor(self.tag + "_" + name, list(shape), dt)
        t = cm.__enter__()
        self._ctx.append(cm)
        return t

    def ps(self, name, shape, dt=F32):
        cm = self.nc.psum_tensor(self.tag + "_" + name, list(shape), dt)
        t = cm.__enter__()
        self._ctx.append(cm)
        return t

    def _waits(self, eng, deps):
        w = []
        for d in deps:
            if d is None:
                continue
            sem, val = d
            key = id(sem)
            if self.waited[eng].get(key, 0) >= val:
                continue
            self.waited[eng][key] = val
            w.append((sem, val))
        return w

    def op(self, eng, fn, deps=(), inc=True):
        w = self._waits(eng, deps)
        tok = None
        if inc:
            self.n[eng] += 1
            tok = (self.cnt[eng], self.n[eng])
        cnt = self.cnt[eng]

        def run(e, w=w, fn=fn, inc=inc, cnt=cnt):
            for sem, val in w:
                e.wait_ge(sem, val)
            ins = fn(e)
            if inc:
                ins.then_inc(cnt, 1)
        self.ops[eng].append(run)
        return tok

    def dma(self, eng, out, in_, ds, deps=()):
        w = self._waits(eng, deps)
        sem, sv = ds
        sv["v"] += 16
        tok = (sem, sv["v"])

        def run(e, w=w):
            for s, val in w:
                e.wait_ge(s, val)
            e.dma_start(out=out, in_=in_).then_inc(sem, 16)
        self.ops[eng].append(run)
        return tok

    def wait(self, eng, deps):
        w = self._waits(eng, deps)

        def run(e, w=w):
            for s, val in w:
                e.wait_ge(s, val)
        self.ops[eng].append(run)

    def build(self):
        nc = self.nc
        ops = self.ops
        with nc.Block() as block:
            @block.sync
            def _(e):
                for f in ops["sync"]:
                    f(e)

            @block.scalar
            def _(e):
                for f in ops["scalar"]:
                    f(e)

            @block.vector
            def _(e):
                for f in ops["vector"]:
                    f(e)

            @block.gpsimd
            def _(e):
                for f in ops["gpsimd"]:
                    f(e)

            @block.tensor
            def _(e):
                for f in ops["tensor"]:
                    f(e)
        for cm in reversed(self._ctx):
            cm.__exit__(None, None, None)
        self._ctx = []


def phase0(nc, I, G):
    p = Prog(nc, "p0")
    ds = p.dsem("ld")
    dw = [p.dsem("aw%d" % i) for i in range(4)]
    dsc = p.dsem("scr")
    cT = p.sb("cT", [128, 8], F32)
    adab = p.sb("adab", [128, 48], F32)
    gat = p.sb("gat", [128, 8], F32)
    gff = p.sb("gff", [128, 8], F32)
    lq = p.sb("lq", [1, 256], F32)
    fb = p.sb("fb", [8, 1], F32)
    sg = p.sb("sg", [128, 1], F32)
    id32 = p.sb("id32", [128, 128], F32)
    loads = [
        p.dma("sync", cT[:], I["cT"], ds), p.dma("sync", adab[:], I["adabT"], ds),
        p.dma("sync", gat[:], I["gatT"], ds), p.dma("sync", gff[:], I["gffT"], ds),
        p.dma("sync", lq[:], I["lam4"], ds), p.dma("sync", fb[:], I["fb"], ds),
        p.dma("sync", sg[:], I["subg"], ds), p.dma("sync", id32[:], I["ident"], ds),
        p.dma("sync", G["fg_bc"][:], bass.AP(I["_H"]["fng"], 0, [[0, 128], [1, 1024]]), ds),
        p.dma("sync", G["b31"][:], bass.AP(I["_H"]["relb"], 31 * 4, [[0, 128], [1, 4]]), ds),
        p.dma("sync", G["convw"][:], I["convwT"], ds), p.dma("sync", G["convb"][:], I["convbT"], ds),
        p.dma("gpsimd", G["ident16"][:], I["ident"], ds),
    ]
    LD = loads[-1]
    LD = (ds[0], ds[1]["v"])
    m_ = [p.op("gpsimd", lambda e: e.memset(G["epsc"][:], EPS)),
          p.op("gpsimd", lambda e: e.memset(G["onec"][:], 1.0)),
          p.op("gpsimd", lambda e: e.memset(G["zeroc"][:], 0.0)),
          p.op("gpsimd", lambda e: e.memset(G["ones32"][:], 1.0)),
          p.op("gpsimd", lambda e: e.memset(G["ones16"][:], 1.0)),
          p.op("gpsimd", lambda e: e.memset(G["ones16"][:], 1.0), deps=[LD])]
    MS = m_[-1]
    cact = p.sb("cact", [128, 8], F32)
    t_c = p.op("scalar", lambda e: e.activation(cact[:], cT[:], AF.Silu), deps=[LD])
    aw = [p.sb("aw%d" % i, [128, 6144], F32) for i in range(4)]
    acc = p.sb("acc", [128, 6144], F32)
    ta = None
    rd = [None] * 4
    tls = []
    for c in range(8):
        if c >= 4:
            pass
        tls.append(None)
    for c in range(8):
        if c < 4:
            tls[c] = p.dma("sync", aw[c % 4][:], I["ada_w"][c * 128:(c + 1) * 128, :], dw[c % 4])
    for c in range(8):
        tl = tls[c]
        if c == 0:
            ta = p.op("vector", lambda e, c=c: e.tensor_scalar(acc[:], aw[0][:], cact[:, 0:1], None, op0=ALU.mult), deps=[tl, t_c])
        else:
            ta = p.op("vector", lambda e, c=c: e.scalar_tensor_tensor(acc[:], aw[c % 4][:], cact[:, c:c + 1], acc[:], op0=ALU.mult, op1=ALU.add), deps=[tl, ta])
        rd[c % 4] = ta
        if c + 4 < 8:
            tls[c + 4] = p.dma("sync", aw[c % 4][:], I["ada_w"][(c + 4) * 128:(c + 5) * 128, :], dw[c % 4], deps=[ta])
    pm = p.ps("pm", [128, 512], F32)
    tm = None
    for j in range(48):
        tm = p.op("tensor", lambda e, j=j: e.matmul(pm[:, j:j + 1], acc[:, j * 128:(j + 1) * 128], G["ones32"][:, 0:1], start=True, stop=True), deps=[ta, MS], inc=(j == 47))
    modT = G["modT"]
    t_mod = p.op("vector", lambda e: e.tensor_tensor(modT[:], pm[:, 0:48], adab[:], op=ALU.add), deps=[tm, LD])
    t1 = p.op("vector", lambda e: e.scalar_tensor_tensor(G["gsc1"][:], modT[:, 8:16], 1.0, gat[:], op0=ALU.add, op1=ALU.mult), deps=[t_mod])
    t2 = p.op("vector", lambda e: e.scalar_tensor_tensor(G["gsc2"][:], modT[:, 32:40], 1.0, gff[:], op0=ALU.add, op1=ALU.mult), deps=[t_mod])
    dg = p.sb("dg", [128, 16, 128], F32)
    pb = [p.ps("pb0", [128, 512], F32), p.ps("pb1", [128, 512], F32)]
    tcp = [None, None]
    for which, base, dst in ((0, 16, G["g1_bc"]), (1, 40, G["g2_bc"])):
        for c in range(8):
            k = which * 8 + c
            td = p.op("vector", lambda e, k=k, c=c, base=base: e.tensor_scalar(dg[:, k, :], id32[:], modT[:, base + c:base + c + 1], None, op0=ALU.mult), deps=[t_mod, LD])
            bi = k // 4
            tmm = p.op("tensor", lambda e, k=k, bi=bi: e.matmul(pb[bi % 2][:, (k % 4) * 128:(k % 4 + 1) * 128], G["ones32"][:], dg[:, k, :], start=True, stop=True), deps=[td, MS, tcp[bi % 2] if k % 4 == 0 else None])
            if k % 4 == 3:
                half = (k % 8) // 4
                tcp[bi % 2] = p.op("vector", lambda e, bi=bi, dst=dst, half=half: e.tensor_copy(dst[:, half * 512:(half + 1) * 512], pb[bi % 2][:]), deps=[tmm])
    lp = p.sb("lp", [1, 128], F32)
    ls = p.sb("ls", [1, 4], F32)
    tl1 = p.op("vector", lambda e: e.tensor_tensor(lp[:, 0:64], lq[:, 0:64], lq[:, 64:128], op=ALU.mult), deps=[LD])
    tl2 = p.op("vector", lambda e: e.tensor_tensor(lp[:, 64:128], lq[:, 128:192], lq[:, 192:256], op=ALU.mult), deps=[LD])
    tl3 = p.op("vector", lambda e: e.reduce_sum(ls[:, 0:1], lp[:, 0:64], axis=mybir.AxisListType.X), deps=[tl1])
    tl4 = p.op("vector", lambda e: e.reduce_sum(ls[:, 1:2], lp[:, 64:128], axis=mybir.AxisListType.X), deps=[tl2])
    tl5 = p.op("scalar", lambda e: e.activation(ls[:, 2:4], ls[:, 0:2], AF.Exp), deps=[tl3, tl4])
    tl6 = p.op("vector", lambda e: e.scalar_tensor_tensor(ls[:, 0:1], ls[:, 3:4], -LAMBDA_INIT, ls[:, 2:3], op0=ALU.add, op1=ALU.subtract), deps=[tl5])
    tl7 = p.op("tensor", lambda e: e.matmul(pm[:, 64:65], G["ones32"][0:1, :], ls[0:1, 0:1], start=True, stop=True), deps=[tl6, MS, t_mod])
    tl8 = p.op("vector", lambda e: e.tensor_copy(G["neglam"][:], pm[:, 64:65]), deps=[tl7])
    p.op("vector", lambda e: e.tensor_scalar(G["gsub"][:], sg[:], 1.0 - LAMBDA_INIT, None, op0=ALU.mult), deps=[LD])
    p.op("vector", lambda e: e.tensor_scalar(G["nfb"][:], fb[:], -1.0, None, op0=ALU.mult), deps=[LD])
    p.build()


def phase1(nc, I, G):
    p = Prog(nc, "p1")
    w16 = p.sb("w16", [128, 8, NIN], BF16)
    wsrc = I["w_in"].rearrange("(c p) n -> p c n", p=128)
    wgroups = [(0, 512), (512, 1024), (1536, 2048), (2048, 2560), (1024, 1536), (2560, NIN)]
    wgtok = []
    for gi, (ca, cb) in enumerate(wgroups):
        dwg = p.dsem("wg%d" % gi)
        wgtok.append(p.dma("gpsimd", w16[:, :, ca:cb], wsrc[:, :, ca:cb], dwg))

    def wtok(col):
        for gi, (ca, cb) in enumerate(wgroups):
            if ca <= col < cb:
                return wgtok[gi]
        raise ValueError(col)
    xt = [p.sb("xt%d" % i, [128, D], F32) for i in range(8)]
    dx = [p.dsem("x%d" % i) for i in range(8)]
    xn = [p.sb("xn%d" % i, [128, D], BF16) for i in range(8)]
    junk = p.sb("junk", [128, D], F32)
    ss = p.sb("ss", [128, NT], F32)
    rs = p.sb("rs", [128, NT], F32)
    hT = [p.sb("hT%d" % i, [128, 8, 512], BF16) for i in range(2)]
    pst = [p.ps("pst%d" % i, [128, 512], BF16) for i in range(2)]
    pp = [p.ps("pp%d" % i, [128, 512], F32) for i in range(4)]
    stg = [p.sb("stg%d" % i, [128, 512], BF16) for i in range(4)]
    dstg = [p.dsem("stg%d" % i) for i in range(4)]
    flst = [p.sb("flst%d" % i, [8, 512], F32) for i in range(2)]
    dfl = [p.dsem("fl%d" % i) for i in range(2)]
    fl_rd = [None, None]
    xn_rd = [None] * 8
    xt_rd = [None] * 8
    hT_rd = [None, None]
    pst_rd = [None, None]
    pp_rd = [None] * 4
    stg_rd = [None] * 4
    hT_ev = [[None] * 8, [None] * 8]
    st = {"pp": 0, "stg": 0, "alt": 0}
    fl_tok = []
    ptoe = p.ps("ptoe", [128, 1024], F32)
    btile1 = p.sb("btile1", [128, 4, 640], BF16)
    toe_tok, _ = build_toeplitz(p, I, G, btile1, None, ptoe)
    dbt = p.dsem("bts")
    bt_store = p.dma("gpsimd", I["bt_s"].rearrange("p (h x) -> p h x", h=4), btile1[:], dbt, deps=[toe_tok])

    def norm_tr(tc):
        xnt = []
        for t in range(4):
            sl = (tc % 2) * 4 + t
            idx = tc * 4 + t
            tx = p.dma("sync", xt[sl][:], I["x"][idx * 128:(idx + 1) * 128, :], dx[sl], deps=[xt_rd[sl]])
            tsq = p.op("scalar", lambda e, sl=sl, idx=idx: e.activation(junk[:], xt[sl][:], AF.Square, accum_out=ss[:, idx:idx + 1]), deps=[tx])
            tln = p.op("scalar", lambda e, idx=idx: e.activation(rs[:, idx:idx + 1], ss[:, idx:idx + 1], AF.Ln, scale=1.0 / D, bias=G["epsc"][:, 0:1]), deps=[tsq])
            tex = p.op("scalar", lambda e, idx=idx: e.activation(rs[:, idx:idx + 1], rs[:, idx:idx + 1], AF.Exp, scale=-0.5), deps=[tln])
            tn = p.op("vector", lambda e, sl=sl, idx=idx: e.tensor_scalar(xn[sl][:], xt[sl][:], rs[:, idx:idx + 1], None, op0=ALU.mult), deps=[tex, tx, xn_rd[sl]])
            xt_rd[sl] = tn
            xnt.append(tn)
        hb = tc % 2
        for c in range(8):
            tt = None
            for t in range(4):
                sl = (tc % 2) * 4 + t
                tt = p.op("tensor", lambda e, c=c, t=t, sl=sl: e.transpose(pst[c % 2][:, t * 128:(t + 1) * 128], xn[sl][:, c * 128:(c + 1) * 128], G["ident16"][:]),
                          deps=[xnt[t], pst_rd[c % 2] if t == 0 else None], inc=(t == 3))
            tev = p.op("vector", lambda e, c=c, hb=hb: e.tensor_scalar(hT[hb][:, c, :], pst[c % 2][:], G["gsc1"][:, c:c + 1], G["modT"][:, c:c + 1], op0=ALU.mult, op1=ALU.add),
                       deps=[tt, hT_rd[hb] if c == 0 else None])
            pst_rd[c % 2] = tev
            hT_ev[hb][c] = tev
        for t in range(4):
            xn_rd[(tc % 2) * 4 + t] = tt

    def evac(src, dst_dram, scale, rows=128):
        si = st["stg"] % 4
        st["stg"] += 1
        pi = src
        if st["alt"] % 2 == 0:
            te = p.op("scalar", lambda e: e.activation(stg[si][0:rows, :], pp[pi][0:rows, :], AF.Copy, scale=scale), deps=[st["last_mm"], stg_rd[si]])
        else:
            te = p.op("vector", lambda e: e.tensor_scalar(stg[si][0:rows, :], pp[pi][0:rows, :], scale, None, op0=ALU.mult), deps=[st["last_mm"], stg_rd[si]])
        st["alt"] += 1
        pp_rd[pi] = te
        stg_rd[si] = p.dma("gpsimd", dst_dram, stg[si][0:rows, :], dstg[si], deps=[te])

    def proj(tc):
        hb = tc % 2
        tok0 = tc * 512
        last = None
        for m in range(16):
            col0 = [0, 512, 1536, 2048][m // 4] + (m % 4) * 128
            pi = st["pp"] % 4
            st["pp"] += 1
            for c in range(8):
                last = p.op("tensor", lambda e, c=c, pi=pi, col0=col0: e.matmul(pp[pi][:], w16[:, c, col0:col0 + 128], hT[hb][:, c, :], start=(c == 0), stop=(c == 7)),
                            deps=[hT_ev[hb][c], wtok(col0), pp_rd[pi] if c == 0 else None], inc=(c == 7))
            st["last_mm"] = last
            scale = 0.125 if (m // 4) in (0, 2) else 1.0
            evac(pi, I["qk_s"][m * 128:(m + 1) * 128, tok0:tok0 + 512], scale)
        for t in range(4):
            for half in range(2):
                vc0 = 1024 if half == 0 else 2560
                pi = st["pp"] % 4
                st["pp"] += 1
                for c in range(8):
                    last = p.op("tensor", lambda e, c=c, pi=pi, vc0=vc0, t=t: e.matmul(pp[pi][:], hT[hb][:, c, t * 128:(t + 1) * 128], w16[:, c, vc0:vc0 + 512], start=(c == 0), stop=(c == 7)),
                                deps=[hT_ev[hb][c], wtok(vc0), pp_rd[pi] if c == 0 else None], inc=(c == 7))
                st["last_mm"] = last
                r0 = tc * 512 + t * 128
                evac(pi, I["v_s"][r0:r0 + 128, half * 512:(half + 1) * 512], 1.0)
        pi = st["pp"] % 4
        st["pp"] += 1
        for c in range(8):
            last = p.op("tensor", lambda e, c=c, pi=pi: e.matmul(pp[pi][0:8, :], w16[:, c, 3072:3080], hT[hb][:, c, :], start=(c == 0), stop=(c == 7)),
                        deps=[hT_ev[hb][c], wtok(3072), pp_rd[pi] if c == 0 else None], inc=(c == 7))
        tf = p.op("vector", lambda e, pi=pi, hb=hb: e.tensor_copy(flst[hb][:], pp[pi][0:8, :]), deps=[last, fl_rd[hb]])
        pp_rd[pi] = tf
        fl_rd[hb] = p.dma("gpsimd", I["fl_s"][:, tok0:tok0 + 512], flst[hb][:], dfl[hb], deps=[tf])
        hT_rd[hb] = last

    norm_tr(0)
    for tc in range(8):
        if tc + 1 < 8:
            norm_tr(tc + 1)
        proj(tc)

    p.wait("gpsimd", [(d[0], d[1]["v"]) for d in dstg] + [(d[0], d[1]["v"]) for d in dfl])
    p.wait("sync", [bt_store])
    p.build()


def phase2(nc, I, G):
    p = Prog(nc, "p2")
    SEG = 256
    F = p.sb("F", [128, SEG], F32)
    A = p.sb("A", [128, SEG], F32)
    B = p.sb("B", [128, SEG], F32)
    tri = p.sb("tri", [128, 128], F32)
    nb = p.sb("nb", [128, 1], F32)
    off = p.sb("off", [128, 1], F32)
    pm = p.ps("pm", [128, 512], F32)
    dl = p.dsem("ld")
    p.dma("sync", F[:], I["fl_s"].rearrange("h (s t) -> (h s) t", t=SEG), dl)
    p.dma("sync", tri[:], I["segtri"], dl)
    p.dma("sync", nb[:], I["fb128"], dl)
    LD = (dl[0], dl[1]["v"])
    t0 = p.op("vector", lambda e: e.tensor_scalar(nb[:], nb[:], -1.0, None, op0=ALU.mult), deps=[LD])
    t1 = p.op("scalar", lambda e: e.activation(A[:], F[:], AF.Exp, scale=-1.0, bias=nb[:, 0:1]), deps=[t0, LD])
    t2 = p.op("scalar", lambda e: e.activation(A[:], A[:], AF.Ln, bias=G["onec"][:, 0:1]), deps=[t1])
    cur, nxt = A, B
    tk = t2
    ta = None
    s_ = 1
    while s_ < SEG:
        ta = p.op("vector", lambda e, cur=cur, nxt=nxt, s_=s_: e.tensor_copy(nxt[:, 0:s_], cur[:, 0:s_]), deps=[tk])
        tk = p.op("vector", lambda e, cur=cur, nxt=nxt, s_=s_: e.tensor_tensor(nxt[:, s_:SEG], cur[:, s_:SEG], cur[:, 0:SEG - s_], op=ALU.add), deps=[tk])
        cur, nxt = nxt, cur
        s_ *= 2
    tm = p.op("tensor", lambda e: e.matmul(pm[:, 0:1], tri[:], cur[:, SEG - 1:SEG], start=True, stop=True), deps=[tk, ta, LD])
    to_ = p.op("vector", lambda e: e.tensor_copy(off[:], pm[:, 0:1]), deps=[tm])
    tP = p.op("vector", lambda e: e.tensor_scalar(nxt[:], cur[:], off[:, 0:1], None, op0=ALU.add), deps=[to_, tk, ta])
    P_ = nxt
    R = cur
    ch = p.sb("ch", [128, SEG], BF16)
    cm = p.sb("cm", [128, SEG], BF16)
    cl = p.sb("cl", [128, SEG], BF16)
    nh = p.sb("nh", [128, SEG], BF16)
    nm = p.sb("nm", [128, SEG], BF16)
    nl = p.sb("nl", [128, SEG], BF16)
    on = p.sb("on", [128, SEG], BF16)
    R2 = p.sb("R2", [128, SEG], F32)
    to = p.op("gpsimd", lambda e: e.memset(on[:], 1.0))
    a1 = p.op("vector", lambda e: e.tensor_scalar(ch[:], P_[:], -1.0, None, op0=ALU.mult), deps=[tP])
    a2 = p.op("vector", lambda e: e.scalar_tensor_tensor(R[:], P_[:], -1.0, ch[:], op0=ALU.mult, op1=ALU.subtract), deps=[a1])
    a3 = p.op("vector", lambda e: e.tensor_copy(cm[:], R[:]), deps=[a2])
    a4 = p.op("vector", lambda e: e.tensor_tensor(R2[:], R[:], cm[:], op=ALU.subtract), deps=[a3])
    a5 = p.op("vector", lambda e: e.tensor_copy(cl[:], R2[:]), deps=[a4])
    a6 = p.op("vector", lambda e: e.tensor_scalar(nh[:], ch[:], -1.0, None, op0=ALU.mult), deps=[a1])
    a7 = p.op("vector", lambda e: e.tensor_scalar(nm[:], cm[:], -1.0, None, op0=ALU.mult), deps=[a3])
    a8 = p.op("vector", lambda e: e.tensor_scalar(nl[:], cl[:], -1.0, None, op0=ALU.mult), deps=[a5])
    da = p.dsem("aug")
    fin = []
    for r, (qsrc, ksrc) in enumerate(((ch, on), (cm, on), (cl, on), (on, nh), (on, nm), (on, nl))):
        fin.append(p.dma("sync", I["augq"][r, :, :].rearrange("h (s t) -> (h s) t", t=SEG), qsrc[:], da, deps=[a8, to]))
        fin.append(p.dma("sync", I["augk"][r, :, :].rearrange("h (s t) -> (h s) t", t=SEG), ksrc[:], da, deps=[a8, to]))
    p.wait("sync", [fin[-1]])
    p.build()


def build_toeplitz(p, I, G, btile, cmask, Sp):
    ds = p.dsem("tld")
    dsc = p.dsem("scr")
    rbe = p.sb("rbe", [33, 4], F32)
    oh = p.sb("oh", [33, 768], F32)
    p.dma("sync", rbe[0:32, :], I["relb"], ds)
    p.dma("sync", oh[:], I["onehot"], ds)
    if cmask is not None:
        p.dma("gpsimd", cmask[:], I["cmask"], ds)
    LD = (ds[0], ds[1]["v"])
    MS = p.op("gpsimd", lambda e: e.memset(rbe[32:33, :], -1e30), deps=[LD])
    rl = p.sb("rl", [33, 4, 128], F32)
    vecs = [p.sb("vec%d" % h, [128, 768], F32) for h in range(4)]
    b32s = [p.sb("b32_%d" % h, [128, 640], F32) for h in range(4)]
    tprev = None
    tlast = []
    for h in range(4):
        vec = vecs[h]
        b32 = b32s[h]
        tr = p.op("vector", lambda e, h=h: e.tensor_scalar(rl[:, h, :], G["ones32"][0:33, :], rbe[:, h:h + 1], None, op0=ALU.mult), deps=[MS, LD])
        ta_ = p.op("tensor", lambda e, h=h: e.matmul(Sp[:, 0:384], rl[:, h, :], oh[:, 0:384], start=True, stop=True), deps=[tr, LD, tprev], inc=False)
        tb_ = p.op("tensor", lambda e, h=h: e.matmul(Sp[:, 512:896], rl[:, h, :], oh[:, 384:768], start=True, stop=True), deps=[])
        tv1 = p.op("vector", lambda e, vec=vec: e.tensor_copy(vec[:, 0:384], Sp[:, 0:384]), deps=[tb_])
        tv2 = p.op("vector", lambda e, vec=vec: e.tensor_copy(vec[:, 384:768], Sp[:, 512:896]), deps=[tb_])
        tprev = tv2
        dsh = p.dsem("scr%d" % h)
        tw = p.dma("gpsimd", bass.AP(I["_H"]["tscr"], h * 128 * 768, [[768, 128], [1, 768]]), vec[:], dsh, deps=[tv2, tv1])
        trd = p.dma("gpsimd", b32[:], bass.AP(I["_H"]["tscr"], h * 128 * 768 + 127, [[767, 128], [1, 640]]), dsh, deps=[tw])
        tlast.append((h, trd))
    for h, trd in tlast:
        tprev = p.op("vector", lambda e, h=h: e.tensor_copy(btile[:, h, :], b32s[h][:]), deps=[trd])
    return tprev, LD


def phase3(nc, I, G):
    p = Prog(nc, "p3")
    qd = [p.sb("qd%d" % i, [128, S], BF16) for i in range(2)]
    kd = [p.sb("kd%d" % i, [128, S], BF16) for i in range(2)]
    qf = [[p.sb("qf%d_%d" % (b, m), [70, S], BF16) for m in range(2)] for b in range(2)]
    kf = [[p.sb("kf%d_%d" % (b, m), [70, S], BF16) for m in range(2)] for b in range(2)]
    dlf2 = [p.dsem("ldf%d" % i) for i in range(2)]
    foxb_rd = [None, None]
    vt = [p.sb("vt%d" % i, [128, NT, 128], BF16) for i in range(2)]
    vtf = [p.sb("vtf%d" % i, [128, NT, 130], BF16) for i in range(2)]
    NPT = 4
    NFILL = 1
    PT = [p.sb("PT%d" % i, [128, 1024], BF16) for i in range(NPT)]
    co = [[p.sb("co%d_%d" % (a, m), [128, 512], F32) for m in range(2)] for a in range(2)]
    rsb = [[p.sb("rsb%d_%d" % (a, m), [128, 512], F32) for m in range(2)] for a in range(2)]
    tt_ = [[p.sb("tt%d_%d" % (a, m), [128, 512], F32) for m in range(2)] for a in range(2)]
    av = [p.sb("av%d" % a, [128, 512], F32) for a in range(2)]
    sq = [p.sb("sq%d" % a, [128, 512], F32) for a in range(2)]
    rstd = [p.sb("rstd%d" % a, [128, 512], F32) for a in range(2)]
    dd = [p.sb("dd%d" % a, [128, 512], F32) for a in range(2)]
    acc = [p.sb("acc%d" % i, [128, 1024], F32) for i in range(2)]
    acc_free = [None, None]
    acc_tok = [None]
    NOB = 2
    ob = [p.sb("ob%d" % i, [128, 512], BF16) for i in range(NOB)]
    dob = [p.dsem("ob%d" % i) for i in range(NOB)]
    Sps = [p.ps("S%d" % i, [128, 1024], F32) for i in range(2)]
    Ops = [[p.ps("O%d_%d" % (a, m), [128, 512], F32) for m in range(2)] for a in range(2)]
    dl = [p.dsem("ld%d" % i) for i in range(2)]
    dlf = p.dsem("ldf")
    dlv = [p.dsem("ldv%d" % i) for i in range(2)]
    btile = p.sb("btile", [128, 4, 640], BF16)
    cmask = p.sb("cmask", [128, 512], BF16)
    dtoe = p.dsem("toe")
    TOE = p.dma("sync", btile[:], I["bt_s"].rearrange("p (h x) -> p h x", h=4), dtoe)
    dcm = p.dsem("cm")
    CML = p.dma("gpsimd", cmask[:], I["cmask"], dcm)
    ones_tok = None
    for i in range(2):
        p.op("gpsimd", lambda e, i=i: e.memset(vtf[i][:, :, 64:65], 1.0))
        ones_tok = p.op("gpsimd", lambda e, i=i: e.memset(vtf[i][:, :, 129:130], 1.0))

    unit_rd = [None, None]
    fox_rd = [None]
    vt_rd = [None, None]
    vtf_rd = [None, None]
    ld_tok = {}

    def load_unit(u):
        if u < 4:
            b = u % 2
            p.dma("sync", qd[b][:], I["qk_s"][u * 128:(u + 1) * 128, :], dl[b], deps=[unit_rd[b]])
            p.dma("sync", kd[b][:], I["qk_s"][512 + u * 128:512 + (u + 1) * 128, :], dl[b], deps=[unit_rd[b]])
            for tg in range(4):
                vsrc = I["v_s"][tg * 1024:(tg + 1) * 1024, u * 128:(u + 1) * 128].rearrange("(t p) d -> p t d", p=128)
                p.dma("sync", vt[b][:, tg * 8:(tg + 1) * 8, :], vsrc, dl[b], deps=[vt_rd[b]])
            ld_tok[u] = [(dl[b][0], dl[b][1]["v"])]
        else:
            f = u - 4
            b = f % 2
            for m in range(2):
                hd = 2 * f + m
                p.dma("sync", qf[b][m][0:64, :], I["qk_s"][1024 + hd * 64:1024 + (hd + 1) * 64, :], dlf2[b], deps=[foxb_rd[b]])
                p.dma("sync", kf[b][m][0:64, :], I["qk_s"][1536 + hd * 64:1536 + (hd + 1) * 64, :], dlf2[b], deps=[foxb_rd[b]])
                p.dma("sync", qf[b][m][64:70, :], I["augq"][:, hd, :], dlf2[b], deps=[foxb_rd[b]])
                p.dma("sync", kf[b][m][64:70, :], I["augk"][:, hd, :], dlf2[b], deps=[foxb_rd[b]])
                c0 = 512 + f * 128 + m * 64
                for tg in range(4):
                    vsrc = I["v_s"][tg * 1024:(tg + 1) * 1024, c0:c0 + 64].rearrange("(t p) d -> p t d", p=128)
                    p.dma("sync", vtf[b][:, tg * 8:(tg + 1) * 8, 65 * m:65 * m + 64], vsrc, dlv[b], deps=[vtf_rd[b], ones_tok])
            ld_tok[u] = [(dlf2[b][0], dlf2[b][1]["v"]), (dlv[b][0], dlv[b][1]["v"]), ones_tok]

    S_rd = [None, None]
    PT_rd = [[] for _ in range(NPT)]
    o_free = [[], []]
    sp_free = [None, None]
    pend = []
    gbc = [0]
    gch = [0]
    ob_i = [0]

    def next_ob():
        i = ob_i[0] % NOB
        ob_i[0] += 1
        return i, (dob[i][0], dob[i][1]["v"])

    def run_pending(limit_chunk=None, force=False):
        if force:
            while pend and (limit_chunk is None or pend[0][1] <= limit_chunk):
                pend.pop(0)[2]()
            return
        for _ in range(2):
            if pend and pend[0][0] <= gbc[0]:
                pend.pop(0)[2]()

    def epilogue(u, qc, pv_tok, atok, ai):
        cid = gch[0]
        gch[0] += 1
        run_pending(limit_chunk=cid - 2, force=True)
        a = cid % 2
        q0 = qc * 512
        diff = u < 4
        SPa = Ops[a][0]
        g = gbc[0]
        if diff:
            e1 = p.op("scalar", lambda e: e.activation(co[a][0][:], Ops[a][0][:], AF.Copy), deps=[pv_tok])
            e2 = p.op("scalar", lambda e: e.activation(co[a][1][:], Ops[a][1][:], AF.Copy), deps=[pv_tok])
            o_free[a] = [e1, e2]
            sp_free[a] = e1
            stt = {}

            def s1():
                stt["m0"] = p.op("tensor", lambda e: e.matmul(SPa[:], G["ones32"][:], acc[ai][:, 0:512], start=True, stop=True), deps=[atok, sp_free[a]])

            def s2():
                tl_ = p.op("scalar", lambda e: e.activation(rsb[a][0][:], SPa[:], AF.Ln), deps=[stt["m0"]])
                sp_free[a] = tl_
                o_free[a].append(tl_)
                stt["x0"] = p.op("scalar", lambda e: e.activation(rsb[a][0][:], rsb[a][0][:], AF.Exp, scale=-1.0), deps=[tl_])

            def s3():
                t = p.op("tensor", lambda e: e.matmul(SPa[:], G["ones32"][:], acc[ai][:, 512:1024], start=True, stop=True), deps=[atok, sp_free[a]])
                stt["m1"] = t
                acc_free[ai] = t
                stt["t0"] = p.op("vector", lambda e: e.tensor_tensor(tt_[a][0][:], co[a][0][:], rsb[a][0][:], op=ALU.mult), deps=[stt["x0"], e1])

            def s4():
                tl_ = p.op("scalar", lambda e: e.activation(rsb[a][1][:], SPa[:], AF.Ln), deps=[stt["m1"]])
                sp_free[a] = tl_
                o_free[a].append(tl_)
                stt["x1"] = p.op("scalar", lambda e: e.activation(rsb[a][1][:], rsb[a][1][:], AF.Exp, scale=-1.0), deps=[tl_])

            def s5():
                t1 = p.op("vector", lambda e: e.tensor_tensor(tt_[a][1][:], co[a][1][:], rsb[a][1][:], op=ALU.mult), deps=[stt["x1"], e2])
                stt["a"] = p.op("vector", lambda e: e.scalar_tensor_tensor(av[a][:], tt_[a][1][:], G["neglam"][:, 0:1], tt_[a][0][:], op0=ALU.mult, op1=ALU.add), deps=[t1, stt["t0"]])

            def s6():
                stt["sq"] = p.op("scalar", lambda e: e.activation(sq[a][:], av[a][:], AF.Square), deps=[stt["a"]])

            def s7():
                stt["m2"] = p.op("tensor", lambda e: e.matmul(SPa[:], G["ones32"][:], sq[a][:], start=True, stop=True), deps=[stt["sq"], sp_free[a]])

            def s8():
                tl = p.op("scalar", lambda e: e.activation(rstd[a][:], SPa[:], AF.Ln, scale=1.0 / 128, bias=G["epsc"][:, 0:1]), deps=[stt["m2"]])
                sp_free[a] = tl
                o_free[a].append(tl)
                stt["x2"] = p.op("scalar", lambda e: e.activation(rstd[a][:], rstd[a][:], AF.Exp, scale=-0.5), deps=[tl])

            def s9():
                stt["d"] = p.op("vector", lambda e: e.tensor_tensor(dd[a][:], av[a][:], rstd[a][:], op=ALU.mult), deps=[stt["x2"], stt["a"]])

            def s10():
                i, obt = next_ob()
                tob = p.op("gpsimd", lambda e, i=i: e.tensor_scalar(ob[i][:], dd[a][:], G["gsub"][:, 0:1], None, op0=ALU.mult), deps=[stt["d"], obt])
                p.dma("gpsimd", I["o_s"][u * 128:(u + 1) * 128, q0:q0 + 512], ob[i][:], dob[i], deps=[tob])

            nxt_len = 4 * (qc - 1) + 4 if qc > 0 else 32
            sp = 2 if nxt_len >= 22 else 1
            for k, fn in enumerate((s1, s2, s3, s4, s5, s6, s7, s8, s9, s10)):
                pend.append((g + 2 + sp * k, cid, fn))
        else:
            es = []
            rr = []
            for m in range(2):
                es.append(p.op("vector", lambda e, m=m: e.tensor_copy(co[a][m][0:65, :], Ops[a][m][0:65, :]), deps=[pv_tok]))
            o_free[a] = [es[1]]
            sp_free[a] = es[0]
            for m in range(2):
                rr.append(p.op("vector", lambda e, m=m: e.reciprocal(rsb[a][m][64:65, :], co[a][m][64:65, :]), deps=[es[m]]))
            stt = {}

            def mk_pe(m):
                def st():
                    stt[m] = p.op("tensor", lambda e: e.matmul(SPa[0:64, :], G["ones32"][64:65, 0:64], rsb[a][m][64:65, :], start=True, stop=True), deps=[rr[m], sp_free[a]])
                return st

            def mk_dv(m):
                def st():
                    i, obt = next_ob()
                    t0 = p.op("vector", lambda e, i=i: e.tensor_tensor(ob[i][0:64, :], co[a][m][0:64, :], SPa[0:64, :], op=ALU.mult), deps=[stt[m], es[m], obt])
                    sp_free[a] = t0
                    o_free[a].append(t0)
                    hd = 2 * (u - 4) + m
                    p.dma("gpsimd", I["o_s"][512 + hd * 64:512 + (hd + 1) * 64, q0:q0 + 512], ob[i][0:64, :], dob[i], deps=[t0])
                return st
            for k, fn in enumerate((mk_pe(0), mk_dv(0), mk_pe(1), mk_dv(1))):
                pend.append((g + 5 + 2 * k, cid, fn))

    def run_unit(u):
        diff = u < 4
        b = (u % 2) if diff else ((u - 4) % 2)
        M = 128 if diff else 65
        lts = ld_tok[u]
        blocks = [(qc, j) for qc in reversed(range(8)) for j in range(4 * qc + 4)]

        def qap(m, c0, c1):
            return qd[b][64 * m:64 * m + 64, c0:c1] if diff else qf[b][m][0:70, c0:c1]

        def kap(m, j):
            return kd[b][64 * m:64 * m + 64, j * 128:(j + 1) * 128] if diff else kf[b][m][0:70, j * 128:(j + 1) * 128]

        def vap(m, j):
            return vt[b][:, j, :] if diff else vtf[b][:, j, 65 * m:65 * m + 65]

        def geom(qc, j):
            c0 = max(0, j - 4 * qc) * 128
            return c0, 512 - c0

        def QK(bi):
            qc, j = blocks[bi]
            sb_ = bi % 2
            c0, W = geom(qc, j)
            near = diff and (j >= 4 * qc - 1)
            diag = (not diff) and (j >= 4 * qc)
            last = None
            for m in range(2):
                extra = near or diag
                last = p.op("tensor", lambda e, m=m: e.matmul(Sps[sb_][:, m * 512 + c0:m * 512 + 512], kap(m, j), qap(m, qc * 512 + c0, qc * 512 + 512), start=True, stop=not extra),
                            deps=lts + [S_rd[sb_] if m == 0 else None], inc=(m == 1 and not extra))
                if near:
                    xs = qc * 512 + c0 - 128 * j
                    last = p.op("tensor", lambda e, m=m, xs=xs: e.matmul(Sps[sb_][:, m * 512 + c0:m * 512 + 512], G["ident16"][:], btile[:, u, xs:xs + W], start=False, stop=True), deps=[TOE], inc=(m == 1))
                elif diag:
                    last = p.op("tensor", lambda e, m=m: e.matmul(Sps[sb_][:, m * 512 + c0:m * 512 + 512], G["ident16"][:], cmask[:, 0:W], start=False, stop=True), deps=[CML], inc=(m == 1))
            return last

        def EXP(bi, qk_tok):
            qc, j = blocks[bi]
            sb_ = bi % 2
            pb_ = bi % NPT
            c0, W = geom(qc, j)
            far = diff and not (j >= 4 * qc - 1)
            bias = G["b31"][:, u:u + 1] if far else G["zeroc"][:, 0:1]
            if W == 512:
                src, dst = Sps[sb_][:, :], PT[pb_][:, :]
            else:
                src = Sps[sb_][:, :].rearrange("p (m w) -> p m w", m=2)[:, :, c0:512]
                dst = PT[pb_][:, :].rearrange("p (m w) -> p m w", m=2)[:, :, c0:512]
            t = p.op("scalar", lambda e: e.activation(dst, src, AF.Exp, bias=bias), deps=[qk_tok] + PT_rd[pb_])
            S_rd[sb_] = t
            return t

        def PV(bi, exp_tok):
            qc, j = blocks[bi]
            pb_ = bi % NPT
            c0, W = geom(qc, j)
            first = (j == 0)
            lastj = (j == 4 * qc + 3)
            last = None
            a_ = gch[0] % 2
            if first:
                run_pending(limit_chunk=gch[0] - 2, force=True)
            for m in range(2):
                last = p.op("tensor", lambda e, m=m: e.matmul(Ops[a_][m][0:M, c0:512], vap(m, j), PT[pb_][:, m * 512 + c0:m * 512 + 512], start=first, stop=lastj),
                            deps=[exp_tok] + lts + (o_free[a_] if first else []), inc=(m == 1))
            ai = gch[0] % 2
            ta = None
            if diff:
                if first:
                    ta = p.op("vector", lambda e: e.tensor_copy(acc[ai][:, :], PT[pb_][:, :]), deps=[exp_tok, acc_free[ai]])
                elif W == 512:
                    ta = p.op("vector", lambda e: e.tensor_tensor(acc[ai][:, :], acc[ai][:, :], PT[pb_][:, :], op=ALU.add), deps=[exp_tok, acc_tok[0]])
                else:
                    a3 = acc[ai][:, :].rearrange("p (m w) -> p m w", m=2)[:, :, c0:512]
                    p3 = PT[pb_][:, :].rearrange("p (m w) -> p m w", m=2)[:, :, c0:512]
                    ta = p.op("vector", lambda e: e.tensor_tensor(a3, a3, p3, op=ALU.add), deps=[exp_tok, acc_tok[0]])
                acc_tok[0] = ta
                PT_rd[pb_] = [last, ta]
            else:
                PT_rd[pb_] = [last]
            gbc[0] += 1
            if lastj:
                epilogue(u, qc, last, ta, ai)
            else:
                run_pending()
            return last

        nb = len(blocks)
        qk = QK(0)
        last = None
        for bi in range(nb):
            ex = EXP(bi, qk)
            if bi + 1 < nb:
                qk = QK(bi + 1)
            last = PV(bi, ex)
        if diff:
            unit_rd[b] = last
            vt_rd[b] = last
        else:
            fox_rd[0] = last
            foxb_rd[b] = last
            vtf_rd[b] = last

    load_unit(0)
    for u in range(8):
        if u + 1 < 8:
            load_unit(u + 1)
        run_unit(u)
    run_pending(force=True)
    p.wait("gpsimd", [(d[0], d[1]["v"]) for d in dob])
    p.build()


def phase4(nc, I, G):
    p = Prog(nc, "p4")
    CH = 256
    NCH = S // CH
    wup = p.sb("wup", [128, 8, 2 * DFF], BF16)
    wdn = p.sb("wdn", [128, 22, D], BF16)
    wo = p.sb("wo", [128, 8, D], BF16)
    dwo = p.dsem("wo")
    dwu = p.dsem("wu")
    dwd = p.dsem("wd")
    for c in range(8):
        p.dma("gpsimd", wo[:, c, :], I["w_out"][c * 128:(c + 1) * 128, :], dwo)
    NG = 11
    dwug = [p.dsem("wug%d" % g) for g in range(NG)]
    wsrc = I["w_up"].rearrange("(c p) n -> p c n", p=128)
    WUg = []
    for g in range(NG):
        for base in (0, DFF):
            c0 = base + g * 256
            p.dma("gpsimd", wup[:, :, c0:c0 + 256], wsrc[:, :, c0:c0 + 256], dwug[g])
        WUg.append((dwug[g][0], dwug[g][1]["v"]))
    for c in range(22):
        p.dma("gpsimd", wdn[:, c, :], I["w_down"][c * 128:(c + 1) * 128, :], dwd)
    WO = (dwo[0], dwo[1]["v"])
    WU = (dwu[0], dwu[1]["v"])
    WD = (dwd[0], dwd[1]["v"])
    oT = [p.sb("oT%d" % i, [128, 8, CH], BF16) for i in range(2)]
    doT = [p.dsem("oT%d" % i) for i in range(2)]
    xt = [p.sb("xt%d" % i, [128, D], F32) for i in range(2)]
    dx = [p.dsem("x%d" % i) for i in range(2)]
    dot_ = [p.dsem("ot%d" % i) for i in range(2)]
    xn = [p.sb("xn%d" % i, [128, D], BF16) for i in range(2)]
    ss = p.sb("ss", [128, 4 * NT], F32)
    h2T = p.sb("h2T", [128, 8, CH], BF16)
    yT = p.sb("yT", [128, 22, CH], BF16)
    ug = [p.sb("ug%d" % i, [128, CH + 2], F32) for i in range(1)]
    uv = [p.sb("uv%d" % i, [128, CH + 2], F32) for i in range(1)]
    tmp = p.sb("tmp", [128, 512], F32)
    gc = [p.sb("gc%d" % i, [128, CH], F32) for i in range(1)]
    vc = [p.sb("vc%d" % i, [128, CH], F32) for i in range(1)]
    sgt = [p.sb("sg%d" % i, [128, CH], F32) for i in range(1)]
    carry = p.sb("carry", [128, 44, 2], F32)
    pm = [p.ps("pm%d" % i, [128, 512], F32) for i in range(4)]
    pst = [p.ps("pst%d" % i, [128, 512], BF16) for i in range(2)]
    pgv = [p.ps("pgv%d" % i, [128, 512], F32) for i in range(2)]
    tz = p.op("gpsimd", lambda e: e.memset(carry[:], 0.0))
    CW = G["convw"]
    CB = G["convb"]

    pm_rd = [None] * 4
    pst_rd = [None, None]
    pgv_rd = [[None, None], [None, None]]
    oT_rd = [None, None]
    oT_ld = {}
    xt_rd = [None, None]
    xn_rd = [None, None]
    h2T_rd = [None]
    yT_rd = [None]
    ug_rd = [None, None]
    uv_rd = [None, None]
    gc_rd = [None]
    vc_rd = [None]
    sg_rd = [None]
    cnt = {"f": 0, "ss": 0}
    carry_w = [tz] * 44
    ugc_rd = [None, None]
    uvc_rd = [None, None]

    def rstd_of(src_ap, dep, dump, dump_dep):
        k = cnt["ss"]
        cnt["ss"] += 1
        a = p.op("scalar", lambda e: e.activation(dump, src_ap, AF.Square, accum_out=ss[:, k:k + 1]), deps=[dep, dump_dep])
        b_ = p.op("scalar", lambda e: e.activation(ss[:, k:k + 1], ss[:, k:k + 1], AF.Ln, scale=1.0 / D, bias=G["epsc"][:, 0:1]), deps=[a])
        c_ = p.op("scalar", lambda e: e.activation(ss[:, k:k + 1], ss[:, k:k + 1], AF.Exp, scale=-0.5), deps=[b_])
        return ss[:, k:k + 1], c_

    def load_oT(ci):
        ob_ = ci % 2
        tok0 = ci * CH
        oT_ld[ci] = p.dma("sync", oT[ob_][:], I["o_s"][:, tok0:tok0 + CH].rearrange("(c p) t -> p c t", p=128), doT[ob_], deps=[oT_rd[ob_]])

    def emit_mix(ci):
        ob_ = ci % 2
        res = {}
        mm = None
        for t in range(2):
            for half in range(2):
                pi = t * 2 + half
                for c in range(8):
                    mm = p.op("tensor", lambda e, c=c, pi=pi, t=t, half=half: e.matmul(pm[pi][:], oT[ob_][:, c, t * 128:(t + 1) * 128], wo[:, c, half * 512:(half + 1) * 512], start=(c == 0), stop=(c == 7)),
                              deps=[oT_ld[ci], WO, pm_rd[pi] if c == 0 else None], inc=(c == 7))
                res[pi] = mm
        oT_rd[ob_] = mm
        return res

    load_oT(0)
    mixres = emit_mix(0)
    for ci in range(NCH):
        tok0 = ci * CH
        if ci + 1 < NCH:
            load_oT(ci + 1)
        x1tok = []
        for t in range(2):
            sl = t
            r0 = tok0 + t * 128
            tx = p.dma("sync", xt[sl][:], I["x"][r0:r0 + 128, :], dx[sl], deps=[xt_rd[sl]])
            lastadd = None
            for half in range(2):
                pi = t * 2 + half
                t1 = p.op("vector", lambda e, pi=pi, half=half: e.tensor_tensor(tmp[:], pm[pi][:], G["g1_bc"][:, half * 512:(half + 1) * 512], op=ALU.mult), deps=[mixres[pi], lastadd])
                pm_rd[pi] = t1
                lastadd = p.op("vector", lambda e, sl=sl, half=half, pi=pi: e.tensor_tensor(xt[sl][:, half * 512:(half + 1) * 512], tmp[:], xt[sl][:, half * 512:(half + 1) * 512], op=ALU.add), deps=[t1, tx])
            rs_ap, rtok = rstd_of(xt[sl][:], lastadd, xn[t][:], xn_rd[t])
            tn = p.op("vector", lambda e, sl=sl, t=t, rs_ap=rs_ap: e.tensor_scalar(xn[t][:], xt[sl][:], rs_ap, None, op0=ALU.mult), deps=[rtok, lastadd, xn_rd[t]])
            x1tok.append(tn)
        ev = [None] * 8
        tt = None
        for c in range(8):
            for t in range(2):
                tt = p.op("tensor", lambda e, c=c, t=t: e.transpose(pst[c % 2][:, t * 128:(t + 1) * 128], xn[t][:, c * 128:(c + 1) * 128], G["ident16"][:]),
                          deps=[x1tok[t], pst_rd[c % 2] if t == 0 else None], inc=(t == 1))
            ev[c] = p.op("vector", lambda e, c=c: e.tensor_scalar(h2T[:, c, :], pst[c % 2][:, 0:CH], G["gsc2"][:, c:c + 1], G["modT"][:, 24 + c:25 + c], op0=ALU.mult, op1=ALU.add),
                         deps=[tt, h2T_rd[0] if c == 0 else None])
            pst_rd[c % 2] = ev[c]
        xn_rd[0] = tt
        xn_rd[1] = tt
        ylast = None
        for fc in range(22):
            fj = cnt["f"] % 2
            cnt["f"] += 1
            PG = pgv[fj][:, 0:CH]
            PV_ = pgv[fj][:, 256:256 + CH]
            mg = mv = None
            for c in range(8):
                mg = p.op("tensor", lambda e, c=c, fc=fc, PG=PG: e.matmul(PG, wup[:, c, fc * 128:(fc + 1) * 128], h2T[:, c, :], start=(c == 0), stop=(c == 7)),
                          deps=[ev[c], WUg[fc // 2]] + (pgv_rd[fj] if c == 0 else []), inc=(c == 7))
            for c in range(8):
                mv = p.op("tensor", lambda e, c=c, fc=fc, PV_=PV_: e.matmul(PV_, wup[:, c, DFF + fc * 128:DFF + (fc + 1) * 128], h2T[:, c, :], start=(c == 0), stop=(c == 7)),
                          deps=[ev[c]], inc=(c == 7))
            h1 = p.op("vector", lambda e, fc=fc, fj=fj: e.tensor_copy(ug[0][:, 0:2], carry[:, fc, :]), deps=[carry_w[fc], ug_rd[0]])
            h2 = p.op("vector", lambda e, fc=fc, fj=fj: e.tensor_copy(uv[0][:, 0:2], carry[:, 22 + fc, :]), deps=[carry_w[22 + fc], uv_rd[0]])
            eg = p.op("scalar", lambda e, fj=fj, PG=PG: e.activation(ug[0][:, 2:CH + 2], PG, AF.Copy), deps=[mv, ug_rd[0], ugc_rd[0]])
            g1_ = p.op("scalar", lambda e, fc=fc, PG=PG: e.activation(gc[0][:], PG, AF.Identity, scale=CW[:, 2, fc:fc + 1], bias=CB[:, fc:fc + 1]), deps=[mv, gc_rd[0]])
            cg = p.op("gpsimd", lambda e, fc=fc, fj=fj: e.tensor_copy(carry[:, fc, :], ug[0][:, CH:CH + 2]), deps=[eg, h1])
            carry_w[fc] = cg
            ugc_rd[0] = cg
            evv = p.op("scalar", lambda e, fj=fj, PV_=PV_: e.activation(uv[0][:, 2:CH + 2], PV_, AF.Copy), deps=[mv, uv_rd[0], uvc_rd[0]])
            v1_ = p.op("scalar", lambda e, fc=fc, PV_=PV_: e.activation(vc[0][:], PV_, AF.Identity, scale=CW[:, 2, 22 + fc:23 + fc], bias=CB[:, 22 + fc:23 + fc]), deps=[mv, vc_rd[0]])
            cv = p.op("gpsimd", lambda e, fc=fc, fj=fj: e.tensor_copy(carry[:, 22 + fc, :], uv[0][:, CH:CH + 2]), deps=[evv, h2])
            carry_w[22 + fc] = cv
            uvc_rd[0] = cv
            pgv_rd[fj] = [g1_, v1_]
            g2_ = p.op("vector", lambda e, fc=fc, fj=fj: e.scalar_tensor_tensor(gc[0][:], ug[0][:, 1:CH + 1], CW[:, 1, fc:fc + 1], gc[0][:], op0=ALU.mult, op1=ALU.add), deps=[g1_, eg, h1])
            g3_ = p.op("vector", lambda e, fc=fc, fj=fj: e.scalar_tensor_tensor(gc[0][:], ug[0][:, 0:CH], CW[:, 0, fc:fc + 1], gc[0][:], op0=ALU.mult, op1=ALU.add), deps=[g2_])
            ug_rd[0] = g3_
            v2_ = p.op("vector", lambda e, fc=fc, fj=fj: e.scalar_tensor_tensor(vc[0][:], uv[0][:, 1:CH + 1], CW[:, 1, 22 + fc:23 + fc], vc[0][:], op0=ALU.mult, op1=ALU.add), deps=[v1_, evv, h2])
            v3_ = p.op("vector", lambda e, fc=fc, fj=fj: e.scalar_tensor_tensor(vc[0][:], uv[0][:, 0:CH], CW[:, 0, 22 + fc:23 + fc], vc[0][:], op0=ALU.mult, op1=ALU.add), deps=[v2_])
            uv_rd[0] = v3_
            s1 = p.op("scalar", lambda e: e.activation(sgt[0][:], gc[0][:], AF.Silu), deps=[g3_, sg_rd[0]])
            gc_rd[0] = s1
            y1 = p.op("gpsimd", lambda e, fc=fc: e.tensor_tensor(yT[:, fc, :], sgt[0][:], vc[0][:], op=ALU.mult), deps=[s1, v3_, yT_rd[0] if fc == 0 else None])
            sg_rd[0] = y1
            vc_rd[0] = y1
            ylast = y1
        h2T_rd[0] = mv
        wd_tok = {}
        mm = None
        for t in range(2):
            for half in range(2):
                pi = t * 2 + half
                for fc in range(22):
                    mm = p.op("tensor", lambda e, fc=fc, pi=pi, t=t, half=half: e.matmul(pm[pi][:], yT[:, fc, t * 128:(t + 1) * 128], wdn[:, fc, half * 512:(half + 1) * 512], start=(fc == 0), stop=(fc == 21)),
                              deps=[ylast, WD, pm_rd[pi] if fc == 0 else None], inc=(fc == 21))
                wd_tok[pi] = mm
        yT_rd[0] = mm
        fin = []
        for t in range(2):
            sl = t
            r0 = tok0 + t * 128
            lastadd = None
            for half in range(2):
                pi = t * 2 + half
                t1 = p.op("vector", lambda e, pi=pi, half=half: e.tensor_tensor(tmp[:], pm[pi][:], G["g2_bc"][:, half * 512:(half + 1) * 512], op=ALU.mult), deps=[wd_tok[pi], lastadd])
                pm_rd[pi] = t1
                lastadd = p.op("vector", lambda e, sl=sl, half=half, pi=pi: e.tensor_tensor(xt[sl][:, half * 512:(half + 1) * 512], tmp[:], xt[sl][:, half * 512:(half + 1) * 512], op=ALU.add), deps=[t1])
            rs_ap, rtok = rstd_of(xt[sl][:], lastadd, xn[t][:], xn_rd[t])
            tf = p.op("vector", lambda e, sl=sl, rs_ap=rs_ap: e.scalar_tensor_tensor(xt[sl][:], xt[sl][:], rs_ap, G["fg_bc"][:], op0=ALU.mult, op1=ALU.mult), deps=[rtok, lastadd])
            xt_rd[sl] = p.dma("gpsimd", I["out"][r0:r0 + 128, :], xt[sl][:], dot_[sl], deps=[tf])
        if ci + 1 < NCH:
            mixres = emit_mix(ci + 1)
    p.wait("sync", [(d[0], d[1]["v"]) for d in dot_])
    p.build()


def build_nc():
    nc = bass.Bass("TRN2", target_bir_lowering=False)
    I = {}

    H = {}
    I["_H"] = H

    def inp(name, shape, dt=F32):
        H[name] = nc.dram_tensor(name, list(shape), dt, kind="ExternalInput")
        I[name] = H[name].ap()

    inp("x", [S, D]); inp("cT", [128, 8]); inp("ada_w", [D, 6 * D]); inp("adabT", [128, 48])
    inp("gatT", [128, 8]); inp("gffT", [128, 8]); inp("fng", [1, D]); inp("w_in", [D, NIN])
    inp("fb", [8, 1]); inp("lam4", [1, 256]); inp("subg", [128, 1]); inp("relb", [32, 4])
    inp("w_out", [D, D]); inp("w_up", [D, 2 * DFF]); inp("convwT", [128, 3, 44]); inp("convbT", [128, 44])
    inp("w_down", [DFF, D]); inp("segtri", [128, 128]); inp("fb128", [128, 1]); inp("onehot", [33, 768]); inp("ident", [128, 128]); inp("cmask", [128, 512])
    I["out"] = nc.dram_tensor("out", [S, D], F32, kind="ExternalOutput").ap()
    sk = "ExternalOutput" if DEBUG else "Internal"
    I["qk_s"] = nc.dram_tensor("qk_s", [2048, S], BF16, kind=sk).ap()
    I["v_s"] = nc.dram_tensor("v_s", [S, 1024], BF16, kind=sk).ap()
    I["o_s"] = nc.dram_tensor("o_s", [1024, S], BF16, kind=sk).ap()
    I["fl_s"] = nc.dram_tensor("fl_s", [8, S], F32, kind=sk).ap()
    I["bt_s"] = nc.dram_tensor("bt_s", [128, 4 * 640], BF16, kind="Internal").ap()
    I["augq"] = nc.dram_tensor("augq", [6, 8, S], BF16, kind=sk).ap()
    I["augk"] = nc.dram_tensor("augk", [6, 8, S], BF16, kind=sk).ap()
    H["tscr"] = nc.dram_tensor("tscr", [4 * 128 * 768], F32, kind="Internal")

    ctx = []

    def gsb(name, shape, dt):
        cm = nc.sbuf_tensor(name, list(shape), dt)
        t = cm.__enter__()
        ctx.append(cm)
        return t

    G = {
        "ident16": gsb("g_ident16", [128, 128], BF16), "ones16": gsb("g_ones16", [128, 128], BF16),
        "ones32": gsb("g_ones32", [128, 128], F32),
        "modT": gsb("g_modT", [128, 48], F32), "gsc1": gsb("g_gsc1", [128, 8], F32), "gsc2": gsb("g_gsc2", [128, 8], F32),
        "g1_bc": gsb("g_g1bc", [128, D], F32), "g2_bc": gsb("g_g2bc", [128, D], F32), "fg_bc": gsb("g_fgbc", [128, D], F32),
        "neglam": gsb("g_neglam", [128, 1], F32), "gsub": gsb("g_gsub", [128, 1], F32), "nfb": gsb("g_nfb", [8, 1], F32),
        "b31": gsb("g_b31", [128, 4], F32),
        "convw": gsb("g_convw", [128, 3, 44], F32), "convb": gsb("g_convb", [128, 44], F32),
        "epsc": gsb("g_epsc", [128, 1], F32), "onec": gsb("g_onec", [128, 1], F32), "zeroc": gsb("g_zeroc", [128, 1], F32),
    }
    phase0(nc, I, G)
    phase1(nc, I, G)
    phase2(nc, I, G)
    phase3(nc, I, G)
    phase4(nc, I, G)
    for cm in reversed(ctx):
        cm.__exit__(None, None, None)
    return nc


def _bucket_table():
    n = np.arange(640, dtype=np.int64)
    nf = np.maximum(n, 1).astype(np.float32)
    large = 16 + (np.log(nf / np.float32(16)) / np.float32(math.log(128 / 16)) * np.float32(16)).astype(np.int32)
    large = np.minimum(large, 31)
    return np.where(n < 16, n, large)


def make_in_maps(inputs):
    f = lambda a: np.ascontiguousarray(np.asarray(a, dtype=np.float32))
    x = f(inputs["x"]); c = f(inputs["c"])
    bk = _bucket_table()
    onehot = np.zeros((33, 768), np.float32)
    onehot[32, :127] = 1.0
    for m in range(127, 767):
        onehot[bk[m - 127], m] = 1.0
    ident = np.eye(128, dtype=np.float32)
    kk = np.arange(128)[:, None]; xx = np.arange(512)[None, :]
    cmask = np.where(xx >= kk, 0.0, -1e30).astype(np.float32)
    cw = f(inputs["conv_w"])[0]
    pi_ = np.arange(128)
    segtri = ((pi_[:, None] // 16 == pi_[None, :] // 16) & (pi_[:, None] % 16 < pi_[None, :] % 16)).astype(np.float32)
    shared = {
        "ada_w": f(inputs["ada_w"])[0],
        "adabT": np.ascontiguousarray(f(inputs["ada_b"])[0].reshape(48, 128).T),
        "gatT": np.ascontiguousarray(f(inputs["attn_norm_g"])[0].reshape(8, 128).T),
        "gffT": np.ascontiguousarray(f(inputs["ffn_norm_g"])[0].reshape(8, 128).T),
        "fng": f(inputs["final_norm_g"]).reshape(1, D),
        "w_in": f(inputs["w_in"])[0],
        "fb": f(inputs["forget_b"])[0].reshape(8, 1),
        "lam4": np.concatenate([f(inputs["lambda_q1"])[0], f(inputs["lambda_k1"])[0], f(inputs["lambda_q2"])[0], f(inputs["lambda_k2"])[0]]).reshape(1, 256),
        "subg": f(inputs["subln_g"])[0].reshape(128, 1),
        "relb": f(inputs["rel_bias"]),
        "w_out": f(inputs["w_out"])[0],
        "w_up": f(inputs["w_up"])[0],
        "convwT": np.ascontiguousarray(cw.reshape(3, 44, 128).transpose(2, 0, 1)),
        "convbT": np.ascontiguousarray(f(inputs["conv_b"])[0].reshape(44, 128).T),
        "w_down": f(inputs["w_down"])[0],
        "onehot": onehot, "ident": ident, "cmask": cmask,
        "segtri": segtri, "fb128": np.ascontiguousarray(np.repeat(f(inputs["forget_b"])[0], 16).reshape(128, 1)),
    }
    maps = []
    for b in range(8):
        m = dict(shared)
        m["x"] = np.ascontiguousarray(x[b])
        m["cT"] = np.ascontiguousarray(c[b].reshape(8, 128).T)
        maps.append(m)
    return maps


def kernel(**inputs):
    nc = build_nc()
    in_maps = make_in_maps(inputs)
    res = run_bass_kernel_spmd(nc, in_maps, core_ids=list(range(8)))
    out = np.stack([np.asarray(r["out"], dtype=np.float32) for r in res.results], axis=0)
    return out
```

```python
import math
import os
import numpy as np
import concourse.bass as bass
import concourse.mybir as mybir
from concourse.bass_utils import run_bass_kernel_spmd

F32 = mybir.dt.float32
BF16 = mybir.dt.bfloat16
AF = mybir.ActivationFunctionType
ALU = mybir.AluOpType

ENGS = ("sync", "scalar", "vector", "gpsimd", "tensor")
S = 4096
D = 1024
NT = S // 128
NIN = 3080
DFF = 2816
EPS = 1e-6
LAMBDA_INIT = 0.8 - 0.6 * math.exp(-0.3 * 0)
DEBUG = bool(int(os.environ.get("MK_DEBUG", "0")))


class Prog:
    def __init__(self, nc, tag):
        self.nc = nc
        self.tag = tag
        self.ops = {e: [] for e in ENGS}
        self.cnt = {}
        self.n = {e: 0 for e in ENGS}
        self.waited = {e: {} for e in ENGS}
        self._ctx = []
        for e in ENGS:
            self.cnt[e] = self.sem("cnt_" + e)

    def sem(self, name):
        cm = self.nc.semaphore(self.tag + "_" + name)
        s = cm.__enter__()
        self._ctx.append(cm)
        return s

    def dsem(self, name):
        return [self.sem(name), {"v": 0}]

    def sb(self, name, shape, dt):
        cm = self.nc.sbuf_tensor(self.tag + "_" + name, list(shape), dt)
        t = cm.__enter__()
        self._ctx.append(cm)
        return t

    def ps(self, name, shape, dt=F32):
        cm = self.nc.psum_tensor(self.tag + "_" + name, list(shape), dt)
        t = cm.__enter__()
        self._ctx.append(cm)
        return t

    def _waits(self, eng, deps):
        w = []
        for d in deps:
            if d is None:
                continue
            sem, val = d
            key = id(sem)
            if self.waited[eng].get(key, 0) >= val:
                continue
            self.waited[eng][key] = val
            w.append((sem, val))
        return w

    def op(self, eng, fn, deps=(), inc=True):
        w = self._waits(eng, deps)
        tok = None
        if inc:
            self.n[eng] += 1
            tok = (self.cnt[eng], self.n[eng])
        cnt = self.cnt[eng]

        def run(e, w=w, fn=fn, inc=inc, cnt=cnt):
            for sem, val in w:
                e.wait_ge(sem, val)
            ins = fn(e)
            if inc:
                ins.then_inc(cnt, 1)
        self.ops[eng].append(run)
        return tok

    def dma(self, eng, out, in_, ds, deps=()):
        w = self._waits(eng, deps)
        sem, sv = ds
        sv["v"] += 16
        tok = (sem, sv["v"])

        def run(e, w=w):
            for s, val in w:
                e.wait_ge(s, val)
            e.dma_start(out=out, in_=in_).then_inc(sem, 16)
        self.ops[eng].append(run)
        return tok

    def wait(self, eng, deps):
        w = self._waits(eng, deps)

        def run(e, w=w):
            for s, val in w:
                e.wait_ge(s, val)
        self.ops[eng].append(run)

    def build(self):
        nc = self.nc
        ops = self.ops
        with nc.Block() as block:
            @block.sync
            def _(e):
                for f in ops["sync"]:
                    f(e)

            @block.scalar
            def _(e):
                for f in ops["scalar"]:
                    f(e)

            @block.vector
            def _(e):
                for f in ops["vector"]:
                    f(e)

            @block.gpsimd
            def _(e):
                for f in ops["gpsimd"]:
                    f(e)

            @block.tensor
            def _(e):
                for f in ops["tensor"]:
                    f(e)
        for cm in reversed(self._ctx):
            cm.__exit__(None, None, None)
        self._ctx = []


def phase0(nc, I, G):
    p = Prog(nc, "p0")
    ds = p.dsem("ld")
    dw = [p.dsem("aw%d" % i) for i in range(4)]
    dsc = p.dsem("scr")
    cT = p.sb("cT", [128, 8], F32)
    adab = p.sb("adab", [128, 48], F32)
    gat = p.sb("gat", [128, 8], F32)
    gff = p.sb("gff", [128, 8], F32)
    lq = p.sb("lq", [1, 256], F32)
    fb = p.sb("fb", [8, 1], F32)
    sg = p.sb("sg", [128, 1], F32)
    id32 = p.sb("id32", [128, 128], F32)
    loads = [
        p.dma("sync", cT[:], I["cT"], ds), p.dma("sync", adab[:], I["adabT"], ds),
        p.dma("sync", gat[:], I["gatT"], ds), p.dma("sync", gff[:], I["gffT"], ds),
        p.dma("sync", lq[:], I["lam4"], ds), p.dma("sync", fb[:], I["fb"], ds),
        p.dma("sync", sg[:], I["subg"], ds), p.dma("sync", id32[:], I["ident"], ds),
        p.dma("sync", G["fg_bc"][:], bass.AP(I["_H"]["fng"], 0, [[0, 128], [1, 1024]]), ds),
        p.dma("sync", G["b31"][:], bass.AP(I["_H"]["relb"], 31 * 4, [[0, 128], [1, 4]]), ds),
        p.dma("sync", G["convw"][:], I["convwT"], ds), p.dma("sync", G["convb"][:], I["convbT"], ds),
        p.dma("gpsimd", G["ident16"][:], I["ident"], ds),
    ]
    LD = loads[-1]
    LD = (ds[0], ds[1]["v"])
    m_ = [p.op("gpsimd", lambda e: e.memset(G["epsc"][:], EPS)),
          p.op("gpsimd", lambda e: e.memset(G["onec"][:], 1.0)),
          p.op("gpsimd", lambda e: e.memset(G["zeroc"][:], 0.0)),
          p.op("gpsimd", lambda e: e.memset(G["ones32"][:], 1.0)),
          p.op("gpsimd", lambda e: e.memset(G["ones16"][:], 1.0)),
          p.op("gpsimd", lambda e: e.memset(G["ones16"][:], 1.0), deps=[LD])]
    MS = m_[-1]
    cact = p.sb("cact", [128, 8], F32)
    t_c = p.op("scalar", lambda e: e.activation(cact[:], cT[:], AF.Silu), deps=[LD])
    aw = [p.sb("aw%d" % i, [128, 6144], F32) for i in range(4)]
    acc = p.sb("acc", [128, 6144], F32)
    ta = None
    rd = [None] * 4
    tls = []
    for c in range(8):
        if c >= 4:
            pass
        tls.append(None)
    for c in range(8):
        if c < 4:
            tls[c] = p.dma("sync", aw[c % 4][:], I["ada_w"][c * 128:(c + 1) * 128, :], dw[c % 4])
    for c in range(8):
        tl = tls[c]
        if c == 0:
            ta = p.op("vector", lambda e, c=c: e.tensor_scalar(acc[:], aw[0][:], cact[:, 0:1], None, op0=ALU.mult), deps=[tl, t_c])
        else:
            ta = p.op("vector", lambda e, c=c: e.scalar_tensor_tensor(acc[:], aw[c % 4][:], cact[:, c:c + 1], acc[:], op0=ALU.mult, op1=ALU.add), deps=[tl, ta])
        rd[c % 4] = ta
        if c + 4 < 8:
            tls[c + 4] = p.dma("sync", aw[c % 4][:], I["ada_w"][(c + 4) * 128:(c + 5) * 128, :], dw[c % 4], deps=[ta])
    pm = p.ps("pm", [128, 512], F32)
    tm = None
    for j in range(48):
        tm = p.op("tensor", lambda e, j=j: e.matmul(pm[:, j:j + 1], acc[:, j * 128:(j + 1) * 128], G["ones32"][:, 0:1], start=True, stop=True), deps=[ta, MS], inc=(j == 47))
    modT = G["modT"]
    t_mod = p.op("vector", lambda e: e.tensor_tensor(modT[:], pm[:, 0:48], adab[:], op=ALU.add), deps=[tm, LD])
    t1 = p.op("vector", lambda e: e.scalar_tensor_tensor(G["gsc1"][:], modT[:, 8:16], 1.0, gat[:], op0=ALU.add, op1=ALU.mult), deps=[t_mod])
    t2 = p.op("vector", lambda e: e.scalar_tensor_tensor(G["gsc2"][:], modT[:, 32:40], 1.0, gff[:], op0=ALU.add, op1=ALU.mult), deps=[t_mod])
    dg = p.sb("dg", [128, 16, 128], F32)
    pb = [p.ps("pb0", [128, 512], F32), p.ps("pb1", [128, 512], F32)]
    tcp = [None, None]
    for which, base, dst in ((0, 16, G["g1_bc"]), (1, 40, G["g2_bc"])):
        for c in range(8):
            k = which * 8 + c
            td = p.op("vector", lambda e, k=k, c=c, base=base: e.tensor_scalar(dg[:, k, :], id32[:], modT[:, base + c:base + c + 1], None, op0=ALU.mult), deps=[t_mod, LD])
            bi = k // 4
            tmm = p.op("tensor", lambda e, k=k, bi=bi: e.matmul(pb[bi % 2][:, (k % 4) * 128:(k % 4 + 1) * 128], G["ones32"][:], dg[:, k, :], start=True, stop=True), deps=[td, MS, tcp[bi % 2] if k % 4 == 0 else None])
            if k % 4 == 3:
                half = (k % 8) // 4
                tcp[bi % 2] = p.op("vector", lambda e, bi=bi, dst=dst, half=half: e.tensor_copy(dst[:, half * 512:(half + 1) * 512], pb[bi % 2][:]), deps=[tmm])
    lp = p.sb("lp", [1, 128], F32)
    ls = p.sb("ls", [1, 4], F32)
    tl1 = p.op("vector", lambda e: e.tensor_tensor(lp[:, 0:64], lq[:, 0:64], lq[:, 64:128], op=ALU.mult), deps=[LD])
    tl2 = p.op("vector", lambda e: e.tensor_tensor(lp[:, 64:128], lq[:, 128:192], lq[:, 192:256], op=ALU.mult), deps=[LD])
    tl3 = p.op("vector", lambda e: e.reduce_sum(ls[:, 0:1], lp[:, 0:64], axis=mybir.AxisListType.X), deps=[tl1])
    tl4 = p.op("vector", lambda e: e.reduce_sum(ls[:, 1:2], lp[:, 64:128], axis=mybir.AxisListType.X), deps=[tl2])
    tl5 = p.op("scalar", lambda e: e.activation(ls[:, 2:4], ls[:, 0:2], AF.Exp), deps=[tl3, tl4])
    tl6 = p.op("vector", lambda e: e.scalar_tensor_tensor(ls[:, 0:1], ls[:, 3:4], -LAMBDA_INIT, ls[:, 2:3], op0=ALU.add, op1=ALU.subtract), deps=[tl5])
    tl7 = p.op("tensor", lambda e: e.matmul(pm[:, 64:65], G["ones32"][0:1, :], ls[0:1, 0:1], start=True, stop=True), deps=[tl6, MS, t_mod])
    tl8 = p.op("vector", lambda e: e.tensor_copy(G["neglam"][:], pm[:, 64:65]), deps=[tl7])
    p.op("vector", lambda e: e.tensor_scalar(G["gsub"][:], sg[:], 1.0 - LAMBDA_INIT, None, op0=ALU.mult), deps=[LD])
    p.op("vector", lambda e: e.tensor_scalar(G["nfb"][:], fb[:], -1.0, None, op0=ALU.mult), deps=[LD])
    p.build()


def phase1(nc, I, G):
    p = Prog(nc, "p1")
    w16 = p.sb("w16", [128, 8, NIN], BF16)
    wsrc = I["w_in"].rearrange("(c p) n -> p c n", p=128)
    wgroups = [(0, 512), (512, 1024), (1536, 2048), (2048, 2560), (1024, 1536), (2560, NIN)]
    wgtok = []
    for gi, (ca, cb) in enumerate(wgroups):
        dwg = p.dsem("wg%d" % gi)
        wgtok.append(p.dma("gpsimd", w16[:, :, ca:cb], wsrc[:, :, ca:cb], dwg))

    def wtok(col):
        for gi, (ca, cb) in enumerate(wgroups):
            if ca <= col < cb:
                return wgtok[gi]
        raise ValueError(col)
    xt = [p.sb("xt%d" % i, [128, D], F32) for i in range(8)]
    dx = [p.dsem("x%d" % i) for i in range(8)]
    xn = [p.sb("xn%d" % i, [128, D], BF16) for i in range(8)]
    junk = p.sb("junk", [128, D], F32)
    ss = p.sb("ss", [128, NT], F32)
    rs = p.sb("rs", [128, NT], F32)
    hT = [p.sb("hT%d" % i, [128, 8, 512], BF16) for i in range(2)]
    pst = [p.ps("pst%d" % i, [128, 512], BF16) for i in range(2)]
    pp = [p.ps("pp%d" % i, [128, 512], F32) for i in range(4)]
    stg = [p.sb("stg%d" % i, [128, 512], BF16) for i in range(4)]
    dstg = [p.dsem("stg%d" % i) for i in range(4)]
    flst = [p.sb("flst%d" % i, [8, 512], F32) for i in range(2)]
    dfl = [p.dsem("fl%d" % i) for i in range(2)]
    fl_rd = [None, None]
    xn_rd = [None] * 8
    xt_rd = [None] * 8
    hT_rd = [None, None]
    pst_rd = [None, None]
    pp_rd = [None] * 4
    stg_rd = [None] * 4
    hT_ev = [[None] * 8, [None] * 8]
    st = {"pp": 0, "stg": 0, "alt": 0}
    fl_tok = []
    ptoe = p.ps("ptoe", [128, 1024], F32)
    btile1 = p.sb("btile1", [128, 4, 640], BF16)
    toe_tok, _ = build_toeplitz(p, I, G, btile1, None, ptoe)
    dbt = p.dsem("bts")
    bt_store = p.dma("gpsimd", I["bt_s"].rearrange("p (h x) -> p h x", h=4), btile1[:], dbt, deps=[toe_tok])

    def norm_tr(tc):
        xnt = []
        for t in range(4):
            sl = (tc % 2) * 4 + t
            idx = tc * 4 + t
            tx = p.dma("sync", xt[sl][:], I["x"][idx * 128:(idx + 1) * 128, :], dx[sl], deps=[xt_rd[sl]])
            tsq = p.op("scalar", lambda e, sl=sl, idx=idx: e.activation(junk[:], xt[sl][:], AF.Square, accum_out=ss[:, idx:idx + 1]), deps=[tx])
            tln = p.op("scalar", lambda e, idx=idx: e.activation(rs[:, idx:idx + 1], ss[:, idx:idx + 1], AF.Ln, scale=1.0 / D, bias=G["epsc"][:, 0:1]), deps=[tsq])
            tex = p.op("scalar", lambda e, idx=idx: e.activation(rs[:, idx:idx + 1], rs[:, idx:idx + 1], AF.Exp, scale=-0.5), deps=[tln])
            tn = p.op("vector", lambda e, sl=sl, idx=idx: e.tensor_scalar(xn[sl][:], xt[sl][:], rs[:, idx:idx + 1], None, op0=ALU.mult), deps=[tex, tx, xn_rd[sl]])
            xt_rd[sl] = tn
            xnt.append(tn)
        hb = tc % 2
        for c in range(8):
            tt = None
            for t in range(4):
                sl = (tc % 2) * 4 + t
                tt = p.op("tensor", lambda e, c=c, t=t, sl=sl: e.transpose(pst[c % 2][:, t * 128:(t + 1) * 128], xn[sl][:, c * 128:(c + 1) * 128], G["ident16"][:]),
                          deps=[xnt[t], pst_rd[c % 2] if t == 0 else None], inc=(t == 3))
            tev = p.op("vector", lambda e, c=c, hb=hb: e.tensor_scalar(hT[hb][:, c, :], pst[c % 2][:], G["gsc1"][:, c:c + 1], G["modT"][:, c:c + 1], op0=ALU.mult, op1=ALU.add),
                       deps=[tt, hT_rd[hb] if c == 0 else None])
            pst_rd[c % 2] = tev
            hT_ev[hb][c] = tev
        for t in range(4):
            xn_rd[(tc % 2) * 4 + t] = tt

    def evac(src, dst_dram, scale, rows=128):
        si = st["stg"] % 4
        st["stg"] += 1
        pi = src
        if st["alt"] % 2 == 0:
            te = p.op("scalar", lambda e: e.activation(stg[si][0:rows, :], pp[pi][0:rows, :], AF.Copy, scale=scale), deps=[st["last_mm"], stg_rd[si]])
        else:
            te = p.op("vector", lambda e: e.tensor_scalar(stg[si][0:rows, :], pp[pi][0:rows, :], scale, None, op0=ALU.mult), deps=[st["last_mm"], stg_rd[si]])
        st["alt"] += 1
        pp_rd[pi] = te
        stg_rd[si] = p.dma("gpsimd", dst_dram, stg[si][0:rows, :], dstg[si], deps=[te])

    def proj(tc):
        hb = tc % 2
        tok0 = tc * 512
        last = None
        for m in range(16):
            col0 = [0, 512, 1536, 2048][m // 4] + (m % 4) * 128
            pi = st["pp"] % 4
            st["pp"] += 1
            for c in range(8):
                last = p.op("tensor", lambda e, c=c, pi=pi, col0=col0: e.matmul(pp[pi][:], w16[:, c, col0:col0 + 128], hT[hb][:, c, :], start=(c == 0), stop=(c == 7)),
                            deps=[hT_ev[hb][c], wtok(col0), pp_rd[pi] if c == 0 else None], inc=(c == 7))
            st["last_mm"] = last
            scale = 0.125 if (m // 4) in (0, 2) else 1.0
            evac(pi, I["qk_s"][m * 128:(m + 1) * 128, tok0:tok0 + 512], scale)
        for t in range(4):
            for half in range(2):
                vc0 = 1024 if half == 0 else 2560
                pi = st["pp"] % 4
                st["pp"] += 1
                for c in range(8):
                    last = p.op("tensor", lambda e, c=c, pi=pi, vc0=vc0, t=t: e.matmul(pp[pi][:], hT[hb][:, c, t * 128:(t + 1) * 128], w16[:, c, vc0:vc0 + 512], start=(c == 0), stop=(c == 7)),
                                deps=[hT_ev[hb][c], wtok(vc0), pp_rd[pi] if c == 0 else None], inc=(c == 7))
                st["last_mm"] = last
                r0 = tc * 512 + t * 128
                evac(pi, I["v_s"][r0:r0 + 128, half * 512:(half + 1) * 512], 1.0)
        pi = st["pp"] % 4
        st["pp"] += 1
        for c in range(8):
            last = p.op("tensor", lambda e, c=c, pi=pi: e.matmul(pp[pi][0:8, :], w16[:, c, 3072:3080], hT[hb][:, c, :], start=(c == 0), stop=(c == 7)),
                        deps=[hT_ev[hb][c], wtok(3072), pp_rd[pi] if c == 0 else None], inc=(c == 7))
        tf = p.op("vector", lambda e, pi=pi, hb=hb: e.tensor_copy(flst[hb][:], pp[pi][0:8, :]), deps=[last, fl_rd[hb]])
        pp_rd[pi] = tf
        fl_rd[hb] = p.dma("gpsimd", I["fl_s"][:, tok0:tok0 + 512], flst[hb][:], dfl[hb], deps=[tf])
        hT_rd[hb] = last

    norm_tr(0)
    for tc in range(8):
        if tc + 1 < 8:
            norm_tr(tc + 1)
        proj(tc)

    p.wait("gpsimd", [(d[0], d[1]["v"]) for d in dstg] + [(d[0], d[1]["v"]) for d in dfl])
    p.wait("sync", [bt_store])
    p.build()


def phase2(nc, I, G):
    p = Prog(nc, "p2")
    SEG = 256
    F = p.sb("F", [128, SEG], F32)
    A = p.sb("A", [128, SEG], F32)
    B = p.sb("B", [128, SEG], F32)
    tri = p.sb("tri", [128, 128], F32)
    nb = p.sb("nb", [128, 1], F32)
    off = p.sb("off", [128, 1], F32)
    pm = p.ps("pm", [128, 512], F32)
    dl = p.dsem("ld")
    p.dma("sync", F[:], I["fl_s"].rearrange("h (s t) -> (h s) t", t=SEG), dl)
    p.dma("sync", tri[:], I["segtri"], dl)
    p.dma("sync", nb[:], I["fb128"], dl)
    LD = (dl[0], dl[1]["v"])
    t0 = p.op("vector", lambda e: e.tensor_scalar(nb[:], nb[:], -1.0, None, op0=ALU.mult), deps=[LD])
    t1 = p.op("scalar", lambda e: e.activation(A[:], F[:], AF.Exp, scale=-1.0, bias=nb[:, 0:1]), deps=[t0, LD])
    t2 = p.op("scalar", lambda e: e.activation(A[:], A[:], AF.Ln, bias=G["onec"][:, 0:1]), deps=[t1])
    cur, nxt = A, B
    tk = t2
    ta = None
    s_ = 1
    while s_ < SEG:
        ta = p.op("vector", lambda e, cur=cur, nxt=nxt, s_=s_: e.tensor_copy(nxt[:, 0:s_], cur[:, 0:s_]), deps=[tk])
        tk = p.op("vector", lambda e, cur=cur, nxt=nxt, s_=s_: e.tensor_tensor(nxt[:, s_:SEG], cur[:, s_:SEG], cur[:, 0:SEG - s_], op=ALU.add), deps=[tk])
        cur, nxt = nxt, cur
        s_ *= 2
    tm = p.op("tensor", lambda e: e.matmul(pm[:, 0:1], tri[:], cur[:, SEG - 1:SEG], start=True, stop=True), deps=[tk, ta, LD])
    to_ = p.op("vector", lambda e: e.tensor_copy(off[:], pm[:, 0:1]), deps=[tm])
    tP = p.op("vector", lambda e: e.tensor_scalar(nxt[:], cur[:], off[:, 0:1], None, op0=ALU.add), deps=[to_, tk, ta])
    P_ = nxt
    R = cur
    ch = p.sb("ch", [128, SEG], BF16)
    cm = p.sb("cm", [128, SEG], BF16)
    cl = p.sb("cl", [128, SEG], BF16)
    nh = p.sb("nh", [128, SEG], BF16)
    nm = p.sb("nm", [128, SEG], BF16)
    nl = p.sb("nl", [128, SEG], BF16)
    on = p.sb("on", [128, SEG], BF16)
    R2 = p.sb("R2", [128, SEG], F32)
    to = p.op("gpsimd", lambda e: e.memset(on[:], 1.0))
    a1 = p.op("vector", lambda e: e.tensor_scalar(ch[:], P_[:], -1.0, None, op0=ALU.mult), deps=[tP])
    a2 = p.op("vector", lambda e: e.scalar_tensor_tensor(R[:], P_[:], -1.0, ch[:], op0=ALU.mult, op1=ALU.subtract), deps=[a1])
    a3 = p.op("vector", lambda e: e.tensor_copy(cm[:], R[:]), deps=[a2])
    a4 = p.op("vector", lambda e: e.tensor_tensor(R2[:], R[:], cm[:], op=ALU.subtract), deps=[a3])
    a5 = p.op("vector", lambda e: e.tensor_copy(cl[:], R2[:]), deps=[a4])
    a6 = p.op("vector", lambda e: e.tensor_scalar(nh[:], ch[:], -1.0, None, op0=ALU.mult), deps=[a1])
    a7 = p.op("vector", lambda e: e.tensor_scalar(nm[:], cm[:], -1.0, None, op0=ALU.mult), deps=[a3])
    a8 = p.op("vector", lambda e: e.tensor_scalar(nl[:], cl[:], -1.0, None, op0=ALU.mult), deps=[a5])
    da = p.dsem("aug")
    fin = []
    for r, (qsrc, ksrc) in enumerate(((ch, on), (cm, on), (cl, on), (on, nh), (on, nm), (on, nl))):
        fin.append(p.dma("sync", I["augq"][r, :, :].rearrange("h (s t) -> (h s) t", t=SEG), qsrc[:], da, deps=[a8, to]))
        fin.append(p.dma("sync", I["augk"][r, :, :].rearrange("h (s t) -> (h s) t", t=SEG), ksrc[:], da, deps=[a8, to]))
    p.wait("sync", [fin[-1]])
    p.build()


def build_toeplitz(p, I, G, btile, cmask, Sp):
    ds = p.dsem("tld")
    dsc = p.dsem("scr")
    rbe = p.sb("rbe", [33, 4], F32)
    oh = p.sb("oh", [33, 768], F32)
    p.dma("sync", rbe[0:32, :], I["relb"], ds)
    p.dma("sync", oh[:], I["onehot"], ds)
    if cmask is not None:
        p.dma("gpsimd", cmask[:], I["cmask"], ds)
    LD = (ds[0], ds[1]["v"])
    MS = p.op("gpsimd", lambda e: e.memset(rbe[32:33, :], -1e30), deps=[LD])
    rl = p.sb("rl", [33, 4, 128], F32)
    vecs = [p.sb("vec%d" % h, [128, 768], F32) for h in range(4)]
    b32s = [p.sb("b32_%d" % h, [128, 640], F32) for h in range(4)]
    tprev = None
    tlast = []
    for h in range(4):
        vec = vecs[h]
        b32 = b32s[h]
        tr = p.op("vector", lambda e, h=h: e.tensor_scalar(rl[:, h, :], G["ones32"][0:33, :], rbe[:, h:h + 1], None, op0=ALU.mult), deps=[MS, LD])
        ta_ = p.op("tensor", lambda e, h=h: e.matmul(Sp[:, 0:384], rl[:, h, :], oh[:, 0:384], start=True, stop=True), deps=[tr, LD, tprev], inc=False)
        tb_ = p.op("tensor", lambda e, h=h: e.matmul(Sp[:, 512:896], rl[:, h, :], oh[:, 384:768], start=True, stop=True), deps=[])
        tv1 = p.op("vector", lambda e, vec=vec: e.tensor_copy(vec[:, 0:384], Sp[:, 0:384]), deps=[tb_])
        tv2 = p.op("vector", lambda e, vec=vec: e.tensor_copy(vec[:, 384:768], Sp[:, 512:896]), deps=[tb_])
        tprev = tv2
        dsh = p.dsem("scr%d" % h)
        tw = p.dma("gpsimd", bass.AP(I["_H"]["tscr"], h * 128 * 768, [[768, 128], [1, 768]]), vec[:], dsh, deps=[tv2, tv1])
        trd = p.dma("gpsimd", b32[:], bass.AP(I["_H"]["tscr"], h * 128 * 768 + 127, [[767, 128], [1, 640]]), dsh, deps=[tw])
        tlast.append((h, trd))
    for h, trd in tlast:
        tprev = p.op("vector", lambda e, h=h: e.tensor_copy(btile[:, h, :], b32s[h][:]), deps=[trd])
    return tprev, LD


def phase3(nc, I, G):
    p = Prog(nc, "p3")
    qd = [p.sb("qd%d" % i, [128, S], BF16) for i in range(2)]
    kd = [p.sb("kd%d" % i, [128, S], BF16) for i in range(2)]
    qf = [[p.sb("qf%d_%d" % (b, m), [70, S], BF16) for m in range(2)] for b in range(2)]
    kf = [[p.sb("kf%d_%d" % (b, m), [70, S], BF16) for m in range(2)] for b in range(2)]
    dlf2 = [p.dsem("ldf%d" % i) for i in range(2)]
    foxb_rd = [None, None]
    vt = [p.sb("vt%d" % i, [128, NT, 128], BF16) for i in range(2)]
    vtf = [p.sb("vtf%d" % i, [128, NT, 130], BF16) for i in range(2)]
    NPT = 4
    NFILL = 1
    PT = [p.sb("PT%d" % i, [128, 1024], BF16) for i in range(NPT)]
    co = [[p.sb("co%d_%d" % (a, m), [128, 512], F32) for m in range(2)] for a in range(2)]
    rsb = [[p.sb("rsb%d_%d" % (a, m), [128, 512], F32) for m in range(2)] for a in range(2)]
    tt_ = [[p.sb("tt%d_%d" % (a, m), [128, 512], F32) for m in range(2)] for a in range(2)]
    av = [p.sb("av%d" % a, [128, 512], F32) for a in range(2)]
    sq = [p.sb("sq%d" % a, [128, 512], F32) for a in range(2)]
    rstd = [p.sb("rstd%d" % a, [128, 512], F32) for a in range(2)]
    dd = [p.sb("dd%d" % a, [128, 512], F32) for a in range(2)]
    acc = [p.sb("acc%d" % i, [128, 1024], F32) for i in range(2)]
    acc_free = [None, None]
    acc_tok = [None]
    NOB = 2
    ob = [p.sb("ob%d" % i, [128, 512], BF16) for i in range(NOB)]
    dob = [p.dsem("ob%d" % i) for i in range(NOB)]
    Sps = [p.ps("S%d" % i, [128, 1024], F32) for i in range(2)]
    Ops = [[p.ps("O%d_%d" % (a, m), [128, 512], F32) for m in range(2)] for a in range(2)]
    dl = [p.dsem("ld%d" % i) for i in range(2)]
    dlf = p.dsem("ldf")
    dlv = [p.dsem("ldv%d" % i) for i in range(2)]
    btile = p.sb("btile", [128, 4, 640], BF16)
    cmask = p.sb("cmask", [128, 512], BF16)
    dtoe = p.dsem("toe")
    TOE = p.dma("sync", btile[:], I["bt_s"].rearrange("p (h x) -> p h x", h=4), dtoe)
    dcm = p.dsem("cm")
    CML = p.dma("gpsimd", cmask[:], I["cmask"], dcm)
    ones_tok = None
    for i in range(2):
        p.op("gpsimd", lambda e, i=i: e.memset(vtf[i][:, :, 64:65], 1.0))
        ones_tok = p.op("gpsimd", lambda e, i=i: e.memset(vtf[i][:, :, 129:130], 1.0))

    unit_rd = [None, None]
    fox_rd = [None]
    vt_rd = [None, None]
    vtf_rd = [None, None]
    ld_tok = {}

    def load_unit(u):
        if u < 4:
            b = u % 2
            p.dma("sync", qd[b][:], I["qk_s"][u * 128:(u + 1) * 128, :], dl[b], deps=[unit_rd[b]])
            p.dma("sync", kd[b][:], I["qk_s"][512 + u * 128:512 + (u + 1) * 128, :], dl[b], deps=[unit_rd[b]])
            for tg in range(4):
                vsrc = I["v_s"][tg * 1024:(tg + 1) * 1024, u * 128:(u + 1) * 128].rearrange("(t p) d -> p t d", p=128)
                p.dma("sync", vt[b][:, tg * 8:(tg + 1) * 8, :], vsrc, dl[b], deps=[vt_rd[b]])
            ld_tok[u] = [(dl[b][0], dl[b][1]["v"])]
        else:
            f = u - 4
            b = f % 2
            for m in range(2):
                hd = 2 * f + m
                p.dma("sync", qf[b][m][0:64, :], I["qk_s"][1024 + hd * 64:1024 + (hd + 1) * 64, :], dlf2[b], deps=[foxb_rd[b]])
                p.dma("sync", kf[b][m][0:64, :], I["qk_s"][1536 + hd * 64:1536 + (hd + 1) * 64, :], dlf2[b], deps=[foxb_rd[b]])
                p.dma("sync", qf[b][m][64:70, :], I["augq"][:, hd, :], dlf2[b], deps=[foxb_rd[b]])
                p.dma("sync", kf[b][m][64:70, :], I["augk"][:, hd, :], dlf2[b], deps=[foxb_rd[b]])
                c0 = 512 + f * 128 + m * 64
                for tg in range(4):
                    vsrc = I["v_s"][tg * 1024:(tg + 1) * 1024, c0:c0 + 64].rearrange("(t p) d -> p t d", p=128)
                    p.dma("sync", vtf[b][:, tg * 8:(tg + 1) * 8, 65 * m:65 * m + 64], vsrc, dlv[b], deps=[vtf_rd[b], ones_tok])
            ld_tok[u] = [(dlf2[b][0], dlf2[b][1]["v"]), (dlv[b][0], dlv[b][1]["v"]), ones_tok]

    S_rd = [None, None]
    PT_rd = [[] for _ in range(NPT)]
    o_free = [[], []]
    sp_free = [None, None]
    pend = []
    gbc = [0]
    gch = [0]
    ob_i = [0]

    def next_ob():
        i = ob_i[0] % NOB
        ob_i[0] += 1
        return i, (dob[i][0], dob[i][1]["v"])

    def run_pending(limit_chunk=None, force=False):
        if force:
            while pend and (limit_chunk is None or pend[0][1] <= limit_chunk):
                pend.pop(0)[2]()
            return
        for _ in range(2):
            if pend and pend[0][0] <= gbc[0]:
                pend.pop(0)[2]()

    def epilogue(u, qc, pv_tok, atok, ai):
        cid = gch[0]
        gch[0] += 1
        run_pending(limit_chunk=cid - 2, force=True)
        a = cid % 2
        q0 = qc * 512
        diff = u < 4
        SPa = Ops[a][0]
        g = gbc[0]
        if diff:
            e1 = p.op("scalar", lambda e: e.activation(co[a][0][:], Ops[a][0][:], AF.Copy), deps=[pv_tok])
            e2 = p.op("scalar", lambda e: e.activation(co[a][1][:], Ops[a][1][:], AF.Copy), deps=[pv_tok])
            o_free[a] = [e1, e2]
            sp_free[a] = e1
            stt = {}

            def s1():
                stt["m0"] = p.op("tensor", lambda e: e.matmul(SPa[:], G["ones32"][:], acc[ai][:, 0:512], start=True, stop=True), deps=[atok, sp_free[a]])

            def s2():
                tl_ = p.op("scalar", lambda e: e.activation(rsb[a][0][:], SPa[:], AF.Ln), deps=[stt["m0"]])
                sp_free[a] = tl_
                o_free[a].append(tl_)
                stt["x0"] = p.op("scalar", lambda e: e.activation(rsb[a][0][:], rsb[a][0][:], AF.Exp, scale=-1.0), deps=[tl_])

            def s3():
                t = p.op("tensor", lambda e: e.matmul(SPa[:], G["ones32"][:], acc[ai][:, 512:1024], start=True, stop=True), deps=[atok, sp_free[a]])
                stt["m1"] = t
                acc_free[ai] = t
                stt["t0"] = p.op("vector", lambda e: e.tensor_tensor(tt_[a][0][:], co[a][0][:], rsb[a][0][:], op=ALU.mult), deps=[stt["x0"], e1])

            def s4():
                tl_ = p.op("scalar", lambda e: e.activation(rsb[a][1][:], SPa[:], AF.Ln), deps=[stt["m1"]])
                sp_free[a] = tl_
                o_free[a].append(tl_)
                stt["x1"] = p.op("scalar", lambda e: e.activation(rsb[a][1][:], rsb[a][1][:], AF.Exp, scale=-1.0), deps=[tl_])

            def s5():
                t1 = p.op("vector", lambda e: e.tensor_tensor(tt_[a][1][:], co[a][1][:], rsb[a][1][:], op=ALU.mult), deps=[stt["x1"], e2])
                stt["a"] = p.op("vector", lambda e: e.scalar_tensor_tensor(av[a][:], tt_[a][1][:], G["neglam"][:, 0:1], tt_[a][0][:], op0=ALU.mult, op1=ALU.add), deps=[t1, stt["t0"]])

            def s6():
                stt["sq"] = p.op("scalar", lambda e: e.activation(sq[a][:], av[a][:], AF.Square), deps=[stt["a"]])

            def s7():
                stt["m2"] = p.op("tensor", lambda e: e.matmul(SPa[:], G["ones32"][:], sq[a][:], start=True, stop=True), deps=[stt["sq"], sp_free[a]])

            def s8():
                tl = p.op("scalar", lambda e: e.activation(rstd[a][:], SPa[:], AF.Ln, scale=1.0 / 128, bias=G["epsc"][:, 0:1]), deps=[stt["m2"]])
                sp_free[a] = tl
                o_free[a].append(tl)
                stt["x2"] = p.op("scalar", lambda e: e.activation(rstd[a][:], rstd[a][:], AF.Exp, scale=-0.5), deps=[tl])

            def s9():
                stt["d"] = p.op("vector", lambda e: e.tensor_tensor(dd[a][:], av[a][:], rstd[a][:], op=ALU.mult), deps=[stt["x2"], stt["a"]])

            def s10():
                i, obt = next_ob()
                tob = p.op("gpsimd", lambda e, i=i: e.tensor_scalar(ob[i][:], dd[a][:], G["gsub"][:, 0:1], None, op0=ALU.mult), deps=[stt["d"], obt])
                p.dma("gpsimd", I["o_s"][u * 128:(u + 1) * 128, q0:q0 + 512], ob[i][:], dob[i], deps=[tob])

            nxt_len = 4 * (qc - 1) + 4 if qc > 0 else 32
            sp = 2 if nxt_len >= 22 else 1
            for k, fn in enumerate((s1, s2, s3, s4, s5, s6, s7, s8, s9, s10)):
                pend.append((g + 2 + sp * k, cid, fn))
        else:
            es = []
            rr = []
            for m in range(2):
                es.append(p.op("vector", lambda e, m=m: e.tensor_copy(co[a][m][0:65, :], Ops[a][m][0:65, :]), deps=[pv_tok]))
            o_free[a] = [es[1]]
            sp_free[a] = es[0]
            for m in range(2):
                rr.append(p.op("vector", lambda e, m=m: e.reciprocal(rsb[a][m][64:65, :], co[a][m][64:65, :]), deps=[es[m]]))
            stt = {}

            def mk_pe(m):
                def st():
                    stt[m] = p.op("tensor", lambda e: e.matmul(SPa[0:64, :], G["ones32"][64:65, 0:64], rsb[a][m][64:65, :], start=True, stop=True), deps=[rr[m], sp_free[a]])
                return st

            def mk_dv(m):
                def st():
                    i, obt = next_ob()
                    t0 = p.op("vector", lambda e, i=i: e.tensor_tensor(ob[i][0:64, :], co[a][m][0:64, :], SPa[0:64, :], op=ALU.mult), deps=[stt[m], es[m], obt])
                    sp_free[a] = t0
                    o_free[a].append(t0)
                    hd = 2 * (u - 4) + m
                    p.dma("gpsimd", I["o_s"][512 + hd * 64:512 + (hd + 1) * 64, q0:q0 + 512], ob[i][0:64, :], dob[i], deps=[t0])
                return st
            for k, fn in enumerate((mk_pe(0), mk_dv(0), mk_pe(1), mk_dv(1))):
                pend.append((g + 5 + 2 * k, cid, fn))

    def run_unit(u):
        diff = u < 4
        b = (u % 2) if diff else ((u - 4) % 2)
        M = 128 if diff else 65
        lts = ld_tok[u]
        blocks = [(qc, j) for qc in reversed(range(8)) for j in range(4 * qc + 4)]

        def qap(m, c0, c1):
            return qd[b][64 * m:64 * m + 64, c0:c1] if diff else qf[b][m][0:70, c0:c1]

        def kap(m, j):
            return kd[b][64 * m:64 * m + 64, j * 128:(j + 1) * 128] if diff else kf[b][m][0:70, j * 128:(j + 1) * 128]

        def vap(m, j):
            return vt[b][:, j, :] if diff else vtf[b][:, j, 65 * m:65 * m + 65]

        def geom(qc, j):
            c0 = max(0, j - 4 * qc) * 128
            return c0, 512 - c0

        def QK(bi):
            qc, j = blocks[bi]
            sb_ = bi % 2
            c0, W = geom(qc, j)
            near = diff and (j >= 4 * qc - 1)
            diag = (not diff) and (j >= 4 * qc)
            last = None
            for m in range(2):
                extra = near or diag
                last = p.op("tensor", lambda e, m=m: e.matmul(Sps[sb_][:, m * 512 + c0:m * 512 + 512], kap(m, j), qap(m, qc * 512 + c0, qc * 512 + 512), start=True, stop=not extra),
                            deps=lts + [S_rd[sb_] if m == 0 else None], inc=(m == 1 and not extra))
                if near:
                    xs = qc * 512 + c0 - 128 * j
                    last = p.op("tensor", lambda e, m=m, xs=xs: e.matmul(Sps[sb_][:, m * 512 + c0:m * 512 + 512], G["ident16"][:], btile[:, u, xs:xs + W], start=False, stop=True), deps=[TOE], inc=(m == 1))
                elif diag:
                    last = p.op("tensor", lambda e, m=m: e.matmul(Sps[sb_][:, m * 512 + c0:m * 512 + 512], G["ident16"][:], cmask[:, 0:W], start=False, stop=True), deps=[CML], inc=(m == 1))
            return last

        def EXP(bi, qk_tok):
            qc, j = blocks[bi]
            sb_ = bi % 2
            pb_ = bi % NPT
            c0, W = geom(qc, j)
            far = diff and not (j >= 4 * qc - 1)
            bias = G["b31"][:, u:u + 1] if far else G["zeroc"][:, 0:1]
            if W == 512:
                src, dst = Sps[sb_][:, :], PT[pb_][:, :]
            else:
                src = Sps[sb_][:, :].rearrange("p (m w) -> p m w", m=2)[:, :, c0:512]
                dst = PT[pb_][:, :].rearrange("p (m w) -> p m w", m=2)[:, :, c0:512]
            t = p.op("scalar", lambda e: e.activation(dst, src, AF.Exp, bias=bias), deps=[qk_tok] + PT_rd[pb_])
            S_rd[sb_] = t
            return t

        def PV(bi, exp_tok):
            qc, j = blocks[bi]
            pb_ = bi % NPT
            c0, W = geom(qc, j)
            first = (j == 0)
            lastj = (j == 4 * qc + 3)
            last = None
            a_ = gch[0] % 2
            if first:
                run_pending(limit_chunk=gch[0] - 2, force=True)
            for m in range(2):
                last = p.op("tensor", lambda e, m=m: e.matmul(Ops[a_][m][0:M, c0:512], vap(m, j), PT[pb_][:, m * 512 + c0:m * 512 + 512], start=first, stop=lastj),
                            deps=[exp_tok] + lts + (o_free[a_] if first else []), inc=(m == 1))
            ai = gch[0] % 2
            ta = None
            if diff:
                if first:
                    ta = p.op("vector", lambda e: e.tensor_copy(acc[ai][:, :], PT[pb_][:, :]), deps=[exp_tok, acc_free[ai]])
                elif W == 512:
                    ta = p.op("vector", lambda e: e.tensor_tensor(acc[ai][:, :], acc[ai][:, :], PT[pb_][:, :], op=ALU.add), deps=[exp_tok, acc_tok[0]])
                else:
                    a3 = acc[ai][:, :].rearrange("p (m w) -> p m w", m=2)[:, :, c0:512]
                    p3 = PT[pb_][:, :].rearrange("p (m w) -> p m w", m=2)[:, :, c0:512]
                    ta = p.op("vector", lambda e: e.tensor_tensor(a3, a3, p3, op=ALU.add), deps=[exp_tok, acc_tok[0]])
                acc_tok[0] = ta
                PT_rd[pb_] = [last, ta]
            else:
                PT_rd[pb_] = [last]
            gbc[0] += 1
            if lastj:
                epilogue(u, qc, last, ta, ai)
            else:
                run_pending()
            return last

        nb = len(blocks)
        qk = QK(0)
        last = None
        for bi in range(nb):
            ex = EXP(bi, qk)
            if bi + 1 < nb:
                qk = QK(bi + 1)
            last = PV(bi, ex)
        if diff:
            unit_rd[b] = last
            vt_rd[b] = last
        else:
            fox_rd[0] = last
            foxb_rd[b] = last
            vtf_rd[b] = last

    load_unit(0)
    for u in range(8):
        if u + 1 < 8:
            load_unit(u + 1)
        run_unit(u)
    run_pending(force=True)
    p.wait("gpsimd", [(d[0], d[1]["v"]) for d in dob])
    p.build()


def phase4(nc, I, G):
    p = Prog(nc, "p4")
    CH = 256
    NCH = S // CH
    wup = p.sb("wup", [128, 8, 2 * DFF], BF16)
    wdn = p.sb("wdn", [128, 22, D], BF16)
    wo = p.sb("wo", [128, 8, D], BF16)
    dwo = p.dsem("wo")
    dwu = p.dsem("wu")
    dwd = p.dsem("wd")
    wosrc = I["w_out"].rearrange("(c p) n -> p c n", p=128)
    WOh = []
    for hf in range(2):
        dwoh = p.dsem("woh%d" % hf)
        WOh.append(p.dma("gpsimd", wo[:, :, hf * 512:(hf + 1) * 512], wosrc[:, :, hf * 512:(hf + 1) * 512], dwoh))
    NG = 11
    dwug = [p.dsem("wug%d" % g) for g in range(NG)]
    wsrc = I["w_up"].rearrange("(c p) n -> p c n", p=128)
    WUg = []
    for g in range(NG):
        for base in (0, DFF):
            c0 = base + g * 256
            p.dma("gpsimd", wup[:, :, c0:c0 + 256], wsrc[:, :, c0:c0 + 256], dwug[g])
        WUg.append((dwug[g][0], dwug[g][1]["v"]))
    for c in range(22):
        p.dma("gpsimd", wdn[:, c, :], I["w_down"][c * 128:(c + 1) * 128, :], dwd)
    WO = (dwo[0], dwo[1]["v"])
    WU = (dwu[0], dwu[1]["v"])
    WD = (dwd[0], dwd[1]["v"])
    oT = [p.sb("oT%d" % i, [128, 8, CH], BF16) for i in range(2)]
    doT = [p.dsem("oT%d" % i) for i in range(2)]
    xt = [p.sb("xt%d" % i, [128, D], F32) for i in range(2)]
    dx = [p.dsem("x%d" % i) for i in range(2)]
    dot_ = [p.dsem("ot%d" % i) for i in range(2)]
    xn = [p.sb("xn%d" % i, [128, D], BF16) for i in range(2)]
    ss = p.sb("ss", [128, 4 * NT], F32)
    h2T = p.sb("h2T", [128, 8, CH], BF16)
    yT = p.sb("yT", [128, 22, CH], BF16)
    ug = [p.sb("ug%d" % i, [128, CH + 2], F32) for i in range(1)]
    uv = [p.sb("uv%d" % i, [128, CH + 2], F32) for i in range(1)]
    tmp = p.sb("tmp", [128, 512], F32)
    gc = [p.sb("gc%d" % i, [128, CH], F32) for i in range(1)]
    vc = [p.sb("vc%d" % i, [128, CH], F32) for i in range(1)]
    sgt = [p.sb("sg%d" % i, [128, CH], F32) for i in range(1)]
    carry = p.sb("carry", [128, 44, 2], F32)
    pm = [p.ps("pm%d" % i, [128, 512], F32) for i in range(4)]
    pst = [p.ps("pst%d" % i, [128, 512], BF16) for i in range(2)]
    pgv = [p.ps("pgv%d" % i, [128, 512], F32) for i in range(2)]
    tz = p.op("gpsimd", lambda e: e.memset(carry[:], 0.0))
    CW = G["convw"]
    CB = G["convb"]

    pm_rd = [None] * 4
    pst_rd = [None, None]
    pgv_rd = [[None, None], [None, None]]
    oT_rd = [None, None]
    oT_ld = {}
    xt_rd = [None, None]
    xn_rd = [None, None]
    h2T_rd = [None]
    yT_rd = [None]
    ug_rd = [None, None]
    uv_rd = [None, None]
    gc_rd = [None]
    vc_rd = [None]
    sg_rd = [None]
    cnt = {"f": 0, "ss": 0}
    carry_w = [tz] * 44
    ugc_rd = [None, None]
    uvc_rd = [None, None]

    def rstd_of(src_ap, dep, dump, dump_dep):
        k = cnt["ss"]
        cnt["ss"] += 1
        a = p.op("scalar", lambda e: e.activation(dump, src_ap, AF.Square, accum_out=ss[:, k:k + 1]), deps=[dep, dump_dep])
        b_ = p.op("scalar", lambda e: e.activation(ss[:, k:k + 1], ss[:, k:k + 1], AF.Ln, scale=1.0 / D, bias=G["epsc"][:, 0:1]), deps=[a])
        c_ = p.op("scalar", lambda e: e.activation(ss[:, k:k + 1], ss[:, k:k + 1], AF.Exp, scale=-0.5), deps=[b_])
        return ss[:, k:k + 1], c_

    def load_oT(ci):
        ob_ = ci % 2
        tok0 = ci * CH
        oT_ld[ci] = p.dma("sync", oT[ob_][:], I["o_s"][:, tok0:tok0 + CH].rearrange("(c p) t -> p c t", p=128), doT[ob_], deps=[oT_rd[ob_]])

    def emit_mix(ci):
        ob_ = ci % 2
        res = {}
        mm = None
        for t in range(2):
            for half in range(2):
                pi = t * 2 + half
                for c in range(8):
                    mm = p.op("tensor", lambda e, c=c, pi=pi, t=t, half=half: e.matmul(pm[pi][:], oT[ob_][:, c, t * 128:(t + 1) * 128], wo[:, c, half * 512:(half + 1) * 512], start=(c == 0), stop=(c == 7)),
                              deps=[oT_ld[ci], WOh[half], pm_rd[pi] if c == 0 else None], inc=(c == 7))
                res[pi] = mm
        oT_rd[ob_] = mm
        return res

    load_oT(0)
    mixres = emit_mix(0)
    for ci in range(NCH):
        tok0 = ci * CH
        if ci + 1 < NCH:
            load_oT(ci + 1)
        x1tok = []
        for t in range(2):
            sl = t
            r0 = tok0 + t * 128
            tx = p.dma("sync", xt[sl][:], I["x"][r0:r0 + 128, :], dx[sl], deps=[xt_rd[sl]])
            lastadd = None
            for half in range(2):
                pi = t * 2 + half
                t1 = p.op("vector", lambda e, pi=pi, half=half: e.tensor_tensor(tmp[:], pm[pi][:], G["g1_bc"][:, half * 512:(half + 1) * 512], op=ALU.mult), deps=[mixres[pi], lastadd])
                pm_rd[pi] = t1
                lastadd = p.op("vector", lambda e, sl=sl, half=half, pi=pi: e.tensor_tensor(xt[sl][:, half * 512:(half + 1) * 512], tmp[:], xt[sl][:, half * 512:(half + 1) * 512], op=ALU.add), deps=[t1, tx])
            rs_ap, rtok = rstd_of(xt[sl][:], lastadd, xn[t][:], xn_rd[t])
            tn = p.op("vector", lambda e, sl=sl, t=t, rs_ap=rs_ap: e.tensor_scalar(xn[t][:], xt[sl][:], rs_ap, None, op0=ALU.mult), deps=[rtok, lastadd, xn_rd[t]])
            x1tok.append(tn)
        ev = [None] * 8
        tt = None
        for c in range(8):
            for t in range(2):
                tt = p.op("tensor", lambda e, c=c, t=t: e.transpose(pst[c % 2][:, t * 128:(t + 1) * 128], xn[t][:, c * 128:(c + 1) * 128], G["ident16"][:]),
                          deps=[x1tok[t], pst_rd[c % 2] if t == 0 else None], inc=(t == 1))
            ev[c] = p.op("vector", lambda e, c=c: e.tensor_scalar(h2T[:, c, :], pst[c % 2][:, 0:CH], G["gsc2"][:, c:c + 1], G["modT"][:, 24 + c:25 + c], op0=ALU.mult, op1=ALU.add),
                         deps=[tt, h2T_rd[0] if c == 0 else None])
            pst_rd[c % 2] = ev[c]
        xn_rd[0] = tt
        xn_rd[1] = tt
        ylast = None
        for fc in range(22):
            fj = cnt["f"] % 2
            cnt["f"] += 1
            PG = pgv[fj][:, 0:CH]
            PV_ = pgv[fj][:, 256:256 + CH]
            mg = mv = None
            for c in range(8):
                mg = p.op("tensor", lambda e, c=c, fc=fc, PG=PG: e.matmul(PG, wup[:, c, fc * 128:(fc + 1) * 128], h2T[:, c, :], start=(c == 0), stop=(c == 7)),
                          deps=[ev[c], WUg[fc // 2]] + (pgv_rd[fj] if c == 0 else []), inc=(c == 7))
            for c in range(8):
                mv = p.op("tensor", lambda e, c=c, fc=fc, PV_=PV_: e.matmul(PV_, wup[:, c, DFF + fc * 128:DFF + (fc + 1) * 128], h2T[:, c, :], start=(c == 0), stop=(c == 7)),
                          deps=[ev[c]], inc=(c == 7))
            h1 = p.op("vector", lambda e, fc=fc, fj=fj: e.tensor_copy(ug[0][:, 0:2], carry[:, fc, :]), deps=[carry_w[fc], ug_rd[0]])
            h2 = p.op("vector", lambda e, fc=fc, fj=fj: e.tensor_copy(uv[0][:, 0:2], carry[:, 22 + fc, :]), deps=[carry_w[22 + fc], uv_rd[0]])
            eg = p.op("scalar", lambda e, fj=fj, PG=PG: e.activation(ug[0][:, 2:CH + 2], PG, AF.Copy), deps=[mv, ug_rd[0], ugc_rd[0]])
            g1_ = p.op("scalar", lambda e, fc=fc, PG=PG: e.activation(gc[0][:], PG, AF.Identity, scale=CW[:, 2, fc:fc + 1], bias=CB[:, fc:fc + 1]), deps=[mv, gc_rd[0]])
            cg = p.op("gpsimd", lambda e, fc=fc, fj=fj: e.tensor_copy(carry[:, fc, :], ug[0][:, CH:CH + 2]), deps=[eg, h1])
            carry_w[fc] = cg
            ugc_rd[0] = cg
            evv = p.op("scalar", lambda e, fj=fj, PV_=PV_: e.activation(uv[0][:, 2:CH + 2], PV_, AF.Copy), deps=[mv, uv_rd[0], uvc_rd[0]])
            v1_ = p.op("scalar", lambda e, fc=fc, PV_=PV_: e.activation(vc[0][:], PV_, AF.Identity, scale=CW[:, 2, 22 + fc:23 + fc], bias=CB[:, 22 + fc:23 + fc]), deps=[mv, vc_rd[0]])
            cv = p.op("gpsimd", lambda e, fc=fc, fj=fj: e.tensor_copy(carry[:, 22 + fc, :], uv[0][:, CH:CH + 2]), deps=[evv, h2])
            carry_w[22 + fc] = cv
            uvc_rd[0] = cv
            pgv_rd[fj] = [g1_, v1_]
            g2_ = p.op("vector", lambda e, fc=fc, fj=fj: e.scalar_tensor_tensor(gc[0][:], ug[0][:, 1:CH + 1], CW[:, 1, fc:fc + 1], gc[0][:], op0=ALU.mult, op1=ALU.add), deps=[g1_, eg, h1])
            g3_ = p.op("vector", lambda e, fc=fc, fj=fj: e.scalar_tensor_tensor(gc[0][:], ug[0][:, 0:CH], CW[:, 0, fc:fc + 1], gc[0][:], op0=ALU.mult, op1=ALU.add), deps=[g2_])
            ug_rd[0] = g3_
            v2_ = p.op("vector", lambda e, fc=fc, fj=fj: e.scalar_tensor_tensor(vc[0][:], uv[0][:, 1:CH + 1], CW[:, 1, 22 + fc:23 + fc], vc[0][:], op0=ALU.mult, op1=ALU.add), deps=[v1_, evv, h2])
            v3_ = p.op("vector", lambda e, fc=fc, fj=fj: e.scalar_tensor_tensor(vc[0][:], uv[0][:, 0:CH], CW[:, 0, 22 + fc:23 + fc], vc[0][:], op0=ALU.mult, op1=ALU.add), deps=[v2_])
            uv_rd[0] = v3_
            s1 = p.op("scalar", lambda e: e.activation(sgt[0][:], gc[0][:], AF.Silu), deps=[g3_, sg_rd[0]])
            gc_rd[0] = s1
            y1 = p.op("gpsimd", lambda e, fc=fc: e.tensor_tensor(yT[:, fc, :], sgt[0][:], vc[0][:], op=ALU.mult), deps=[s1, v3_, yT_rd[0] if fc == 0 else None])
            sg_rd[0] = y1
            vc_rd[0] = y1
            ylast = y1
        h2T_rd[0] = mv
        wd_tok = {}
        mm = None
        for t in range(2):
            for half in range(2):
                pi = t * 2 + half
                for fc in range(22):
                    mm = p.op("tensor", lambda e, fc=fc, pi=pi, t=t, half=half: e.matmul(pm[pi][:], yT[:, fc, t * 128:(t + 1) * 128], wdn[:, fc, half * 512:(half + 1) * 512], start=(fc == 0), stop=(fc == 21)),
                              deps=[ylast, WD, pm_rd[pi] if fc == 0 else None], inc=(fc == 21))
                wd_tok[pi] = mm
        yT_rd[0] = mm
        fin = []
        for t in range(2):
            sl = t
            r0 = tok0 + t * 128
            lastadd = None
            for half in range(2):
                pi = t * 2 + half
                t1 = p.op("vector", lambda e, pi=pi, half=half: e.tensor_tensor(tmp[:], pm[pi][:], G["g2_bc"][:, half * 512:(half + 1) * 512], op=ALU.mult), deps=[wd_tok[pi], lastadd])
                pm_rd[pi] = t1
                lastadd = p.op("vector", lambda e, sl=sl, half=half, pi=pi: e.tensor_tensor(xt[sl][:, half * 512:(half + 1) * 512], tmp[:], xt[sl][:, half * 512:(half + 1) * 512], op=ALU.add), deps=[t1])
            rs_ap, rtok = rstd_of(xt[sl][:], lastadd, xn[t][:], xn_rd[t])
            tf = p.op("vector", lambda e, sl=sl, rs_ap=rs_ap: e.scalar_tensor_tensor(xt[sl][:], xt[sl][:], rs_ap, G["fg_bc"][:], op0=ALU.mult, op1=ALU.mult), deps=[rtok, lastadd])
            xt_rd[sl] = p.dma("gpsimd", I["out"][r0:r0 + 128, :], xt[sl][:], dot_[sl], deps=[tf])
        if ci + 1 < NCH:
            mixres = emit_mix(ci + 1)
    p.wait("sync", [(d[0], d[1]["v"]) for d in dot_])
    p.build()


def build_nc():
    nc = bass.Bass("TRN2", target_bir_lowering=False)
    I = {}

    H = {}
    I["_H"] = H

    def inp(name, shape, dt=F32):
        H[name] = nc.dram_tensor(name, list(shape), dt, kind="ExternalInput")
        I[name] = H[name].ap()

    inp("x", [S, D]); inp("cT", [128, 8]); inp("ada_w", [D, 6 * D]); inp("adabT", [128, 48])
    inp("gatT", [128, 8]); inp("gffT", [128, 8]); inp("fng", [1, D]); inp("w_in", [D, NIN])
    inp("fb", [8, 1]); inp("lam4", [1, 256]); inp("subg", [128, 1]); inp("relb", [32, 4])
    inp("w_out", [D, D]); inp("w_up", [D, 2 * DFF]); inp("convwT", [128, 3, 44]); inp("convbT", [128, 44])
    inp("w_down", [DFF, D]); inp("segtri", [128, 128]); inp("fb128", [128, 1]); inp("onehot", [33, 768]); inp("ident", [128, 128]); inp("cmask", [128, 512])
    I["out"] = nc.dram_tensor("out", [S, D], F32, kind="ExternalOutput").ap()
    sk = "ExternalOutput" if DEBUG else "Internal"
    I["qk_s"] = nc.dram_tensor("qk_s", [2048, S], BF16, kind=sk).ap()
    I["v_s"] = nc.dram_tensor("v_s", [S, 1024], BF16, kind=sk).ap()
    I["o_s"] = nc.dram_tensor("o_s", [1024, S], BF16, kind=sk).ap()
    I["fl_s"] = nc.dram_tensor("fl_s", [8, S], F32, kind=sk).ap()
    I["bt_s"] = nc.dram_tensor("bt_s", [128, 4 * 640], BF16, kind="Internal").ap()
    I["augq"] = nc.dram_tensor("augq", [6, 8, S], BF16, kind=sk).ap()
    I["augk"] = nc.dram_tensor("augk", [6, 8, S], BF16, kind=sk).ap()
    H["tscr"] = nc.dram_tensor("tscr", [4 * 128 * 768], F32, kind="Internal")

    ctx = []

    def gsb(name, shape, dt):
        cm = nc.sbuf_tensor(name, list(shape), dt)
        t = cm.__enter__()
        ctx.append(cm)
        return t

    G = {
        "ident16": gsb("g_ident16", [128, 128], BF16), "ones16": gsb("g_ones16", [128, 128], BF16),
        "ones32": gsb("g_ones32", [128, 128], F32),
        "modT": gsb("g_modT", [128, 48], F32), "gsc1": gsb("g_gsc1", [128, 8], F32), "gsc2": gsb("g_gsc2", [128, 8], F32),
        "g1_bc": gsb("g_g1bc", [128, D], F32), "g2_bc": gsb("g_g2bc", [128, D], F32), "fg_bc": gsb("g_fgbc", [128, D], F32),
        "neglam": gsb("g_neglam", [128, 1], F32), "gsub": gsb("g_gsub", [128, 1], F32), "nfb": gsb("g_nfb", [8, 1], F32),
        "b31": gsb("g_b31", [128, 4], F32),
        "convw": gsb("g_convw", [128, 3, 44], F32), "convb": gsb("g_convb", [128, 44], F32),
        "epsc": gsb("g_epsc", [128, 1], F32), "onec": gsb("g_onec", [128, 1], F32), "zeroc": gsb("g_zeroc", [128, 1], F32),
    }
    phase0(nc, I, G)
    phase1(nc, I, G)
    phase2(nc, I, G)
    phase3(nc, I, G)
    phase4(nc, I, G)
    for cm in reversed(ctx):
        cm.__exit__(None, None, None)
    return nc


def _bucket_table():
    n = np.arange(640, dtype=np.int64)
    nf = np.maximum(n, 1).astype(np.float32)
    large = 16 + (np.log(nf / np.float32(16)) / np.float32(math.log(128 / 16)) * np.float32(16)).astype(np.int32)
    large = np.minimum(large, 31)
    return np.where(n < 16, n, large)


def make_in_maps(inputs):
    f = lambda a: np.ascontiguousarray(np.asarray(a, dtype=np.float32))
    x = f(inputs["x"]); c = f(inputs["c"])
    bk = _bucket_table()
    onehot = np.zeros((33, 768), np.float32)
    onehot[32, :127] = 1.0
    for m in range(127, 767):
        onehot[bk[m - 127], m] = 1.0
    ident = np.eye(128, dtype=np.float32)
    kk = np.arange(128)[:, None]; xx = np.arange(512)[None, :]
    cmask = np.where(xx >= kk, 0.0, -1e30).astype(np.float32)
    cw = f(inputs["conv_w"])[0]
    pi_ = np.arange(128)
    segtri = ((pi_[:, None] // 16 == pi_[None, :] // 16) & (pi_[:, None] % 16 < pi_[None, :] % 16)).astype(np.float32)
    shared = {
        "ada_w": f(inputs["ada_w"])[0],
        "adabT": np.ascontiguousarray(f(inputs["ada_b"])[0].reshape(48, 128).T),
        "gatT": np.ascontiguousarray(f(inputs["attn_norm_g"])[0].reshape(8, 128).T),
        "gffT": np.ascontiguousarray(f(inputs["ffn_norm_g"])[0].reshape(8, 128).T),
        "fng": f(inputs["final_norm_g"]).reshape(1, D),
        "w_in": f(inputs["w_in"])[0],
        "fb": f(inputs["forget_b"])[0].reshape(8, 1),
        "lam4": np.concatenate([f(inputs["lambda_q1"])[0], f(inputs["lambda_k1"])[0], f(inputs["lambda_q2"])[0], f(inputs["lambda_k2"])[0]]).reshape(1, 256),
        "subg": f(inputs["subln_g"])[0].reshape(128, 1),
        "relb": f(inputs["rel_bias"]),
        "w_out": f(inputs["w_out"])[0],
        "w_up": f(inputs["w_up"])[0],
        "convwT": np.ascontiguousarray(cw.reshape(3, 44, 128).transpose(2, 0, 1)),
        "convbT": np.ascontiguousarray(f(inputs["conv_b"])[0].reshape(44, 128).T),
        "w_down": f(inputs["w_down"])[0],
        "onehot": onehot, "ident": ident, "cmask": cmask,
        "segtri": segtri, "fb128": np.ascontiguousarray(np.repeat(f(inputs["forget_b"])[0], 16).reshape(128, 1)),
    }
    maps = []
    for b in range(8):
        m = dict(shared)
        m["x"] = np.ascontiguousarray(x[b])
        m["cT"] = np.ascontiguousarray(c[b].reshape(8, 128).T)
        maps.append(m)
    return maps


def kernel(**inputs):
    nc = build_nc()
    in_maps = make_in_maps(inputs)
    res = run_bass_kernel_spmd(nc, in_maps, core_ids=list(range(8)))
    out = np.stack([np.asarray(r["out"], dtype=np.float32) for r in res.results], axis=0)
    return out
```
